# Optimizing a Trainium2 kernel written in Bass

```python
import math
import jax, jax.numpy as jnp
from jax import lax
import numpy as np

D_MODEL = 1024
BATCH = 8
SEQ = 2048
DEPTH = 2
DEC_BATCH = 32
DEC_SEQ = 1
PAST_LEN = 16384
PAGE_SIZE = 128

N_HEADS = 12
HEAD_DIM = 64
Q_LORA = 384
KV_LORA = 256
NOPE_DIM = 64
ROPE_DIM = 32
V_DIM = 64
KV_HEADS = 4
MOBA_BLOCK = 256
MOBA_TOPK = 3
MOBA_Q_CHUNK = 16
MEM_LEN = 256
MEM_HEADS = 4
MEM_HEAD_DIM = 64
D_FF = 2816
CONV_W = 3
ROPE_THETA = 10000.0
ATTN_Q_BLOCK = 128
LN_EPS = 1e-5
RMS_EPS = 1e-6
ALPHA = (2 * DEPTH) ** 0.25
BETA = (8 * DEPTH) ** -0.25
N_A = (DEPTH + 1) // 2
N_B = DEPTH // 2
SELF_WIDTH = N_HEADS * V_DIM
MEM_WIDTH = MEM_HEADS * MEM_HEAD_DIM
MIX_WIDTH = SELF_WIDTH + MEM_WIDTH
IN_A = Q_LORA + KV_LORA + ROPE_DIM + MEM_WIDTH
IN_B = N_HEADS * HEAD_DIM + 2 * KV_HEADS * HEAD_DIM + MEM_WIDTH

kernel_name = "mla_moba_memxattn_convffn_decode_step"


def layer_norm(x, g, b):
    xf = x.astype(jnp.float32)
    mu = jnp.mean(xf, -1, keepdims=True)
    var = jnp.mean(jnp.square(xf - mu), -1, keepdims=True)
    return ((xf - mu) * lax.rsqrt(var + LN_EPS) * g.astype(jnp.float32) + b.astype(jnp.float32)).astype(x.dtype)


def rms_norm(x, g):
    xf = x.astype(jnp.float32)
    ms = jnp.mean(jnp.square(xf), -1, keepdims=True)
    return (xf * lax.rsqrt(ms + RMS_EPS) * g.astype(jnp.float32)).astype(x.dtype)


def rope(x, pos):
    d = x.shape[-1]
    inv = ROPE_THETA ** (-jnp.arange(0, d, 2, dtype=jnp.float32) / d)
    ang = pos.astype(jnp.float32)[:, None] * inv[None, :]
    cos, sin = jnp.cos(ang), jnp.sin(ang)
    if x.ndim == 4:
        cos, sin = cos[:, None, :], sin[:, None, :]
    xf = x.astype(jnp.float32)
    x1, x2 = xf[..., : d // 2], xf[..., d // 2:]
    return jnp.concatenate([x1 * cos - x2 * sin, x2 * cos + x1 * sin], -1).astype(x.dtype)


def gather_pages(pool, page_table):
    g = pool[page_table]
    return g.reshape((g.shape[0], g.shape[1] * g.shape[2]) + g.shape[3:])


def map_query_blocks(fn, q_pos, xs, block):
    n_q = q_pos.shape[0]
    blk = block if n_q % block == 0 else n_q
    nb = n_q // blk
    split = lambda a: jnp.moveaxis(a.reshape((a.shape[0], nb, blk) + a.shape[2:]), 1, 0)
    out = lax.map(lambda a: fn(a[0], *a[1]), (q_pos.reshape(nb, blk), tuple(split(a) for a in xs)))
    out = jnp.moveaxis(out, 0, 1)
    return out.reshape((out.shape[0], n_q) + out.shape[3:])


def mla_attend(q_lat, q_rope, c, kr, q_pos):
    scale = (NOPE_DIM + ROPE_DIM) ** -0.5
    k_pos = jnp.arange(c.shape[1], dtype=jnp.int32)

    def block(pos, ql, qr):
        s = (jnp.einsum('bqhr,blr->bhql', ql, c) + jnp.einsum('bqhe,ble->bhql', qr, kr)).astype(jnp.float32) * scale
        s = jnp.where(k_pos[None, :] <= pos[:, None], s, -jnp.inf)
        p = jax.nn.softmax(s, axis=-1).astype(c.dtype)
        return jnp.einsum('bhql,blr->bqhr', p, c)

    return map_query_blocks(block, q_pos, (q_lat, q_rope), ATTN_Q_BLOCK)


def mla_mixer(h, pos, past_c, past_kr, g_q, w_q_b, g_kv, w_uk, w_uv):
    B, Q = h.shape[0], h.shape[1]
    q_a = h[..., :Q_LORA]
    c_kv = h[..., Q_LORA:Q_LORA + KV_LORA]
    k_r = h[..., Q_LORA + KV_LORA:Q_LORA + KV_LORA + ROPE_DIM]
    q = (rms_norm(q_a, g_q) @ w_q_b).reshape(B, Q, N_HEADS, NOPE_DIM + ROPE_DIM)
    q_nope = q[..., :NOPE_DIM]
    q_rope = rope(q[..., NOPE_DIM:], pos)
    c_new = rms_norm(c_kv, g_kv)
    kr_new = rope(k_r, pos)
    c_all = c_new if past_c is None else jnp.concatenate([past_c, c_new], axis=1)
    kr_all = kr_new if past_kr is None else jnp.concatenate([past_kr, kr_new], axis=1)
    q_lat = jnp.einsum('bqhd,rhd->bqhr', q_nope, w_uk)
    o_lat = mla_attend(q_lat, q_rope, c_all, kr_all, pos)
    o = jnp.einsum('bqhr,rhd->bqhd', o_lat, w_uv).reshape(B, Q, SELF_WIDTH)
    return o, c_new, kr_new


def moba_attend(q, k, v, q_pos):
    B, L = k.shape[0], k.shape[1]
    G = N_HEADS // KV_HEADS
    nb = -(-L // MOBA_BLOCK)
    pad = ((0, 0), (0, nb * MOBA_BLOCK - L), (0, 0), (0, 0))
    kb = jnp.pad(k, pad).reshape(B, nb, MOBA_BLOCK, KV_HEADS, HEAD_DIM).transpose(0, 3, 1, 2, 4)
    vb = jnp.pad(v, pad).reshape(B, nb, MOBA_BLOCK, KV_HEADS, HEAD_DIM).transpose(0, 3, 1, 2, 4)
    k_mean = jnp.mean(kb.astype(jnp.float32), axis=3)
    qg = q.reshape(B, q.shape[1], KV_HEADS, G, HEAD_DIM)
    gate = jnp.einsum('bqkgd,bknd->bkgqn', qg.astype(jnp.float32), k_mean)
    own = q_pos // MOBA_BLOCK
    gate = jnp.where(jnp.arange(nb)[None, :] < own[:, None], gate, -jnp.inf)
    top_val, top_idx = lax.top_k(gate, min(MOBA_TOPK, nb))
    sel_ok = jnp.isfinite(top_val)
    own_idx = jnp.broadcast_to(own[:, None], top_idx.shape[:-1] + (1,)).astype(top_idx.dtype)
    idx = jnp.moveaxis(jnp.concatenate([top_idx, own_idx], -1), 3, 1)
    ok = jnp.moveaxis(jnp.concatenate([sel_ok, jnp.ones(own_idx.shape, bool)], -1), 3, 1)
    bi = jnp.arange(B)[:, None, None, None, None]
    hi = jnp.arange(KV_HEADS)[None, None, :, None, None]
    scale = HEAD_DIM ** -0.5

    def chunk(pos, qc, ic, oc):
        kg = kb[bi, hi, ic]
        vg = vb[bi, hi, ic]
        kpos = ic[..., None] * MOBA_BLOCK + jnp.arange(MOBA_BLOCK)
        mask = oc[..., None] & (kpos <= pos[None, :, None, None, None, None])
        s = jnp.einsum('bckgd,bckgnjd->bckgnj', qc, kg).astype(jnp.float32) * scale
        s = jnp.where(mask, s, -jnp.inf)
        p = jax.nn.softmax(s.reshape(s.shape[:4] + (-1,)), axis=-1).reshape(s.shape).astype(vg.dtype)
        return jnp.einsum('bckgnj,bckgnjd->bckgd', p, vg)

    o = map_query_blocks(chunk, q_pos, (qg, idx, ok), MOBA_Q_CHUNK)
    return o.reshape(B, q.shape[1], N_HEADS * HEAD_DIM)


def moba_mixer(h, pos, past_k, past_v):
    B, Q = h.shape[0], h.shape[1]
    nq, nk = N_HEADS * HEAD_DIM, KV_HEADS * HEAD_DIM
    q = rope(h[..., :nq].reshape(B, Q, N_HEADS, HEAD_DIM), pos)
    k = rope(h[..., nq:nq + nk].reshape(B, Q, KV_HEADS, HEAD_DIM), pos)
    v = h[..., nq + nk:nq + 2 * nk].reshape(B, Q, KV_HEADS, HEAD_DIM)
    k_all = k if past_k is None else jnp.concatenate([past_k, k], axis=1)
    v_all = v if past_v is None else jnp.concatenate([past_v, v], axis=1)
    return moba_attend(q, k_all, v_all, pos), k, v


def mem_attend(qm, mk, mv):
    s = jnp.einsum('bqhd,bmhd->bhqm', qm, mk).astype(jnp.float32) * (MEM_HEAD_DIM ** -0.5)
    p = jax.nn.softmax(s, axis=-1).astype(mv.dtype)
    return jnp.einsum('bhqm,bmhd->bqhd', p, mv)


def conv_ffn(x, conv_state, w_up, conv_w, conv_b, w_down):
    Q = x.shape[1]
    ug = x @ w_up
    u, g = ug[..., :D_FF], ug[..., D_FF:]
    g_ext = jnp.concatenate([conv_state.astype(g.dtype), g], axis=1)
    gc = conv_b
    for t in range(CONV_W):
        gc = gc + conv_w[t] * g_ext[:, t:t + Q]
    h = jax.nn.gelu(gc, approximate=False) * u
    return h @ w_down, g_ext[:, -(CONV_W - 1):]


def setup_inputs(seed: int = 0) -> dict:
    key = jax.random.key(seed)
    ks = iter(jax.random.split(key, 40))
    nrm = lambda shape, s=1.0: jax.random.normal(next(ks), shape, jnp.float32) * s
    n_pages = PAST_LEN // PAGE_SIZE
    n_pool = (5 * DEC_BATCH * n_pages + 3) // 4
    x_prompt = nrm((BATCH, SEQ, D_MODEL))
    x_sample = nrm((DEC_BATCH, DEC_SEQ, D_MODEL))
    cache_mla_ckv = nrm((N_A, n_pool, PAGE_SIZE, KV_LORA))
    cache_mla_krope = nrm((N_A, n_pool, PAGE_SIZE, ROPE_DIM))
    cache_moba_k = nrm((N_B, n_pool, PAGE_SIZE, KV_HEADS, HEAD_DIM))
    cache_moba_v = nrm((N_B, n_pool, PAGE_SIZE, KV_HEADS, HEAD_DIM))
    cache_mem_k = nrm((DEPTH, DEC_BATCH, MEM_LEN, MEM_HEADS, MEM_HEAD_DIM))
    cache_mem_v = nrm((DEPTH, DEC_BATCH, MEM_LEN, MEM_HEADS, MEM_HEAD_DIM))
    state_conv = nrm((DEPTH, DEC_BATCH, CONV_W - 1, D_FF))
    perm = jax.random.permutation(next(ks), n_pool)
    page_table = perm[:DEC_BATCH * n_pages].reshape(DEC_BATCH, n_pages).astype(jnp.int32)
    mem_prompt = nrm((BATCH, MEM_LEN, D_MODEL))
    return {
        "x_prompt": x_prompt, "x_sample": x_sample,
        "cache_mla_ckv": cache_mla_ckv, "cache_mla_krope": cache_mla_krope,
        "cache_moba_k": cache_moba_k, "cache_moba_v": cache_moba_v,
        "cache_mem_k": cache_mem_k, "cache_mem_v": cache_mem_v,
        "state_conv": state_conv, "page_table": page_table, "mem_prompt": mem_prompt,
        "w_in_a": nrm((N_A, D_MODEL, IN_A), D_MODEL ** -0.5),
        "g_q": 1.0 + nrm((N_A, Q_LORA), 0.01),
        "w_q_b": nrm((N_A, Q_LORA, N_HEADS * (NOPE_DIM + ROPE_DIM)), Q_LORA ** -0.5),
        "g_kv": 1.0 + nrm((N_A, KV_LORA), 0.01),
        "w_uk": nrm((N_A, KV_LORA, N_HEADS, NOPE_DIM), KV_LORA ** -0.5),
        "w_uv": nrm((N_A, KV_LORA, N_HEADS, V_DIM), KV_LORA ** -0.5),
        "w_in_b": nrm((N_B, D_MODEL, IN_B), D_MODEL ** -0.5),
        "w_mem_k": nrm((DEPTH, D_MODEL, MEM_WIDTH), D_MODEL ** -0.5),
        "w_mem_v": nrm((DEPTH, D_MODEL, MEM_WIDTH), D_MODEL ** -0.5),
        "w_o": nrm((DEPTH, MIX_WIDTH, D_MODEL), BETA * MIX_WIDTH ** -0.5),
        "ln1_g": 1.0 + nrm((DEPTH, D_MODEL), 0.01),
        "ln1_b": nrm((DEPTH, D_MODEL), 0.01),
        "w_up": nrm((DEPTH, D_MODEL, 2 * D_FF), D_MODEL ** -0.5),
        "conv_w": nrm((DEPTH, CONV_W, D_FF), CONV_W ** -0.5),
        "conv_b": nrm((DEPTH, D_FF), 0.01),
        "w_down": nrm((DEPTH, D_FF, D_MODEL), BETA * D_FF ** -0.5),
        "ln2_g": 1.0 + nrm((DEPTH, D_MODEL), 0.01),
        "ln2_b": nrm((DEPTH, D_MODEL), 0.01),
    }


def reference(x_prompt, x_sample, cache_mla_ckv, cache_mla_krope, cache_moba_k, cache_moba_v,
              cache_mem_k, cache_mem_v, state_conv, page_table, mem_prompt,
              w_in_a, g_q, w_q_b, g_kv, w_uk, w_uv, w_in_b, w_mem_k, w_mem_v, w_o,
              ln1_g, ln1_b, w_up, conv_w, conv_b, w_down, ln2_g, ln2_b):
    pos_p = jnp.arange(SEQ, dtype=jnp.int32)
    pos_s = PAST_LEN + jnp.arange(DEC_SEQ, dtype=jnp.int32)

    def layer(i, x, pos, mk, mv, past, conv_state):
        j = i // 2
        B, Q = x.shape[0], x.shape[1]
        if i % 2 == 0:
            proj = x @ w_in_a[j]
            self_out, r0, r1 = mla_mixer(proj[..., :IN_A - MEM_WIDTH], pos, past[0], past[1],
                                         g_q[j], w_q_b[j], g_kv[j], w_uk[j], w_uv[j])
        else:
            proj = x @ w_in_b[j]
            self_out, r0, r1 = moba_mixer(proj[..., :IN_B - MEM_WIDTH], pos, past[0], past[1])
        qm = proj[..., -MEM_WIDTH:].reshape(B, Q, MEM_HEADS, MEM_HEAD_DIM)
        mem_out = mem_attend(qm, mk, mv).reshape(B, Q, MEM_WIDTH)
        mix = jnp.concatenate([self_out, mem_out], axis=-1) @ w_o[i]
        x = layer_norm(ALPHA * x + mix, ln1_g[i], ln1_b[i])
        f, conv_new = conv_ffn(x, conv_state, w_up[i], conv_w[i], conv_b[i], w_down[i])
        x = layer_norm(ALPHA * x + f, ln2_g[i], ln2_b[i])
        return x, r0, r1, conv_new

    xp, xs = x_prompt, x_sample
    rows_p = ([], [])
    rows_s = ([], [])
    memk_p, memv_p, conv_p, conv_s = [], [], [], []
    for i in range(DEPTH):
        j = i // 2
        Bp = mem_prompt.shape[0]
        mk = (mem_prompt @ w_mem_k[i]).reshape(Bp, MEM_LEN, MEM_HEADS, MEM_HEAD_DIM)
        mv = (mem_prompt @ w_mem_v[i]).reshape(Bp, MEM_LEN, MEM_HEADS, MEM_HEAD_DIM)
        zero_conv = jnp.zeros((xp.shape[0], CONV_W - 1, D_FF), xp.dtype)
        xp, r0, r1, cp = layer(i, xp, pos_p, mk, mv, (None, None), zero_conv)
        if i % 2 == 0:
            past_s = (gather_pages(cache_mla_ckv[j], page_table), gather_pages(cache_mla_krope[j], page_table))
        else:
            past_s = (gather_pages(cache_moba_k[j], page_table), gather_pages(cache_moba_v[j], page_table))
        xs, s0, s1, cs = layer(i, xs, pos_s, cache_mem_k[i], cache_mem_v[i], past_s, state_conv[i])
        rows_p[i % 2].append((r0, r1))
        rows_s[i % 2].append((s0, s1))
        memk_p.append(mk)
        memv_p.append(mv)
        conv_p.append(cp)
        conv_s.append(cs)

    ckv_p = jnp.stack([r[0] for r in rows_p[0]])
    krope_p = jnp.stack([r[1] for r in rows_p[0]])
    mobak_p = jnp.stack([r[0] for r in rows_p[1]])
    mobav_p = jnp.stack([r[1] for r in rows_p[1]])
    ckv_s = jnp.stack([r[0] for r in rows_s[0]])
    krope_s = jnp.stack([r[1] for r in rows_s[0]])
    mobak_s = jnp.stack([r[0] for r in rows_s[1]])
    mobav_s = jnp.stack([r[1] for r in rows_s[1]])
    memk_out = jnp.stack(memk_p)
    memv_out = jnp.stack(memv_p)
    conv_p_out = jnp.stack(conv_p)
    conv_s_out = jnp.stack(conv_s)
    return (xp, xs, ckv_p, krope_p, mobak_p, mobav_p, memk_out, memv_out, conv_p_out,
            ckv_s, krope_s, mobak_s, mobav_s, conv_s_out)
```

```python
from contextlib import ExitStack
import numpy as np
import concourse.bass as bass
import concourse.mybir as mybir

F32 = mybir.dt.float32
BF16 = mybir.dt.bfloat16
I32 = mybir.dt.int32
ALU = mybir.AluOpType
AF = mybir.ActivationFunctionType
AX = mybir.AxisListType


class Tile:
    __slots__ = ("t", "name", "lw", "readers", "dsem", "dcount", "sch", "excl")

    def __init__(self, sch, t, name):
        self.sch = sch
        self.t = t
        self.name = name
        self.lw = None
        self.readers = {}
        self.dsem = None
        self.dcount = 0
        self.excl = False

    def __getitem__(self, k):
        return self.t[k]


class Eng:
    def __init__(self, name, h, sem):
        self.name = name
        self.h = h
        self.sem = sem
        self.count = 0
        self.waited = {}
        self.prog = []


class Sched:
    def __init__(self, nc, es):
        self.nc = nc
        self.es = es
        self.eng = {}
        self.sb_bytes = 0
        self.nsem = 0
        self.dma_tiles = []
        for name, h in (("pe", nc.tensor), ("act", nc.scalar), ("dve", nc.vector),
                        ("pool", nc.gpsimd), ("sp", nc.sync)):
            self.eng[name] = Eng(name, h, self.new_sem("e_" + name))

    def new_sem(self, name):
        self.nsem += 1
        return self.es.enter_context(self.nc.semaphore(name))

    def sbuf(self, name, shape, dtype):
        t = self.es.enter_context(self.nc.sbuf_tensor("sb_" + name, list(shape), dtype))
        n = 1
        for d in shape[1:]:
            n *= d
        self.sb_bytes += n * (4 if dtype in (F32, I32) else 2)
        return Tile(self, t, name)

    def psum(self, name, shape, dtype):
        t = self.es.enter_context(self.nc.psum_tensor("pp_" + name, list(shape), dtype))
        tl = Tile(self, t, name)
        tl.excl = True
        return tl

    def view(self, tile, name=None):
        return Tile(self, tile.t, name or tile.name)

    def _deps(self, E, reads, writes):
        deps = []
        for t in reads:
            if t.lw is not None:
                deps.append(t.lw)
            if t.excl:
                deps.extend(ev for k, ev in t.readers.items() if k != id(E.sem))
        for t in writes:
            if t.lw is not None:
                deps.append(t.lw)
            deps.extend(t.readers.values())
        for sem, val in deps:
            if E.name == "pe" and sem is E.sem:
                continue
            k = id(sem)
            if E.waited.get(k, 0) < val:
                E.waited[k] = val
                E.prog.append(("w", sem, val))

    def op(self, eng, fn, reads=(), writes=()):
        E = self.eng[eng]
        self._deps(E, reads, writes)
        E.count += 1
        E.prog.append(("op", fn))
        ev = (E.sem, E.count)
        for t in reads:
            t.readers[id(E.sem)] = ev
        for t in writes:
            t.lw = ev
            t.readers = {}

    def dma(self, eng, out, in_, reads=(), writes=(), fn=None, **kw):
        E = self.eng[eng]
        self._deps(E, reads, writes)
        tl = list(writes) + list(reads)
        t = tl[0]
        if t.dsem is None:
            t.dsem = self.new_sem("d_" + t.name)
            self.dma_tiles.append(t)
        t.dcount += 1
        ev = (t.dsem, 16 * t.dcount)
        E.prog.append(("dma", out, in_, t.dsem, kw, fn))
        for x in reads:
            x.readers[id(t.dsem)] = ev
        for x in writes:
            x.lw = ev
            x.readers = {}

    def finish(self):
        E = self.eng["sp"]
        for t in self.dma_tiles:
            E.prog.append(("w", t.dsem, 16 * t.dcount))

    def emit(self):
        self.finish()
        with self.nc.Block() as block:
            def run(E):
                def body(e):
                    for it in E.prog:
                        if it[0] == "w":
                            e.wait_ge(it[1], it[2])
                        elif it[0] == "op":
                            it[1](e).then_inc(E.sem, 1)
                        else:
                            _, out, in_, dsem, kw, fn = it
                            if fn is not None:
                                fn(e).then_inc(dsem, 16)
                            else:
                                e.dma_start(out=out, in_=in_, **kw).then_inc(dsem, 16)
                return body
            block.tensor(run(self.eng["pe"]))
            block.scalar(run(self.eng["act"]))
            block.vector(run(self.eng["dve"]))
            block.gpsimd(run(self.eng["pool"]))
            block.sync(run(self.eng["sp"]))

from concourse.bass_utils import run_bass_kernel_spmd

D = 1024
SEQ = 2048
NT = 16
DFF = 2816
NFC = 22
ALPHA = float((2 * 2) ** 0.25)
LN_EPS = 1e-5
RMS_EPS = 1e-6
BIG = 30000.0
NPOOL = 5120
FLAGS = {"sample": True, "layers": 2, "stop": "", "prompt": True}


class StopBuild(Exception):
    pass


class K:
    pass


def build_program(flags):
    nc = bass.Bass("TRN2", target_bir_lowering=False)
    g = K()
    g.nc = nc

    def din(name, shape, dt=F32):
        return nc.dram_tensor(name, list(shape), dt, kind="ExternalInput").ap()

    def dout(name, shape, dt=F32):
        return nc.dram_tensor(name, list(shape), dt, kind="ExternalOutput").ap()

    I = {}
    for name, shape in [
        ("xp", [SEQ, D]), ("memp", [256, D]), ("ident", [128, 128]), ("masks", [128, 4 * 512]),
        ("c96", [96, SEQ]), ("s96", [96, SEQ]), ("cos32", [SEQ, 32]), ("sin32", [SEQ, 32]),
        ("cos64", [SEQ, 64]), ("sin64", [SEQ, 64]), ("onehot8", [8, SEQ]),
        ("w_in_a", [D, 928]), ("g_q", [1, 384]), ("w_q_b", [384, 1152]), ("w_q_b_sw", [384, 1152]),
        ("g_kv", [1, 256]), ("w_uk", [256, 768]), ("w_uv", [256, 768]), ("w_in_b", [D, 1536]),
        ("w_mem_k", [2, D, 256]), ("w_mem_v", [2, D, 256]), ("w_o", [2, D, D]),
        ("ln1_g", [2, D]), ("ln1_b", [2, D]), ("ln2_g", [2, D]), ("ln2_b", [2, D]),
        ("w_up", [2, D, 2 * DFF]), ("w_down", [2, DFF, D]),
        ("convw", [2, 128, NFC * 3]), ("convb", [2, 128, NFC]),
    ]:
        I[name] = din(name, shape)
    O = {}
    for name, shape in [
        ("y_p", [SEQ, D]), ("ckv_p", [SEQ, 256]), ("krope_p", [SEQ, 32]), ("mobak_p", [SEQ, 256]),
        ("mobav_p", [SEQ, 256]), ("memk_p", [2, 256, 256]), ("memv_p", [2, 256, 256]),
        ("conv_p", [2, 128, NFC * 2]),
    ]:
        O[name] = dout(name, shape)
    if flags["sample"]:
        declare_sample_io(g, I, O, din, dout)
    X1 = nc.dram_tensor("x1_scr", [SEQ, D], F32, kind="Internal").ap()
    X2 = nc.dram_tensor("x2_scr", [SEQ, D], F32, kind="Internal").ap()

    with ExitStack() as es:
        S = Sched(nc, es)
        g.S = S
        g.I = I
        g.O = O
        g.ps = [S.psum("ps%d" % i, [128, 512], F32) for i in range(4)]
        g.po = [S.psum("po%d" % i, [128, 512], F32) for i in range(2)]
        g.pt = [S.psum("pt%d" % i, [128, 1024], BF16) for i in range(2)]
        g.ips = 0
        g.ipo = 0
        g.ipt = 0

        def nps():
            g.ips = (g.ips + 1) % 4
            return g.ps[g.ips]

        def npo():
            g.ipo = (g.ipo + 1) % 2
            return g.po[g.ipo]

        def npt():
            g.ipt = (g.ipt + 1) % 2
            return g.pt[g.ipt]
        g.nps, g.npo, g.npt = nps, npo, npt

        ident = S.sbuf("ident", [128, 128], BF16)
        masks = S.sbuf("masks", [128, 4, 512], BF16)
        g.ident = ident
        S.dma("pool", ident[:], I["ident"][:, :], writes=[ident])
        S.dma("pool", masks[:], I["masks"].rearrange("p (j q) -> p j q", j=4), writes=[masks])
        XT = S.sbuf("XT", [128, 8, SEQ], BF16)
        MIX = S.sbuf("MIX", [128, 8, SEQ], BF16)
        BUF = S.sbuf("BUF", [128, 10240], F32)
        WA = S.sbuf("WA", [128, 8 * 1536], BF16)
        WB = S.sbuf("WB", [128, 10240], BF16)
        TAB = S.sbuf("TAB", [128, 4096], F32)
        lnp = S.sbuf("lnp", [128, 2, D], F32)
        gqb = S.sbuf("gqb", [128, 384 + 256], F32)
        cvw = S.sbuf("cvw", [128, NFC, 3], F32)
        cvb = S.sbuf("cvb", [128, NFC], F32)
        cst = S.sbuf("cst", [128, NFC, 2], F32)
        xf = [S.sbuf("xf%d" % i, [128, D], F32) for i in range(1)]
        xb = [S.sbuf("xb%d" % i, [128, D], BF16) for i in range(1)]
        zt = [S.sbuf("zt%d" % i, [128, D], F32) for i in range(1)]
        sq = S.sbuf("sq", [128, D], F32)
        st = S.sbuf("st", [128, 16], F32)
        pT = [S.sbuf("pT%d" % i, [128, 512], BF16) for i in range(3)]
        rc = [S.sbuf("rc%d" % i, [128, 512], F32) for i in range(1)]
        stg = [S.sbuf("stg%d" % i, [128, 512], F32) for i in range(2)]
        sgb = [S.sbuf("sgb%d" % i, [128, 1024], BF16) for i in range(1)]
        g.ctr = {}

        def rot(lst, key):
            g.ctr[key] = (g.ctr.get(key, -1) + 1) % len(lst)
            return lst[g.ctr[key]]

        def MM(ps, out, lhsT, rhs, start, stop, reads):
            S.op("pe", lambda e, o=out, l=lhsT, r=rhs, a=start, b=stop: e.matmul(o, lhsT=l, rhs=r, start=a, stop=b),
                 reads=reads, writes=[ps])

        def TR(pt_t, out, in_, reads, npart=128):
            S.op("pe", lambda e, o=out, i=in_, n=npart: e.transpose(out=o, in_=i, identity=ident[0:n, 0:n]),
                 reads=list(reads) + [ident], writes=[pt_t])

        def ACT(out, in_, func, reads, writes, bias=None, scale=None):
            kw = {}
            if bias is not None:
                kw["bias"] = bias
            if scale is not None:
                kw["scale"] = scale
            S.op("act", lambda e, o=out, i=in_, f=func, k=kw: e.activation(out=o, in_=i, func=f, **k),
                 reads=reads, writes=writes)

        def TT(eng, out, in0, in1, op, reads, writes):
            S.op(eng, lambda e, o=out, a=in0, b=in1, p=op: e.tensor_tensor(out=o, in0=a, in1=b, op=p),
                 reads=reads, writes=writes)

        def TS(eng, out, in0, s1, s2, op0, op1, reads, writes):
            if op1 is None:
                S.op(eng, lambda e, o=out, a=in0, x=s1, p=op0: e.tensor_scalar(out=o, in0=a, scalar1=x, scalar2=None, op0=p),
                     reads=reads, writes=writes)
            else:
                S.op(eng, lambda e, o=out, a=in0, x=s1, y=s2, p=op0, q=op1:
                     e.tensor_scalar(out=o, in0=a, scalar1=x, scalar2=y, op0=p, op1=q), reads=reads, writes=writes)

        def STT(eng, out, in0, sc, in1, op0, op1, reads, writes):
            S.op(eng, lambda e, o=out, a=in0, s=sc, b=in1, p=op0, q=op1:
                 e.scalar_tensor_tensor(out=o, in0=a, scalar=s, in1=b, op0=p, op1=q), reads=reads, writes=writes)

        def CP(eng, out, in_, reads, writes):
            S.op(eng, lambda e, o=out, i=in_: e.tensor_copy(out=o, in_=i), reads=reads, writes=writes)

        def MS(eng, ap, val, writes):
            S.op(eng, lambda e, a=ap, v=val: e.memset(a, v), writes=writes)

        def RS(out, in_, reads, writes):
            S.op("dve", lambda e, o=out, i=in_: e.reduce_sum(out=o, in_=i, axis=AX.X), reads=reads, writes=writes)

        def RCP(out, in_, reads, writes):
            S.op("dve", lambda e, o=out, i=in_: e.reciprocal(out=o, in_=i), reads=reads, writes=writes)

        def barrier():
            evs = [(E.sem, E.count) for E in S.eng.values() if E.count > 0]
            evs += [(t.dsem, 16 * t.dcount) for t in S.dma_tiles]
            for E in S.eng.values():
                for sem, val in evs:
                    if sem is E.sem:
                        continue
                    k = id(sem)
                    if E.waited.get(k, 0) < val:
                        E.waited[k] = val
                        E.prog.append(("w", sem, val))
        g.MM, g.TR, g.ACT, g.TT, g.TS, g.STT, g.CP, g.MS, g.RS, g.RCP, g.barrier, g.rot = \
            MM, TR, ACT, TT, TS, STT, CP, MS, RS, RCP, barrier, rot
        g.stg, g.sgb, g.st, g.sq, g.zt, g.xf, g.xb, g.pT, g.rc = stg, sgb, st, sq, zt, xf, xb, pT, rc
        g.lnp, g.gqb, g.cvw, g.cvb = lnp, gqb, cvw, cvb

        import os
        def wload(dst_tile, dst_ap, src_ap):
            if os.environ.get("SKIPW") == "1":
                return
            for kc in range(dst_ap.shape[1]):
                S.dma("pool", dst_ap[:, kc, :], src_ap[:, kc, :], writes=[dst_tile])
        g.wload = wload

        def kcw(w2d):
            return w2d.rearrange("(kc p) n -> p kc n", p=128)

        def rstd_of(ps_slice, n, np_, eps, reads):
            ACT(sq[0:np_, 0:n], ps_slice, AF.Square, reads, [sq])
            RS(st[0:np_, 0:1], sq[0:np_, 0:n], [sq], [st])
            ACT(st[0:np_, 1:2], st[0:np_, 0:1], AF.Sqrt, [st], [st], bias=eps, scale=1.0 / n)
            RCP(st[0:np_, 2:3], st[0:np_, 1:2], [st], [st])
            return st[0:np_, 2:3]
        g.rstd_of = rstd_of

        def layernorm(z, np_, li, out_tile, out_ap):
            zz = z[0:np_, :]
            RS(st[0:np_, 4:5], zz, [z], [st])
            ACT(sq[0:np_, :], zz, AF.Square, [z], [sq])
            RS(st[0:np_, 5:6], sq[0:np_, :], [sq], [st])
            TS("dve", st[0:np_, 6:7], st[0:np_, 4:5], 1.0 / D, None, ALU.mult, None, [st], [st])
            TT("dve", st[0:np_, 7:8], st[0:np_, 6:7], st[0:np_, 6:7], ALU.mult, [st], [st])
            STT("dve", st[0:np_, 8:9], st[0:np_, 5:6], 1.0 / D, st[0:np_, 7:8], ALU.mult, ALU.subtract, [st], [st])
            ACT(st[0:np_, 9:10], st[0:np_, 8:9], AF.Sqrt, [st], [st], bias=LN_EPS, scale=1.0)
            RCP(st[0:np_, 10:11], st[0:np_, 9:10], [st], [st])
            TS("dve", zz, zz, st[0:np_, 6:7], st[0:np_, 10:11], ALU.subtract, ALU.mult, [z, st], [z])
            TT("pool", zz, zz, lnp[0:np_, 0, :], ALU.mult, [z, lnp], [z])
            TT("pool", out_ap, zz, lnp[0:np_, 1, :], ALU.add, [z, lnp], [out_tile])
        g.layernorm = layernorm

        def to_xT(src_bf, dst, t):
            pt_t = npt()
            for kc in range(8):
                TR(pt_t, pt_t[:, kc * 128:(kc + 1) * 128], src_bf[:, kc * 128:(kc + 1) * 128], [src_bf])
            CP("dve", dst[:, :, t * 128:(t + 1) * 128], pt_t[:, :].rearrange("p (k n) -> p k n", k=8), [pt_t], [dst])
        g.to_xT = to_xT

        def attention(qT, qtile, kT, ktile, Kp0, Kp1, vaug_fn, vtile, nkt, causal, scale, den_lo, out_chunk):
            num0 = 64 if den_lo else 0
            den0 = 0 if den_lo else 64
            for qb in range(4):
                po = npo()
                kts = list(range(4 * qb + 4)) if causal else list(range(nkt))
                for i, kt in enumerate(kts):
                    ps = nps()
                    MM(ps, ps[:, :], kT[Kp0:Kp1, kt * 128:(kt + 1) * 128], qT[Kp0:Kp1, qb * 512:(qb + 1) * 512],
                       True, True, [ktile, qtile])
                    p = rot(pT, "pT")
                    ACT(p[:, :], ps[:, :], AF.Exp, [ps], [p], scale=scale)
                    if causal and kt >= 4 * qb:
                        TT("pool", p[:, :], p[:, :], masks[:, kt - 4 * qb, :], ALU.mult, [p, masks], [p])
                    MM(po, po[:, :], vaug_fn(kt), p[:, :], i == 0, i == len(kts) - 1, [vtile, p])
                r = rot(rc, "rc")
                RCP(r[den0:den0 + 64, :], po[den0:den0 + 64, :], [po], [r])
                TT("dve", MIX[num0:num0 + 64, out_chunk, qb * 512:(qb + 1) * 512], po[num0:num0 + 64, :],
                   r[den0:den0 + 64, :], ALU.mult, [po, r], [MIX])
        g.attention = attention

        BUFb = S.view(BUF)
        Gk = [S.view(XT, "Gk0"), S.view(XT, "Gk1")]
        Gq = [S.view(XT, "Gq0"), S.view(XT, "Gq1")]
        Gv = [S.view(XT, "Gv0"), S.view(XT, "Gv1")]
        Gt = [S.view(BUF, "Gt0"), S.view(BUF, "Gt1")]
        g.mkT_sb = S.sbuf("mkT_sb", [128, 2, 256], BF16)
        g.vmem_sb = S.sbuf("vmem_sb", [128, 2, 4, 128], BF16)
        g.X1t = Tile(S, None, "X1t")
        g.X2t = Tile(S, None, "X2t")
        g.wtile = [S.view(WB, "wt0"), S.view(WB, "wt1")]
        g.wdtile = [S.view(WA, "wd0"), S.view(WA, "wd1")]
        g.gftile = [S.view(MIX, "gf0"), S.view(MIX, "gf1")]
        g.fttile = [S.view(MIX, "ft0"), S.view(MIX, "ft1")]
        g.hTtile = S.view(MIX, "hTt")

        def bview(off_words, nwords, pattern=None, **kw):
            ap = BUF[:, off_words:off_words + nwords].bitcast(BF16)
            if pattern:
                ap = ap.rearrange(pattern, **kw)
            return ap

        xsrc = I["xp"]
        def stop_if(tag):
            if flags.get("stop") == tag:
                raise StopBuild()
        try:
          for L in range(flags["layers"] if flags.get("prompt", True) else 0):
              mla = (L == 0)
              barrier()
              def load_ln(which):
                  for i, nm in enumerate([which + "_g", which + "_b"]):
                      S.dma("sp", lnp[:, i, :], I[nm][L:L + 1, :].partition_broadcast(128)[:, 0, :], writes=[lnp])
              load_ln("ln1")
              S.dma("sp", cvw[:], I["convw"][L].rearrange("p (c t) -> p c t", t=3), writes=[cvw])
              S.dma("sp", cvb[:], I["convb"][L], writes=[cvb])
              WMt = WA if mla else WB
              WM = WMt[:, 8192:12288] if mla else WMt[:, 0:4096]
              wmk = WM[:, 0:2048].rearrange("p (k n) -> p k n", k=8)
              wmv = WM[:, 2048:4096].rearrange("p (k n) -> p k n", k=8)
              wload(WMt, wmk, kcw(I["w_mem_k"][L]))
              wload(WMt, wmv, kcw(I["w_mem_v"][L]))
              if mla:
                  NIN = 928
                  wload(WA, WA[:, 0:8 * NIN].rearrange("p (k n) -> p k n", k=8), kcw(I["w_in_a"]))
                  S.dma("sp", gqb[:, 0:384], I["g_q"].partition_broadcast(128)[:, 0, :], writes=[gqb])
                  S.dma("sp", gqb[:, 384:640], I["g_kv"].partition_broadcast(128)[:, 0, :], writes=[gqb])
                  S.dma("sp", TAB[:, 0:512].rearrange("p (t d) -> p t d", d=32), I["cos32"].rearrange("(t p) d -> p t d", p=128), writes=[TAB])
                  S.dma("sp", TAB[:, 512:1024].rearrange("p (t d) -> p t d", d=32), I["sin32"].rearrange("(t p) d -> p t d", p=128), writes=[TAB])
                  cosT = TAB[:, 0:512].rearrange("p (t d) -> p t d", d=32)
                  sinT = TAB[:, 512:1024].rearrange("p (t d) -> p t d", d=32)
              else:
                  NIN = 1536
                  wload(WA, WA[:, 0:8 * NIN].rearrange("p (k n) -> p k n", k=8), kcw(I["w_in_b"]))
                  S.dma("sp", TAB[:, 0:1024].rearrange("p (t d) -> p t d", d=64), I["cos64"].rearrange("(t p) d -> p t d", p=128), writes=[TAB])
                  S.dma("sp", TAB[:, 1024:2048].rearrange("p (t d) -> p t d", d=64), I["sin64"].rearrange("(t p) d -> p t d", p=128), writes=[TAB])
                  cosT = TAB[:, 0:1024].rearrange("p (t d) -> p t d", d=64)
                  sinT = TAB[:, 1024:2048].rearrange("p (t d) -> p t d", d=64)
              win = WA[:, 0:8 * NIN].rearrange("p (k n) -> p k n", k=8)

              if mla:
                  qnT = bview(0, 3072, "p (k n) -> p k n", k=3)
                  cnT = bview(3072, 2048, "p (k n) -> p k n", k=2)
                  krT = bview(5120, 1024)
                  qmT = bview(6144, 2048, "p (k n) -> p k n", k=2)
                  XT2 = XT[:, :, :].rearrange("p k n -> p (k n)")
                  kTh = [XT2[:, 0:2048], XT2[:, 2048:4096]]
                  qTh = [XT2[:, 4096:6144], XT2[:, 6144:8192]]
                  vaug = [XT2[:, 8192:10240].rearrange("p (t n) -> p t n", t=16), XT2[:, 10240:12288].rearrange("p (t n) -> p t n", t=16)]
                  tmpf = [BUF[:, 8960:9472], BUF[:, 9472:9984]]
              elif flags.get('stop') != 'P0all':
                  XT2 = XT[:, :, :].rearrange("p k n -> p (k n)")
                  qtok = XT2[:, 0:12288].rearrange("p (t h d) -> p t h d", t=16, h=12)
                  qTa = XT2[:, 12288:14336]
                  qa = XT2[:, 14336:15488].rearrange("p (t c) -> p t c", t=16)
                  kTa = bview(0, 4096, "p (k n) -> p k n", k=4)
                  vtok = bview(4096, 2048, "p (t n) -> p t n", t=16)
                  qmT = bview(6144, 2048, "p (k n) -> p k n", k=2)
                  vaug = [bview(8192, 1024, "p (t n) -> p t n", t=16), bview(9216, 1024, "p (t n) -> p t n", t=16)]
                  xTt = WB[:, 4096:5120].rearrange("p (k n) -> p k n", k=8)
                  biasb = WB[:, 5120:6656].rearrange("p (t h b) -> p t h b", t=16, h=12)
                  qTt = WB[:, 6656:8192].rearrange("p (h n) -> p h n", h=12)
                  sbm = WB[:, 8192:8704]
                  kmT = WB[:, 8704:8736].rearrange("p (k b) -> p k b", k=4)
                  G_qtok, G_qTa, G_qa = S.view(XT, "G_qtok"), S.view(XT, "G_qTa"), S.view(XT, "G_qa")
                  G_kTa, G_vtok, G_qmT = S.view(BUF, "G_kTa"), S.view(BUF, "G_vtok"), S.view(BUF, "G_qmT")
                  G_xTt, G_bias, G_qTt, G_sbm, G_kmT = (S.view(WB, "G_xTt"), S.view(WB, "G_bias"), S.view(WB, "G_qTt"),
                                                       S.view(WB, "G_sbm"), S.view(WB, "G_kmT"))
              stop_if('S0')
              memT = MIX
              for mt in range(2):
                  f = rot(xf, "xf")
                  S.dma("sp", f[:], I["memp"][mt * 128:(mt + 1) * 128, :], writes=[f])
                  b = rot(xb, "xb")
                  ACT(b[:], f[:], AF.Copy, [f], [b])
                  to_xT(b, memT, mt)
              skipviews = flags.get("stop") == "P0all"
              mkT_ap, vmem_ap = g.mkT_sb[:, :, :], g.vmem_sb[:, :, :, :]
              mkT_tile, vmem_tile = g.mkT_sb, g.vmem_sb
              if not skipviews:
                  MS("pool", vmem_ap, 1.0, [vmem_tile])
              for mt in range(2):
                  for wi, (w, oname) in enumerate([(wmk, "memk_p"), (wmv, "memv_p")]):
                      ps = nps()
                      for kc in range(8):
                          MM(ps, ps[:, 0:256], memT[:, kc, mt * 128:(mt + 1) * 128], w[:, kc, :], kc == 0, kc == 7, [MIX, WMt])
                      sg = rot(stg, "stg")
                      CP("dve", sg[:, 0:256], ps[:, 0:256], [ps], [sg])
                      if True:
                          S.dma("sp", O[oname][L, mt * 128:(mt + 1) * 128, :], sg[:, 0:256], reads=[sg])
                      if wi == 1 and not skipviews:
                          for j in range(4):
                              c0 = 0 if j % 2 == 0 else 64
                              if os.environ.get("DBG") == "dvecp":
                                  CP("dve", vmem_ap[:, mt, j, c0:c0 + 64], ps[:, j * 64:(j + 1) * 64], [ps], [vmem_tile])
                              else:
                                  ACT(vmem_ap[:, mt, j, c0:c0 + 64], ps[:, j * 64:(j + 1) * 64], AF.Copy, [ps], [vmem_tile])
              for pr in range(2):
                  ps = nps()
                  for kc in range(8):
                      MM(ps, ps[:, 0:256], wmk[:, kc, pr * 128:(pr + 1) * 128], memT[:, kc, 0:256], kc == 0, kc == 7, [MIX, WMt])
                  if not skipviews:
                      if os.environ.get("DBG") == "dvecp":
                          CP("dve", mkT_ap[:, pr, :], ps[:, 0:256], [ps], [mkT_tile])
                      else:
                          ACT(mkT_ap[:, pr, :], ps[:, 0:256], AF.Copy, [ps], [mkT_tile])

              stop_if('P0')
              if flags.get('stop') == 'P0all':
                  continue
              if not mla:
                  MS("pool", biasb, -BIG, [G_bias])
                  for kvh in range(4):
                      S.dma("pool", kTa[64:72, kvh, :], I["onehot8"][:, :], writes=[G_kTa])
              for t in range(NT):
                  f = rot(xf, "xf")
                  S.dma("sp", f[:], xsrc[t * 128:(t + 1) * 128, :], writes=[f])
                  b = rot(xb, "xb")
                  ACT(b[:], f[:], AF.Copy, [f], [b])
                  if mla:
                      to_xT(b, XT, t)
                      xts = [XT[:, kc, t * 128:(t + 1) * 128] for kc in range(8)]
                      xtile = XT
                  else:
                      pt_x = npt()
                      for kc in range(8):
                          TR(pt_x, pt_x[:, kc * 128:(kc + 1) * 128], b[:, kc * 128:(kc + 1) * 128], [b])
                      CP("dve", xTt, pt_x[:, :].rearrange("p (k n) -> p k n", k=8), [pt_x], [G_xTt])
                      xts = [xTt[:, kc, :] for kc in range(8)]
                      xtile = G_xTt

                  def proj(c0, c1):
                      ps = nps()
                      for kc in range(8):
                          MM(ps, ps[:, 0:c1 - c0], xts[kc], win[:, kc, c0:c1], kc == 0, kc == 7, [xtile, WA])
                      return ps
                  if mla:
                      psA = proj(0, 384)
                      r = rstd_of(psA[:, 0:384], 384, 128, RMS_EPS, [psA])
                      sb = rot(sgb, "sgb")
                      STT("dve", sb[:, 0:384], psA[:, 0:384], r, gqb[:, 0:384], ALU.mult, ALU.mult, [psA, st, gqb], [sb])
                      psB = proj(384, 672)
                      r = rstd_of(psB[:, 0:256], 256, 128, RMS_EPS, [psB])
                      sg = rot(stg, "stg")
                      STT("dve", sg[:, 0:256], psB[:, 0:256], r, gqb[:, 384:640], ALU.mult, ALU.mult, [psB, st, gqb], [sg])
                      ACT(sb[:, 384:640], sg[:, 0:256], AF.Copy, [sg], [sb])
                      kx = psB[:, 256:288]
                      TT("dve", sg[:, 256:288], kx, cosT[:, t, :], ALU.mult, [psB, TAB], [sg])
                      TT("dve", sg[:, 288:304], psB[:, 272:288], sinT[:, t, 0:16], ALU.mult, [psB, TAB], [sg])
                      TT("dve", sg[:, 304:320], psB[:, 256:272], sinT[:, t, 16:32], ALU.mult, [psB, TAB], [sg])
                      TT("dve", sg[:, 256:288], sg[:, 256:288], sg[:, 288:320], ALU.add, [sg], [sg])
                      S.dma("sp", O["ckv_p"][t * 128:(t + 1) * 128, :], sg[:, 0:256], reads=[sg])
                      S.dma("sp", O["krope_p"][t * 128:(t + 1) * 128, :], sg[:, 256:288], reads=[sg])
                      MS("pool", sb[:, 640:704], 0.0, [sb])
                      ACT(sb[:, 704:736], sg[:, 256:288], AF.Copy, [sg], [sb])
                      psC = proj(672, 928)
                      ACT(sb[:, 768:1024], psC[:, 0:256], AF.Copy, [psC], [sb])
                      pt_t = npt()
                      for j in range(3):
                          TR(pt_t, pt_t[:, j * 128:(j + 1) * 128], sb[:, j * 128:(j + 1) * 128], [sb])
                      for j in range(2):
                          TR(pt_t, pt_t[:, (3 + j) * 128:(4 + j) * 128], sb[:, 384 + j * 128:384 + (j + 1) * 128], [sb])
                      TR(pt_t, pt_t[0:96, 5 * 128:6 * 128], sb[:, 640:736], [sb])
                      for j in range(2):
                          TR(pt_t, pt_t[:, (6 + j) * 128:(7 + j) * 128], sb[:, 768 + j * 128:768 + (j + 1) * 128], [sb])
                      tsl = slice(t * 128, (t + 1) * 128)
                      CP("dve", qnT[:, :, tsl], pt_t[:, 0:384].rearrange("p (k n) -> p k n", k=3), [pt_t], [BUFb])
                      CP("dve", cnT[:, :, tsl], pt_t[:, 384:640].rearrange("p (k n) -> p k n", k=2), [pt_t], [BUFb])
                      ACT(krT[0:96, tsl], pt_t[0:96, 640:768], AF.Copy, [pt_t], [BUFb])
                      ACT(qmT[:, :, tsl], pt_t[:, 768:1024].rearrange("p (k n) -> p k n", k=2), AF.Copy, [pt_t], [BUFb])
                  else:
                      tsl = slice(t * 128, (t + 1) * 128)
                      own = t // 2
                      cosB6 = cosT[:, t:t + 1, :].to_broadcast([128, 6, 64])
                      sinB6 = sinT[:, t:t + 1, :].to_broadcast([128, 6, 64])
                      cosB4 = cosT[:, t:t + 1, :].to_broadcast([128, 4, 64])
                      sinB4 = sinT[:, t:t + 1, :].to_broadcast([128, 4, 64])
                      t1 = sq[:, 0:384].rearrange("p (h d) -> p h d", h=6)
                      t2 = sq[:, 384:768].rearrange("p (h d) -> p h d", h=6)

                      def rope(src3, nh, cB, sB, dst3, dst_tile, src_tile):
                          a, b2 = t1[:, 0:nh, :], t2[:, 0:nh, :]
                          TT("dve", a, src3, cB, ALU.mult, [src_tile, TAB], [sq])
                          TT("dve", b2[:, :, 0:32], src3[:, :, 32:64], sB[:, :, 0:32], ALU.mult, [src_tile, TAB], [sq])
                          TT("dve", b2[:, :, 32:64], src3[:, :, 0:32], sB[:, :, 32:64], ALU.mult, [src_tile, TAB], [sq])
                          TT("dve", dst3, a, b2, ALU.add, [sq], [dst_tile])
                      for qh in range(2):
                          psq = proj(qh * 384, (qh + 1) * 384)
                          rope(psq[:, 0:384].rearrange("p (h d) -> p h d", h=6), 6, cosB6, sinB6,
                               qtok[:, t, qh * 6:(qh + 1) * 6, :], G_qtok, psq)
                      pskv = proj(768, 1280)
                      sgk = rot(stg, "stg")
                      rope(pskv[:, 0:256].rearrange("p (h d) -> p h d", h=4), 4, cosB4, sinB4,
                           sgk[:, 0:256].rearrange("p (h d) -> p h d", h=4), sgk, pskv)
                      S.dma("sp", O["mobak_p"][tsl, :], sgk[:, 0:256], reads=[sgk])
                      ACT(sbm[:, 0:256], sgk[:, 0:256], AF.Copy, [sgk], [G_sbm])
                      sgv = rot(stg, "stg")
                      CP("dve", sgv[:, 0:256], pskv[:, 256:512], [pskv], [sgv])
                      S.dma("sp", O["mobav_p"][tsl, :], sgv[:, 0:256], reads=[sgv])
                      ACT(vtok[:, t, :], sgv[:, 0:256], AF.Copy, [sgv], [G_vtok])
                      psm = proj(1280, 1536)
                      ACT(sbm[:, 256:512], psm[:, 0:256], AF.Copy, [psm], [G_sbm])
                      pt_t = npt()
                      for kvh in range(4):
                          TR(pt_t, pt_t[0:64, kvh * 128:(kvh + 1) * 128], sbm[:, kvh * 64:(kvh + 1) * 64], [G_sbm])
                      for j in range(2):
                          TR(pt_t, pt_t[:, (4 + j) * 128:(5 + j) * 128], sbm[:, 256 + j * 128:256 + (j + 1) * 128], [G_sbm])
                      CP("dve", kTa[0:64, :, tsl], pt_t[0:64, 0:512].rearrange("p (k n) -> p k n", k=4), [pt_t], [G_kTa])
                      ACT(qmT[:, :, tsl], pt_t[:, 512:768].rearrange("p (k n) -> p k n", k=2), AF.Copy, [pt_t], [G_qmT])
                      if own >= 1:
                          if own >= 4:
                              for h0, nh in ((0, 8), (8, 4)):
                                  ptq = npt()
                                  for hh in range(nh):
                                      TR(ptq, ptq[0:64, hh * 128:(hh + 1) * 128], qtok[:, t, h0 + hh, :], [G_qtok])
                                  CP("dve", qTt[0:64, h0:h0 + nh, :], ptq[0:64, 0:nh * 128].rearrange("p (h n) -> p h n", h=nh), [ptq], [G_qTt])
                              psg = nps()
                              for h in range(12):
                                  MM(psg, psg[:, h * 8:h * 8 + own], qTt[0:64, h, :], kmT[0:64, h // 3, 0:own], True, True, [G_qTt, G_kmT])
                              gt = sq[:, 768:864].rearrange("p (h b) -> p h b", h=12)
                              top8 = sq[:, 864:960].rearrange("p (h b) -> p h b", h=12)
                              selt = sq[:, 0:96].rearrange("p (h b) -> p h b", h=12)
                              MS("dve", gt, -1.0e30, [sq])
                              CP("dve", gt[:, :, 0:own], psg[:, 0:96].rearrange("p (h b) -> p h b", h=12)[:, :, 0:own], [psg], [sq])
                              for h in range(12):
                                  S.op("dve", lambda e, o=top8[:, h, :], i=gt[:, h, :]: e.max(out=o, in_=i), reads=[sq], writes=[sq])
                                  TS("dve", selt[:, h, 0:own], gt[:, h, 0:own], top8[:, h, 2:3], None, ALU.is_ge, None, [sq], [sq])
                              TS("dve", biasb[:, t, :, 0:own], selt[:, :, 0:own], -1.0, BIG, ALU.add, ALU.mult, [sq], [G_bias])
                          else:
                              MS("pool", biasb[:, t, :, 0:own], 0.0, [G_bias])
                      MS("pool", biasb[:, t, :, own:own + 1], 0.0, [G_bias])
                      if t % 2 == 1:
                          bb = t // 2
                          kmf = sq[0:64, 960:964]
                          S.op("dve", lambda e, o=kmf, i=kTa[0:64, :, bb * 256:(bb + 1) * 256]: e.reduce_sum(out=o, in_=i, axis=AX.X),
                               reads=[G_kTa], writes=[sq])
                          TS("dve", kmT[0:64, :, bb], kmf, 1.0 / 256.0, None, ALU.mult, None, [sq], [G_kmT])

              stop_if('P1')
              if mla:
                  barrier()
                  wq = WB[:, 0:3456].rearrange("p (k n) -> p k n", k=3)
                  wqs = WB[:, 3456:6912].rearrange("p (k n) -> p k n", k=3)
                  wuk = WB[:, 6912:8448].rearrange("p (k n) -> p k n", k=2)
                  wuv = WB[:, 8448:9984].rearrange("p (k n) -> p k n", k=2)
                  wload(WB, wq, kcw(I["w_q_b"]))
                  wload(WB, wqs, kcw(I["w_q_b_sw"]))
                  wload(WB, wuk, kcw(I["w_uk"]))
                  wload(WB, wuv, kcw(I["w_uv"]))
                  S.dma("sp", TAB[0:96, 0:2048], I["c96"][:, :], writes=[TAB])
                  S.dma("sp", TAB[0:96, 2048:4096], I["s96"][:, :], writes=[TAB])
                  wload(WA, WA[:, 0:8192].rearrange("p (k n) -> p k n", k=8), kcw(I["w_o"][L]))
                  MS("pool", vaug[0][:, :, 64:128], 1.0, [Gv[0]])
                  MS("pool", vaug[1][:, :, 0:64], 1.0, [Gv[1]])
                  for h in range(12):
                      kT_h = kTh[h % 2]
                      qT_h = qTh[h % 2]
                      va = vaug[h % 2]
                      gk, gq, gv = Gk[h % 2], Gq[h % 2], Gv[h % 2]
                      v0 = 0 if h % 2 == 0 else 64
                      for tb in range(4):
                          bs = slice(tb * 512, (tb + 1) * 512)
                          ps = nps()
                          for kc in range(2):
                              MM(ps, ps[0:64, :], wuk[:, kc, h * 64:(h + 1) * 64], cnT[:, kc, bs], kc == 0, kc == 1, [WB, BUFb])
                          ACT(kT_h[0:64, bs], ps[0:64, :], AF.Copy, [ps], [gk])
                          ps1 = nps()
                          for kc in range(3):
                              MM(ps1, ps1[0:96, :], wq[:, kc, h * 96:(h + 1) * 96], qnT[:, kc, bs], kc == 0, kc == 2, [WB, BUFb])
                          ps2 = nps()
                          for kc in range(3):
                              MM(ps2, ps2[0:96, :], wqs[:, kc, h * 96:(h + 1) * 96], qnT[:, kc, bs], kc == 0, kc == 2, [WB, BUFb])
                          TT("dve", tmpf[0][0:96, :], ps1[0:96, :], TAB[0:96, tb * 512:(tb + 1) * 512], ALU.mult, [ps1, TAB], [Gt[0]])
                          TT("dve", tmpf[1][0:96, :], ps2[0:96, :], TAB[0:96, 2048 + tb * 512:2048 + (tb + 1) * 512], ALU.mult, [ps2, TAB], [Gt[1]])
                          TT("pool", qT_h[0:96, bs], tmpf[0][0:96, :], tmpf[1][0:96, :], ALU.add, [Gt[0], Gt[1]], [gq])
                      CP("pool", kT_h[64:96, :], krT[64:96, :], [BUFb], [gk])
                      for half in range(2):
                          ps = nps()
                          for tt in range(8):
                              t = half * 8 + tt
                              for kc in range(2):
                                  MM(ps, ps[:, tt * 64:(tt + 1) * 64], cnT[:, kc, t * 128:(t + 1) * 128], wuv[:, kc, h * 64:(h + 1) * 64],
                                     kc == 0, kc == 1, [WB, BUFb])
                          ACT(va[:, half * 8:(half + 1) * 8, v0:v0 + 64], ps[:, :].rearrange("p (t d) -> p t d", t=8), AF.Copy, [ps], [gv])
                      attention(qT_h, gq, kT_h, gk, 0, 96, (lambda kt, va=va: va[:, kt, :]), gv, 16, True,
                                float(96 ** -0.5), h % 2 == 1, h // 2)
              else:
                  barrier()
                  wload(WA, WA[:, 0:8192].rearrange("p (k n) -> p k n", k=8), kcw(I["w_o"][L]))
                  Gv2 = [S.view(BUF, "Gv2_0"), S.view(BUF, "Gv2_1")]
                  MS("pool", vaug[0][:, :, 64:128], 1.0, [Gv2[0]])
                  MS("pool", vaug[1][:, :, 0:64], 1.0, [Gv2[1]])
                  for h in range(12):
                      kvh = h // 3
                      par = h % 2
                      va, gv = vaug[par], Gv2[par]
                      v0 = 0 if par == 0 else 64
                      CP("pool", qa[:, :, 0:64], qtok[:, :, h, :], [G_qtok], [G_qa])
                      CP("pool", qa[:, :, 64:72], biasb[:, :, h, :], [G_bias], [G_qa])
                      for half in range(2):
                          ptq = npt()
                          for tt in range(8):
                              TR(ptq, ptq[0:72, tt * 128:(tt + 1) * 128], qa[:, half * 8 + tt, :], [G_qa])
                          CP("dve", qTa[0:72, half * 1024:(half + 1) * 1024], ptq[0:72, :], [ptq], [G_qTa])
                      CP("pool", va[:, :, v0:v0 + 64], vtok[:, :, kvh * 64:(kvh + 1) * 64], [G_vtok], [gv])
                      attention(qTa, G_qTa, kTa[:, kvh, :], G_kTa, 0, 72, (lambda kt, va=va: va[:, kt, :]), gv, 16, True,
                                0.125, par == 1, h // 2)
                  BUFb_q = G_qmT
              for j in range(4):
                  p0 = (j % 2) * 64
                  attention(qmT[:, j // 2, :], (BUFb if mla else G_qmT), mkT_ap[:, j // 2, :], mkT_tile, p0, p0 + 64,
                            (lambda kt, j=j: vmem_ap[:, kt, j, :]), vmem_tile, 2, False, 0.125, j % 2 == 1, 6 + j // 2)

              stop_if('P2')
              barrier()
              wo = WA[:, 0:8192].rearrange("p (k n) -> p k n", k=8)
              for t in range(NT):
                  pa, pb = nps(), nps()
                  for hf, ps in enumerate((pa, pb)):
                      for kc in range(8):
                          MM(ps, ps[:, :], MIX[:, kc, t * 128:(t + 1) * 128], wo[:, kc, hf * 512:(hf + 1) * 512], kc == 0, kc == 7, [MIX, WA])
                  f = rot(xf, "xf")
                  S.dma("sp", f[:], xsrc[t * 128:(t + 1) * 128, :], writes=[f])
                  z = rot(zt, "zt")
                  for hf, ps in enumerate((pa, pb)):
                      STT("dve", z[:, hf * 512:(hf + 1) * 512], f[:, hf * 512:(hf + 1) * 512], ALPHA, ps[:, :], ALU.mult, ALU.add, [f, ps], [z])
                  layernorm(z, 128, 0, z, z[:, :])
                  S.dma("sp", X1[t * 128:(t + 1) * 128, :], z[:, :], reads=[z], writes=[g.X1t])
                  b = rot(xb, "xb")
                  ACT(b[:], z[:], AF.Copy, [z], [b])
                  to_xT(b, XT, t)

              stop_if('P3')
              barrier()
              load_ln("ln2")
              acc = BUF[:, 0:8192].rearrange("p (t n) -> p t n", t=8)
              MIXf = MIX[:, :, :].rearrange("p k n -> p (k n)").bitcast(F32)
              g.gfull = [MIXf[:, 0:1026], MIXf[:, 1026:2052]]
              g.ftmp = [MIXf[:, 2052:2564], MIXf[:, 2564:3076]]
              g.hT = MIXf[:, 3076:4100].bitcast(BF16).rearrange("p (j n) -> p j n", j=2)
              g.halo = MIXf[:, 4100:4144].rearrange("p (c t) -> p c t", t=2)
              g.wdn = [WA[:, 0:2048].rearrange("p (j n) -> p j n", j=2), WA[:, 2048:4096].rearrange("p (j n) -> p j n", j=2)]
              g.MIXt, g.WAt, g.WBt, g.XTt, g.BUFb = MIX, WA, WB, XT, BUFb
              MS("pool", g.halo, 0.0, [MIX])
              xdst = X2 if L == 0 else O["y_p"]
              for half in range(2):
                  for t in range(8):
                      tt = half * 8 + t
                      S.dma("sp", acc[:, t, :], X1[tt * 128:(tt + 1) * 128, :], reads=[g.X1t], writes=[BUFb])
                  ffn(g, L, half, acc, cst)
                  for t in range(8):
                      tt = half * 8 + t
                      z = rot(zt, "zt")
                      ACT(z[:], acc[:, t, :], AF.Copy, [BUFb], [z])
                      layernorm(z, 128, 2, z, z[:, :])
                      S.dma("sp", xdst[tt * 128:(tt + 1) * 128, :], z[:, :], reads=[z], writes=[g.X2t])
              S.dma("sp", O["conv_p"][L].rearrange("p (c t) -> p c t", t=2), cst[:], reads=[cst])
              xsrc = X2
        except StopBuild:
            pass
        if flags["sample"]:
            g.XT, g.MIX, g.BUF, g.WA, g.WB, g.TAB, g.flags = XT, MIX, BUF, WA, WB, TAB, flags
            g.wdn = [WA[:, 0:2048].rearrange("p (j n) -> p j n", j=2), WA[:, 2048:4096].rearrange("p (j n) -> p j n", j=2)]
            sample_path(g)
        S.emit()
        print("SBUF KB/partition:", S.sb_bytes / 1024.0, "sems:", S.nsem, {k: len(E.prog) for k, E in S.eng.items()})
    return nc


def ffn(g, L, half, acc, cst):
    S, I = g.S, g.I
    MM, ACT, TT, TS, STT, CP, MS, rot, nps = g.MM, g.ACT, g.TT, g.TS, g.STT, g.CP, g.MS, g.rot, g.nps
    XT, MIX, WA, WB, BUFb = g.XTt, g.MIXt, g.WAt, g.WBt, g.BUFb
    tok0 = half * 1024
    for t in range(8):
        S.op("act", lambda e, o=acc[:, t, :]: e.mul(out=o, in_=o, mul=ALPHA), reads=[BUFb], writes=[BUFb])
    wup = I["w_up"][L].rearrange("(kc p) n -> p kc n", p=128)
    wdn = I["w_down"][L]
    for gi in range(11):
        w = WB[:, (gi % 2) * 4096:(gi % 2) * 4096 + 4096]
        wu = w[:, 0:2048].rearrange("p (k n) -> p k n", k=8)
        wg = w[:, 2048:4096].rearrange("p (k n) -> p k n", k=8)
        wt = g.wtile[gi % 2]
        for kc in range(8):
            S.dma("pool", wu[:, kc, :], wup[:, kc, gi * 256:(gi + 1) * 256], writes=[wt])
            S.dma("pool", wg[:, kc, :], wup[:, kc, DFF + gi * 256:DFF + (gi + 1) * 256], writes=[wt])
        wd = g.wdn[gi % 2]
        wdt = g.wdtile[gi % 2]
        for j in range(2):
            S.dma("pool", wd[:, j, :], wdn[gi * 256 + j * 128:gi * 256 + (j + 1) * 128, :], writes=[wdt])
        for j in range(2):
            fc = gi * 2 + j
            gf = g.gfull[j]
            gft = g.gftile[j]
            CP("pool", gf[:, 0:2], g.halo[:, fc, :], [MIX], [gft])
            for tb in range(2):
                bs = slice(tok0 + tb * 512, tok0 + (tb + 1) * 512)
                pu, pg = nps(), nps()
                for kc in range(8):
                    MM(pu, pu[:, :], wu[:, kc, j * 128:(j + 1) * 128], XT[:, kc, bs], kc == 0, kc == 7, [wt, XT])
                for kc in range(8):
                    MM(pg, pg[:, :], wg[:, kc, j * 128:(j + 1) * 128], XT[:, kc, bs], kc == 0, kc == 7, [wt, XT])
                o = 2 + tb * 512
                ACT(gf[:, o:o + 512], pg[:, :], AF.Copy, [pg], [gft])
                tm = g.ftmp[tb]
                tmt = g.fttile[tb]
                TS("dve", tm, gf[:, o:o + 512], g.cvw[:, fc, 2:3], g.cvb[:, fc:fc + 1], ALU.mult, ALU.add, [gft, g.cvw, g.cvb], [tmt])
                STT("dve", tm, gf[:, o - 1:o + 511], g.cvw[:, fc, 1:2], tm, ALU.mult, ALU.add, [gft, g.cvw, tmt], [tmt])
                STT("dve", tm, gf[:, o - 2:o + 510], g.cvw[:, fc, 0:1], tm, ALU.mult, ALU.add, [gft, g.cvw, tmt], [tmt])
                ACT(tm, tm, AF.Gelu, [tmt], [tmt])
                TT("dve", g.hT[:, j, tb * 512:(tb + 1) * 512], tm, pu[:, :], ALU.mult, [tmt, pu], [g.hTtile])
            CP("pool", g.halo[:, fc, :], gf[:, 1024:1026], [gft], [MIX])
            if half == 1:
                CP("pool", cst[:, fc, :], gf[:, 1024:1026], [gft], [cst])
        for t in range(8):
            pa, pb = nps(), nps()
            for hf, ps in enumerate((pa, pb)):
                for j in range(2):
                    MM(ps, ps[:, :], g.hT[:, j, t * 128:(t + 1) * 128], wd[:, j, hf * 512:(hf + 1) * 512], j == 0, j == 1, [g.hTtile, wdt])
                TT("dve", acc[:, t, hf * 512:(hf + 1) * 512], acc[:, t, hf * 512:(hf + 1) * 512], ps[:, :], ALU.add, [BUFb, ps], [BUFb])


def moba_inproj_tile(*a, **k):
    raise NotImplementedError("MoBA layer not implemented in this revision")


def moba_attention(*a, **k):
    raise NotImplementedError("MoBA layer not implemented in this revision")


NPG = 128
SPOS_SCALE_A = float(96 ** -0.5)


def declare_sample_io(g, I, O, din, dout):
    for name, shape, dt in [
        ("xs", [4, D], F32), ("ptab", [1, 4 * NPG], I32),
        ("p_ckv", [NPOOL * 128, 256], F32), ("p_kr", [NPOOL * 128, 32], F32),
        ("p_k", [NPOOL * 128, 256], F32), ("p_v", [NPOOL * 128, 256], F32),
        ("cmk", [2, 4, 256, 256], F32), ("cmv", [2, 4, 256, 256], F32),
        ("sconv", [2, 128, NFC * 8], F32), ("w_ukT", [64, 12 * 256], F32),
        ("scos32", [4, 32], F32), ("ssin32", [4, 32], F32), ("scos64", [4, 64], F32), ("ssin64", [4, 64], F32),
    ]:
        I[name] = din(name, shape, dt)
    for name, shape in [("y_s", [4, D]), ("ckv_s", [4, 256]), ("krope_s", [4, 32]), ("mobak_s", [4, 256]),
                        ("mobav_s", [4, 256]), ("conv_s", [2, 128, NFC * 8])]:
        O[name] = dout(name, shape)


def sample_path(g):
    S, I, O = g.S, g.I, g.O
    MM, TR, ACT, TT, TS, STT, CP, MS, RS, RCP = g.MM, g.TR, g.ACT, g.TT, g.TS, g.STT, g.CP, g.MS, g.RS, g.RCP
    rot, nps, npo, npt, barrier, wload = g.rot, g.nps, g.npo, g.npt, g.barrier, g.wload
    XT, MIX, BUF, WA, WB, TAB = g.XT, g.MIX, g.BUF, g.WA, g.WB, g.TAB
    st, sq, lnp, gqb, cvw, cvb, ident = g.st, g.sq, g.lnp, g.gqb, g.cvw, g.cvb, g.ident
    flags = g.flags
    barrier()

    def kcw(w2d):
        return w2d.rearrange("(kc p) n -> p kc n", p=128)

    XT2 = XT[:, :, :].rearrange("p k n -> p (k n)")
    MX2 = MIX[:, :, :].rearrange("p k n -> p (k n)")
    G = lambda base, nm: S.view(base, nm)
    xsT = XT2[:, 0:32].rearrange("p (k n) -> p k n", k=8); G_xsT = G(XT, "s_xsT")
    mixT = XT2[:, 32:64].rearrange("p (k n) -> p k n", k=8); G_mix = G(XT, "s_mix")
    x1T = XT2[:, 64:96].rearrange("p (k n) -> p k n", k=8); G_x1T = G(XT, "s_x1T")
    qnT = XT2[:, 96:108].rearrange("p (k n) -> p k n", k=3); G_qnT = G(XT, "s_qnT")
    qmTs = XT2[:, 108:116].rearrange("p (k n) -> p k n", k=2); G_qmT = G(XT, "s_qmT")
    qlT = XT2[:, 116:212].rearrange("p (c h n) -> p c h n", c=2, h=12); G_qlT = G(XT, "s_qlT")
    qrT = XT2[:, 212:260].rearrange("p (h n) -> p h n", h=12); G_qrT = G(XT, "s_qrT")
    qhT = XT2[:, 260:308].rearrange("p (h n) -> p h n", h=12); G_qhT = G(XT, "s_qhT")
    olT = XT2[:, 308:404].rearrange("p (c h n) -> p c h n", c=2, h=12); G_olT = G(XT, "s_olT")
    qTs = XT2[:, 404:428].rearrange("p (c g n) -> p c g n", c=2, g=3); G_qTs = G(XT, "s_qTs")
    sbs = XT2[:, 512:2560]; G_sbs = G(XT, "s_sbs")
    hTs = XT2[:, 2560:2568].rearrange("p (j n) -> p j n", j=2); G_hTs = G(XT, "s_hTs")
    pTs = [XT2[:, 2576 + i * 16:2576 + i * 16 + 12] for i in range(4)]; G_pTs = [G(XT, "s_pT%d" % i) for i in range(4)]
    cT = [XT2[:, 4096 + i * 512:4096 + i * 512 + 384] for i in range(2)]; G_cT = [G(XT, "s_cT%d" % i) for i in range(2)]
    PTall = XT2[:, 6144:6144 + 1548].rearrange("p (g h) -> p g h", h=12); G_PT = G(XT, "s_PT")
    Yb = XT2[:, 8192:8960]; G_Yb = G(XT, "s_Yb")
    ones1 = XT2[:, 8960:9088]; G_on = G(XT, "s_ones")
    NB = 4
    cpg = [MX2[:, i * 264:i * 264 + 257] for i in range(NB)]; G_cpg = [G(MIX, "s_cpg%d" % i) for i in range(NB)]
    krp = [MX2[:, 1056 + i * 32:1056 + (i + 1) * 32] for i in range(NB)]; G_krp = [G(MIX, "s_krp%d" % i) for i in range(NB)]
    vpg = [MX2[:, 2048 + i * 264:2048 + i * 264 + 260].rearrange("p (k n) -> p k n", k=4) for i in range(NB)]
    G_vpg = [G(MIX, "s_vpg%d" % i) for i in range(NB)]
    lpg = MX2[:, 4096:4096 + 264]; G_lpg = G(MIX, "s_lpg")
    lkr = MX2[:, 4360:4392]
    lvp = MX2[:, 4400:4660].rearrange("p (k n) -> p k n", k=4)
    wukT = MX2[:, 8192:8192 + 3072].rearrange("p (h r) -> p h r", h=12); G_wukT = G(MIX, "s_wukT")
    xcur = BUF[:, 0:1024]; G_xcur = G(BUF, "s_xcur")
    zs = BUF[:, 1024:2048]; G_zs = G(BUF, "s_zs")
    accs = BUF[:, 2048:3072]; G_acc = G(BUF, "s_acc")
    sgs = BUF[:, 3072:4608]; G_sgs = G(BUF, "s_sgs")
    Sall = BUF[:, 4608:4608 + 1548].rearrange("p (g h) -> p g h", h=12); G_Sall = G(BUF, "s_Sall")
    biasB = BUF[:, 6400:7168]; G_bB = G(BUF, "s_biasB")
    gts = BUF[:, 7168:7232]; top8s = BUF[:, 7232:7240]; bsel = BUF[:, 7240:7304]; G_gt = G(BUF, "s_gt")
    olat = BUF[:, 7424:7424 + 264]; G_olat = G(BUF, "s_olat")
    gst = BUF[:, 7700:7700 + 3 * 88].rearrange("p (a n) -> p a n", a=3); G_gst = G(BUF, "s_gst")
    sct = BUF[:, 8192:8192 + NFC * 8].rearrange("p (c s t) -> p c s t", c=NFC, s=4); G_sct = G(BUF, "s_sct")
    cso = BUF[:, 8400:8400 + NFC * 8].rearrange("p (c s t) -> p c s t", c=NFC, s=4); G_cso = G(BUF, "s_cso")
    rmask = BUF[:, 8600:8601]; G_rm = G(BUF, "s_rmask")
    io = TAB[:, 0:1].bitcast(I32); ptb = TAB[:, 16:16 + 512].bitcast(I32); idx = TAB[:, 544:544 + 512].bitcast(I32)
    G_idx = G(TAB, "s_idx")
    stab = TAB[:, 1100:1100 + 192]; G_stab = G(TAB, "s_stab")
    identf = TAB[0:64, 2048:2112]; G_idf = G(TAB, "s_identf")
    S.dma("sp", identf, I["ident"][0:64, 0:64], writes=[G_idf])

    S.op("pool", lambda e: e.iota(io, pattern=[[1, 1]], base=0, channel_multiplier=1), writes=[G_idx])
    S.dma("sp", ptb, I["ptab"].partition_broadcast(128)[:, 0, :], writes=[G_idx])
    TS("dve", idx, ptb, 128, io[:, 0:1], ALU.mult, ALU.add, [G_idx], [G_idx])
    S.dma("sp", stab[0:4, 0:32], I["scos32"][:, :], writes=[G_stab])
    S.dma("sp", stab[0:4, 32:64], I["ssin32"][:, :], writes=[G_stab])
    S.dma("sp", stab[0:4, 64:128], I["scos64"][:, :], writes=[G_stab])
    S.dma("sp", stab[0:4, 128:192], I["ssin64"][:, :], writes=[G_stab])
    MS("pool", ones1, 1.0, [G_on])
    maskk = BUF[:, 8610:8614]; G_mk = G(BUF, "s_maskk")
    for kvh in range(4):
        S.op("dve", lambda e, o=maskk[0:12, kvh:kvh + 1], i=ident[0:12, kvh * 3:(kvh + 1) * 3]: e.reduce_sum(out=o, in_=i, axis=AX.X),
             reads=[ident], writes=[G_mk])
    MS("pool", rmask, 0.0, [G_rm])
    MS("pool", rmask[0:1, :], 1.0, [G_rm])
    for i in range(NB):
        MS("pool", cpg[i][:, 256:257], 1.0, [G_cpg[i]])
        MS("pool", vpg[i][:, :, 64:65], 1.0, [G_vpg[i]])
    MS("pool", lpg, 0.0, [G_lpg])
    MS("pool", lpg[:, 256:257], 1.0, [G_lpg])
    MS("pool", lkr, 0.0, [G_lpg])
    MS("pool", lvp, 0.0, [G_lpg])
    MS("pool", lvp[:, :, 64:65], 1.0, [G_lpg])
    S.dma("sp", xcur[0:4, :], I["xs"][:, :], writes=[G_xcur])
    S.dma("pool", wukT[0:64, :, :], I["w_ukT"].rearrange("p (h r) -> p h r", h=12), writes=[G_wukT])

    def tok_T(src_tile, src_ap, ncols_blocks, dst_ap, dst_tile, rows=128):
        pt_t = npt()
        nb = len(ncols_blocks)
        for i, (c0, w) in enumerate(ncols_blocks):
            TR(pt_t, pt_t[0:w, i * 4:(i + 1) * 4], src_ap[0:4, c0:c0 + w], [src_tile], npart=4)
        CP("dve", dst_ap, pt_t[0:rows, 0:nb * 4].rearrange("p (k n) -> p k n", k=nb), [pt_t], [dst_tile])

    def x_to_T(xf32, x_tile, dstT, dst_tile):
        ACT(sbs[0:4, 0:1024], xf32[0:4, :], AF.Copy, [x_tile], [G_sbs])
        tok_T(G_sbs, sbs, [(k * 128, 128) for k in range(8)], dstT, dst_tile)

    def proj_s(win, wtile, c0, c1):
        ps = nps()
        for kc in range(8):
            MM(ps, ps[0:4, 0:c1 - c0], xsT[:, kc, :], win[:, kc, c0:c1], kc == 0, kc == 7, [G_xsT, wtile])
        return ps

    def rope_s(src, nh, dh, ctab, stab_, dst, src_tile, dst_tile):
        hd = dh // 2
        a = sq[0:4, 0:nh * dh].rearrange("p (h d) -> p h d", h=nh)
        b2 = sq[0:4, 768:768 + nh * dh].rearrange("p (h d) -> p h d", h=nh) if nh * dh <= 256 else sq[0:4, 512:512 + nh * dh].rearrange("p (h d) -> p h d", h=nh)
        cB = ctab.to_broadcast([4, nh, dh]) if nh > 1 else ctab
        sB = stab_.to_broadcast([4, nh, dh]) if nh > 1 else stab_
        TT("dve", a, src, cB, ALU.mult, [src_tile, G_stab], [sq])
        TT("dve", b2[:, :, 0:hd], src[:, :, hd:dh], sB[:, :, 0:hd], ALU.mult, [src_tile, G_stab], [sq])
        TT("dve", b2[:, :, hd:dh], src[:, :, 0:hd], sB[:, :, hd:dh], ALU.mult, [src_tile, G_stab], [sq])
        TT("dve", dst, a, b2, ALU.add, [sq], [dst_tile])

    def gather(pool_ap, dst_ap, dst_tile, s, p):
        col = s * NPG + p
        S.dma("pool", None, None, reads=[G_idx], writes=[dst_tile],
              fn=lambda e, o=dst_ap, c=col, src=pool_ap: e.indirect_dma_start(
                  out=o, out_offset=None, in_=src[:, :], in_offset=bass.IndirectOffsetOnAxis(ap=idx[:, c:c + 1], axis=0)))

    def mem_attn_s(L):
        mkT_ap, vmem_ap = g.mkT_sb[:, :, :], g.vmem_sb[:, :, :, :]
        for s in range(4):
            MS("pool", vmem_ap, 1.0, [g.vmem_sb])
            for mt in range(2):
                f = rot(g.xf, "xf")
                S.dma("sp", f[:, 0:256], I["cmk"][L, s, mt * 128:(mt + 1) * 128, :], writes=[f])
                S.dma("sp", f[:, 256:512], I["cmv"][L, s, mt * 128:(mt + 1) * 128, :], writes=[f])
                b = rot(g.xb, "xb")
                ACT(b[:, 0:256], f[:, 0:256], AF.Copy, [f], [b])
                pt_t = npt()
                for pr in range(2):
                    TR(pt_t, pt_t[:, pr * 128:(pr + 1) * 128], b[:, pr * 128:(pr + 1) * 128], [b])
                CP("dve", mkT_ap[:, :, mt * 128:(mt + 1) * 128], pt_t[:, 0:256].rearrange("p (k n) -> p k n", k=2), [pt_t], [g.mkT_sb])
                for j in range(4):
                    c0 = 0 if j % 2 == 0 else 64
                    ACT(vmem_ap[:, mt, j, c0:c0 + 64], f[:, 256 + j * 64:256 + (j + 1) * 64], AF.Copy, [f], [g.vmem_sb])
            for j in range(4):
                p0 = (j % 2) * 64
                num0 = 64 if j % 2 == 1 else 0
                den0 = 64 - num0
                po = npo()
                for mt in range(2):
                    ps = nps()
                    MM(ps, ps[:, 0:1], mkT_ap[p0:p0 + 64, j // 2, mt * 128:(mt + 1) * 128], qmTs[p0:p0 + 64, j // 2, s:s + 1], True, True,
                       [g.mkT_sb, G_qmT])
                    p = rot(g.pT, "pT")
                    ACT(p[:, 0:1], ps[:, 0:1], AF.Exp, [ps], [p], scale=0.125)
                    MM(po, po[:, 0:1], vmem_ap[:, mt, j, :], p[:, 0:1], mt == 0, mt == 1, [g.vmem_sb, p])
                r = rot(g.rc, "rc")
                RCP(r[den0:den0 + 64, 0:1], po[den0:den0 + 64, 0:1], [po], [r])
                TT("dve", mixT[num0:num0 + 64, 6 + j // 2, s:s + 1], po[num0:num0 + 64, 0:1], r[den0:den0 + 64, 0:1], ALU.mult, [po, r], [G_mix])

    def post_attention(L, wo, wotile):
        for hf in range(2):
            ps = nps()
            for kc in range(8):
                MM(ps, ps[0:4, :], mixT[:, kc, :], wo[:, kc, hf * 512:(hf + 1) * 512], kc == 0, kc == 7, [G_mix, wotile])
            STT("dve", zs[0:4, hf * 512:(hf + 1) * 512], xcur[0:4, hf * 512:(hf + 1) * 512], ALPHA, ps[0:4, :], ALU.mult, ALU.add, [G_xcur, ps], [G_zs])
        for i, nm in enumerate(["ln1_g", "ln1_b"]):
            S.dma("sp", lnp[:, i, :], I[nm][L:L + 1, :].partition_broadcast(128)[:, 0, :], writes=[lnp])
        g.layernorm(G_zs_t, 4, 0, G_zs_t, zs[0:4, :])
        x_to_T(zs, G_zs, x1T, G_x1T)
        S.op("act", lambda e: e.mul(out=accs[0:4, :], in_=zs[0:4, :], mul=ALPHA), reads=[G_zs], writes=[G_acc])
        S.dma("sp", sct, I["sconv"][L].rearrange("p (c s t) -> p c s t", c=NFC, s=4), writes=[G_sct])
        S.dma("sp", cvw[:], I["convw"][L].rearrange("p (c t) -> p c t", t=3), writes=[cvw])
        S.dma("sp", cvb[:], I["convb"][L], writes=[cvb])
        wup = I["w_up"][L].rearrange("(kc p) n -> p kc n", p=128)
        wdn = I["w_down"][L]
        for gi in range(11):
            w = WB[:, (gi % 2) * 4096:(gi % 2) * 4096 + 4096]
            wu = w[:, 0:2048].rearrange("p (k n) -> p k n", k=8)
            wg = w[:, 2048:4096].rearrange("p (k n) -> p k n", k=8)
            wt = g.wtile[gi % 2]
            for kc in range(8):
                S.dma("pool", wu[:, kc, :], wup[:, kc, gi * 256:(gi + 1) * 256], writes=[wt])
                S.dma("pool", wg[:, kc, :], wup[:, kc, DFF + gi * 256:DFF + (gi + 1) * 256], writes=[wt])
            wd = g.wdn[gi % 2]
            wdt = g.wdtile[gi % 2]
            for j in range(2):
                S.dma("pool", wd[:, j, :], wdn[gi * 256 + j * 128:gi * 256 + (j + 1) * 128, :], writes=[wdt])
            for j in range(2):
                fc = gi * 2 + j
                pu, pg = nps(), nps()
                for kc in range(8):
                    MM(pu, pu[:, 0:4], wu[:, kc, j * 128:(j + 1) * 128], x1T[:, kc, :], kc == 0, kc == 7, [wt, G_x1T])
                for kc in range(8):
                    MM(pg, pg[:, 0:4], wg[:, kc, j * 128:(j + 1) * 128], x1T[:, kc, :], kc == 0, kc == 7, [wt, G_x1T])
                gn = gst[:, 0, 0:4]
                tm = gst[:, 1, 0:4]
                ACT(gn, pg[:, 0:4], AF.Copy, [pg], [G_gst])
                TS("dve", tm, gn, cvw[:, fc, 2:3], cvb[:, fc:fc + 1], ALU.mult, ALU.add, [G_gst, cvw, cvb], [G_gst])
                STT("dve", tm, sct[:, fc, :, 1], cvw[:, fc, 1:2], tm, ALU.mult, ALU.add, [G_sct, cvw, G_gst], [G_gst])
                STT("dve", tm, sct[:, fc, :, 0], cvw[:, fc, 0:1], tm, ALU.mult, ALU.add, [G_sct, cvw, G_gst], [G_gst])
                ACT(tm, tm, AF.Gelu, [G_gst], [G_gst])
                TT("dve", hTs[:, j, :], tm, pu[:, 0:4], ALU.mult, [G_gst, pu], [G_hTs])
                CP("pool", cso[:, fc, :, 0], sct[:, fc, :, 1], [G_sct], [G_cso])
                CP("pool", cso[:, fc, :, 1], gn, [G_gst], [G_cso])
            for hf in range(2):
                ps = nps()
                for j in range(2):
                    MM(ps, ps[0:4, :], hTs[:, j, :], wd[:, j, hf * 512:(hf + 1) * 512], j == 0, j == 1, [G_hTs, wdt])
                TT("dve", accs[0:4, hf * 512:(hf + 1) * 512], accs[0:4, hf * 512:(hf + 1) * 512], ps[0:4, :], ALU.add, [G_acc, ps], [G_acc])
        S.dma("sp", O["conv_s"][L].rearrange("p (c s t) -> p c s t", c=NFC, s=4), cso, reads=[G_cso])
        for i, nm in enumerate(["ln2_g", "ln2_b"]):
            S.dma("sp", lnp[:, i, :], I[nm][L:L + 1, :].partition_broadcast(128)[:, 0, :], writes=[lnp])
        g.layernorm(G_acc_t, 4, 2, G_xcur, xcur[0:4, :])

    G_zs_t = _TileAP(G_zs, zs)
    G_acc_t = _TileAP(G_acc, accs)

    L = 0
    win = WA[:, 0:8 * 928].rearrange("p (k n) -> p k n", k=8)
    wload(WA, win, kcw(I["w_in_a"]))
    wq = WB[:, 0:3456].rearrange("p (k n) -> p k n", k=3)
    wuv = WB[:, 8448:9984].rearrange("p (k n) -> p k n", k=2)
    wload(WB, wq, kcw(I["w_q_b"]))
    wload(WB, wuv, kcw(I["w_uv"]))
    S.dma("sp", gqb[:, 0:384], I["g_q"].partition_broadcast(128)[:, 0, :], writes=[gqb])
    S.dma("sp", gqb[:, 384:640], I["g_kv"].partition_broadcast(128)[:, 0, :], writes=[gqb])
    x_to_T(xcur, G_xcur, xsT, G_xsT)
    psA = proj_s(win, WA, 0, 384)
    r = g.rstd_of(psA[0:4, 0:384], 384, 4, RMS_EPS, [psA])
    STT("dve", sbs[0:4, 0:384], psA[0:4, 0:384], r, gqb[0:4, 0:384], ALU.mult, ALU.mult, [psA, st, gqb], [G_sbs])
    tok_T(G_sbs, sbs, [(k * 128, 128) for k in range(3)], qnT, G_qnT)
    psB = proj_s(win, WA, 384, 672)
    r = g.rstd_of(psB[0:4, 0:256], 256, 4, RMS_EPS, [psB])
    STT("dve", sgs[0:4, 0:256], psB[0:4, 0:256], r, gqb[0:4, 384:640], ALU.mult, ALU.mult, [psB, st, gqb], [G_sgs])
    rope_s(psB[0:4, 256:288].rearrange("p (h d) -> p h d", h=1), 1, 32, stab[0:4, 0:32].rearrange("p (h d) -> p h d", h=1),
           stab[0:4, 32:64].rearrange("p (h d) -> p h d", h=1), sgs[0:4, 256:288].rearrange("p (h d) -> p h d", h=1), psB, G_sgs)
    S.dma("sp", O["ckv_s"][:, :], sgs[0:4, 0:256], reads=[G_sgs])
    S.dma("sp", O["krope_s"][:, :], sgs[0:4, 256:288], reads=[G_sgs])
    ACT(sbs[0:4, 1152:1440], sgs[0:4, 0:288], AF.Copy, [G_sgs], [G_sbs])
    psC = proj_s(win, WA, 672, 928)
    ACT(sbs[0:4, 1536:1792], psC[0:4, 0:256], AF.Copy, [psC], [G_sbs])
    tok_T(G_sbs, sbs, [(1536, 128), (1664, 128)], qmTs, G_qmT)
    for grp in range(3):
        ps = nps()
        for kc in range(3):
            MM(ps, ps[0:4, 0:384], qnT[:, kc, :], wq[:, kc, grp * 384:(grp + 1) * 384], kc == 0, kc == 2, [G_qnT, WB])
        q3 = ps[0:4, 0:384].rearrange("p (h d) -> p h d", h=4)
        dst = sgs[0:4, 384 + grp * 384:384 + (grp + 1) * 384].rearrange("p (h d) -> p h d", h=4)
        CP("dve", dst[:, :, 0:64], q3[:, :, 0:64], [ps], [G_sgs])
        rope_s(q3[:, :, 64:96], 4, 32, stab[0:4, 0:32].rearrange("p (h d) -> p h d", h=1), stab[0:4, 32:64].rearrange("p (h d) -> p h d", h=1),
               dst[:, :, 64:96], ps, G_sgs)
    ACT(sbs[0:4, 0:1152], sgs[0:4, 384:1536], AF.Copy, [G_sgs], [G_sbs])
    q12 = sbs[0:4, 0:1152].rearrange("p (h d) -> p h d", h=12)
    pt_t = npt()
    for h in range(12):
        TR(pt_t, pt_t[0:64, h * 4:(h + 1) * 4], q12[:, h, 0:64], [G_sbs], npart=4)
        TR(pt_t, pt_t[0:32, 64 + h * 4:64 + (h + 1) * 4], q12[:, h, 64:96], [G_sbs], npart=4)
    CP("dve", qhT[0:64, :, :], pt_t[0:64, 0:48].rearrange("p (h n) -> p h n", h=12), [pt_t], [G_qhT])
    CP("dve", qrT[0:32, :, :], pt_t[0:32, 64:112].rearrange("p (h n) -> p h n", h=12), [pt_t], [G_qrT])
    for c in range(2):
        ps = nps()
        for h in range(12):
            MM(ps, ps[:, h * 4:(h + 1) * 4], wukT[0:64, h, c * 128:(c + 1) * 128], qhT[0:64, h, :], True, True, [G_wukT, G_qhT])
        ACT(qlT[:, c, :, :], ps[:, 0:48].rearrange("p (h n) -> p h n", h=12), AF.Copy, [ps], [G_qlT])
    nbuf = [0]
    for s in range(4):
        po = npo()
        S.dma("sp", lpg[0:1, 0:256], sbs[s:s + 1, 1152:1408], reads=[G_sbs], writes=[G_lpg])
        S.dma("sp", lkr[0:1, 0:32], sbs[s:s + 1, 1408:1440], reads=[G_sbs], writes=[G_lpg])
        for p in range(NPG + 1):
            local = (p == NPG)
            bi = nbuf[0] % NB
            nbuf[0] += 1
            if local:
                cp_ap, cp_t, kr_ap, kr_t = lpg, G_lpg, lkr, G_lpg
            else:
                cp_ap, cp_t, kr_ap, kr_t = cpg[bi], G_cpg[bi], krp[bi], G_krp[bi]
                gather(I["p_ckv"], cp_ap[:, 0:256], cp_t, s, p)
                gather(I["p_kr"], kr_ap, kr_t, s, p)
            pt_t = npt()
            TR(pt_t, pt_t[:, 0:128], cp_ap[:, 0:128], [cp_t])
            TR(pt_t, pt_t[:, 128:256], cp_ap[:, 128:256], [cp_t])
            TR(pt_t, pt_t[0:32, 256:384], kr_ap, [kr_t])
            ci = p % 2
            CP("dve", cT[ci][:, 0:256], pt_t[:, 0:256], [pt_t], [G_cT[ci]])
            ACT(cT[ci][0:32, 256:384], pt_t[0:32, 256:384], AF.Copy, [pt_t], [G_cT[ci]])
            ps = nps()
            MM(ps, ps[:, 0:12], cT[ci][:, 0:128], qlT[:, 0, :, s], True, False, [G_cT[ci], G_qlT])
            MM(ps, ps[:, 0:12], cT[ci][:, 128:256], qlT[:, 1, :, s], False, False, [G_cT[ci], G_qlT])
            MM(ps, ps[:, 0:12], cT[ci][0:32, 256:384], qrT[0:32, :, s], False, True, [G_cT[ci], G_qrT])
            pi = p % 4
            ACT(pTs[pi], ps[:, 0:12], AF.Exp, [ps], [G_pTs[pi]], scale=SPOS_SCALE_A)
            if local:
                TS("dve", pTs[pi], pTs[pi], rmask[:, 0:1], None, ALU.mult, None, [G_pTs[pi], G_rm], [G_pTs[pi]])
            MM(po, po[0:12, 0:257], pTs[pi], cp_ap[:, 0:257], p == 0, local, [G_pTs[pi], cp_t])
        RCP(olat[0:12, 256:257], po[0:12, 256:257], [po], [G_olat])
        TS("dve", olat[0:12, 0:256], po[0:12, 0:256], olat[0:12, 256:257], None, ALU.mult, None, [po, G_olat], [G_olat])
        ACT(Yb[0:12, 0:256], olat[0:12, 0:256], AF.Copy, [G_olat], [G_Yb])
        pt_t = npt()
        for c in range(2):
            TR(pt_t, pt_t[:, c * 12:(c + 1) * 12], Yb[0:12, c * 128:(c + 1) * 128], [G_Yb], npart=12)
        CP("dve", olT[:, :, :, s], pt_t[:, 0:24].rearrange("p (c h) -> p c h", c=2), [pt_t], [G_olT])
    for pr in range(6):
        for hh in range(2):
            h = pr * 2 + hh
            ps = nps()
            for c in range(2):
                MM(ps, ps[:, 0:4], wuv[:, c, pr * 128:(pr + 1) * 128], olT[:, c, h, :], c == 0, c == 1, [WB, G_olT])
            r0 = hh * 64
            ACT(mixT[r0:r0 + 64, pr, :], ps[r0:r0 + 64, 0:4], AF.Copy, [ps], [G_mix])
    mem_attn_s(0)
    wo = WA[:, 0:8192].rearrange("p (k n) -> p k n", k=8)
    wload(WA, wo, kcw(I["w_o"][0]))
    post_attention(0, wo, WA)
    if flags["layers"] < 2:
        return
    L = 1
    win = WA[:, 0:8 * 1536].rearrange("p (k n) -> p k n", k=8)
    wload(WA, win, kcw(I["w_in_b"]))
    x_to_T(xcur, G_xcur, xsT, G_xsT)
    cos64 = stab[0:4, 64:128].rearrange("p (h d) -> p h d", h=1)
    sin64 = stab[0:4, 128:192].rearrange("p (h d) -> p h d", h=1)
    for qh in range(2):
        ps = proj_s(win, WA, qh * 384, (qh + 1) * 384)
        rope_s(ps[0:4, 0:384].rearrange("p (h d) -> p h d", h=6), 6, 64, cos64, sin64,
               sgs[0:4, qh * 384:(qh + 1) * 384].rearrange("p (h d) -> p h d", h=6), ps, G_sgs)
    pskv = proj_s(win, WA, 768, 1280)
    rope_s(pskv[0:4, 0:256].rearrange("p (h d) -> p h d", h=4), 4, 64, cos64, sin64,
           sgs[0:4, 768:1024].rearrange("p (h d) -> p h d", h=4), pskv, G_sgs)
    CP("dve", sgs[0:4, 1024:1280], pskv[0:4, 256:512], [pskv], [G_sgs])
    S.dma("sp", O["mobak_s"][:, :], sgs[0:4, 768:1024], reads=[G_sgs])
    S.dma("sp", O["mobav_s"][:, :], sgs[0:4, 1024:1280], reads=[G_sgs])
    ACT(sbs[0:4, 0:1280], sgs[0:4, 0:1280], AF.Copy, [G_sgs], [G_sbs])
    psm = proj_s(win, WA, 1280, 1536)
    ACT(sbs[0:4, 1280:1536], psm[0:4, 0:256], AF.Copy, [psm], [G_sbs])
    tok_T(G_sbs, sbs, [(1280, 128), (1408, 128)], qmTs, G_qmT)
    q12 = sbs[0:4, 0:768].rearrange("p (h d) -> p h d", h=12)
    pt_t = npt()
    for h in range(12):
        TR(pt_t, pt_t[0:64, h * 4:(h + 1) * 4], q12[:, h, :], [G_sbs], npart=4)
    CP("dve", qhT[0:64, :, :], pt_t[0:64, 0:48].rearrange("p (h n) -> p h n", h=12), [pt_t], [G_qhT])
    qg = sgs[0:4, 0:768].rearrange("p (c g k d) -> p c g k d", c=2, g=3, k=2)
    for c in range(2):
        for gg in range(3):
            for kk in range(2):
                h = (2 * c + kk) * 3 + gg
                CP("dve", qg[:, c, gg, kk, :], q12[:, h, :], [G_sbs], [G_sgs])
    qgb = g.sgb[0]
    ACT(qgb[0:4, 0:768], sgs[0:4, 0:768], AF.Copy, [G_sgs], [qgb])
    tok_T(qgb, qgb, [(i * 128, 128) for i in range(6)], qTs[:, :, :, :].rearrange("p c g n -> p (c g) n"), G_qTs)
    for s in range(4):
        S.dma("sp", lpg[0:1, 0:256], sbs[s:s + 1, 768:1024], reads=[G_sbs], writes=[G_lpg])
        S.dma("sp", lvp[0:1, :, 0:64], sbs[s:s + 1, 1024:1280].rearrange("p (k d) -> p k d", k=4), reads=[G_sbs], writes=[G_lpg])
        pkm = npo()
        for p in range(NPG + 1):
            local = (p == NPG)
            bi = nbuf[0] % NB
            nbuf[0] += 1
            if local:
                kp_ap, kp_t = lpg, G_lpg
            else:
                kp_ap, kp_t = cpg[bi], G_cpg[bi]
                gather(I["p_k"], kp_ap[:, 0:256], kp_t, s, p)
                for kvh in range(4):
                    MM(pkm, pkm[0:64, kvh * 64 + p // 2:kvh * 64 + p // 2 + 1], kp_ap[:, kvh * 64:(kvh + 1) * 64], ones1[:, 0:1],
                       p == 0 and kvh == 0, p == NPG - 1 and kvh == 3, [kp_t, G_on])
            pt_t = npt()
            TR(pt_t, pt_t[:, 0:128], kp_ap[:, 0:128], [kp_t])
            TR(pt_t, pt_t[:, 128:256], kp_ap[:, 128:256], [kp_t])
            ci = p % 2
            CP("dve", cT[ci][:, 0:256], pt_t[:, 0:256], [pt_t], [G_cT[ci]])
            ps = nps()
            for kvh in range(4):
                r0 = (kvh % 2) * 64
                MM(ps, ps[:, kvh * 3:(kvh + 1) * 3], cT[ci][r0:r0 + 64, (kvh // 2) * 128:(kvh // 2) * 128 + 128],
                   qTs[r0:r0 + 64, kvh // 2, :, s], True, True, [G_cT[ci], G_qTs])
            CP("dve", Sall[:, p, :], ps[:, 0:12], [ps], [G_Sall])
        kms = cT[0]
        ACT(kms[0:64, 0:256], pkm[0:64, 0:256], AF.Copy, [pkm], [G_cT[0]])
        psg = nps()
        for kvh in range(4):
            MM(psg, psg[0:12, kvh * 64:(kvh + 1) * 64], qhT[0:64, :, s], kms[0:64, kvh * 64:(kvh + 1) * 64], True, True,
               [G_qhT, G_cT[0]])
        TS("dve", gts[0:12, :], psg[0:12, 0:64], maskk[0:12, 0:1], None, ALU.mult, None, [psg, G_mk], [G_gt])
        for kvh in range(1, 4):
            STT("dve", gts[0:12, :], psg[0:12, kvh * 64:(kvh + 1) * 64], maskk[0:12, kvh:kvh + 1], gts[0:12, :], ALU.mult, ALU.add,
                [psg, G_mk, G_gt], [G_gt])
        S.op("dve", lambda e: e.max(out=top8s[0:12, :], in_=gts[0:12, :]), reads=[G_gt], writes=[G_gt])
        TS("dve", bsel[0:12, :], gts[0:12, :], top8s[0:12, 2:3], None, ALU.is_ge, None, [G_gt], [G_gt])
        TS("dve", bsel[0:12, :], bsel[0:12, :], -1.0, BIG, ALU.add, ALU.mult, [G_gt], [G_gt])
        TT("dve", Yb[0:12, 0:768].rearrange("p (b h) -> p b h", h=12), bsel[0:12, :].rearrange("p (b o) -> p b o", o=1).to_broadcast([12, 64, 12]),
           ident[0:12, 0:12].rearrange("p (o h) -> p o h", o=1).to_broadcast([12, 64, 12]), ALU.mult, [G_gt, ident], [G_Yb])
        for hf in range(2):
            psb = nps()
            MM(psb, psb[:, 0:384], ones1[0:12, 0:128], Yb[0:12, hf * 384:(hf + 1) * 384], True, True, [G_on, G_Yb])
            CP("dve", biasB[:, hf * 384:(hf + 1) * 384], psb[:, 0:384], [psb], [G_bB])
        S4 = BUF[:, 4608:4608 + 1536].rearrange("p (b t h) -> p b t h", b=64, t=2)
        TT("dve", S4, S4, biasB[:, 0:768].rearrange("p (b o h) -> p b o h", b=64, o=1).to_broadcast([128, 64, 2, 12]), ALU.add,
           [G_Sall, G_bB], [G_Sall])
        ACT(PTall, Sall, AF.Exp, [G_Sall], [G_PT], scale=0.125)
        TS("dve", PTall[:, NPG, :], PTall[:, NPG, :], rmask[:, 0:1], None, ALU.mult, None, [G_PT, G_rm], [G_PT])
        po = npo()
        for p in range(NPG + 1):
            local = (p == NPG)
            bi = nbuf[0] % NB
            nbuf[0] += 1
            if local:
                vp_ap, vp_t = lvp, G_lpg
            else:
                vp_ap, vp_t = vpg[bi], G_vpg[bi]
                gather(I["p_v"], cpg[bi][:, 0:256], G_cpg[bi], s, p)
                CP("pool", vp_ap[:, :, 0:64], cpg[bi][:, 0:256].rearrange("p (k d) -> p k d", k=4), [G_cpg[bi]], [vp_t])
            for kvh in range(4):
                MM(po, po[0:3, kvh * 65:(kvh + 1) * 65], PTall[:, p, kvh * 3:(kvh + 1) * 3], vp_ap[:, kvh, :], p == 0 and kvh == 0, local and kvh == 3, [G_PT, vp_t])
        o3 = olat[0:3, 0:260].rearrange("p (k n) -> p k n", k=4)
        RCP(o3[:, :, 64:65], po[0:3, 0:260].rearrange("p (k n) -> p k n", k=4)[:, :, 64:65], [po], [G_olat])
        TT("dve", o3[:, :, 0:64], po[0:3, 0:260].rearrange("p (k n) -> p k n", k=4)[:, :, 0:64], o3[:, :, 64:65].to_broadcast([3, 4, 64]), ALU.mult,
           [po, G_olat], [G_olat])
        ACT(Yb[0:3, 0:256].rearrange("p (k n) -> p k n", k=4), o3[:, :, 0:64], AF.Copy, [G_olat], [G_Yb])
        pt_t = npt()
        for kvh in range(4):
            TR(pt_t, pt_t[0:64, kvh * 4:kvh * 4 + 3], Yb[0:3, kvh * 64:(kvh + 1) * 64], [G_Yb], npart=3)
        for h in range(12):
            r0 = (h % 2) * 64
            cc = (h // 3) * 4 + (h % 3)
            ACT(mixT[r0:r0 + 64, h // 2, s:s + 1], pt_t[0:64, cc:cc + 1], AF.Copy, [pt_t], [G_mix])
    mem_attn_s(1)
    wo = WA[:, 0:8192].rearrange("p (k n) -> p k n", k=8)
    wload(WA, wo, kcw(I["w_o"][1]))
    post_attention(1, wo, WA)
    S.dma("sp", O["y_s"][:, :], xcur[0:4, :], reads=[G_xcur])


class _TileAP:
    def __init__(self, tile, ap):
        self.tile, self.ap = tile, ap

    def __getitem__(self, k):
        return self.ap[k]

    lw = property(lambda self: self.tile.lw, lambda self, v: setattr(self.tile, "lw", v))
    readers = property(lambda self: self.tile.readers, lambda self, v: setattr(self.tile, "readers", v))
    excl = property(lambda self: self.tile.excl)
    dsem = property(lambda self: self.tile.dsem, lambda self, v: setattr(self.tile, "dsem", v))
    dcount = property(lambda self: self.tile.dcount, lambda self, v: setattr(self.tile, "dcount", v))
    name = property(lambda self: self.tile.name)


def _consts():
    c = {}
    c["ident"] = np.eye(128, dtype=np.float32)
    k = np.arange(128)[:, None, None]
    j = np.arange(4)[None, :, None]
    q = np.arange(512)[None, None, :]
    c["masks"] = (j * 128 + k <= q).astype(np.float32).reshape(128, 2048)
    pos = np.arange(SEQ, dtype=np.float32)

    def tabs(d, p):
        inv = (np.float32(10000.0) ** (-np.arange(0, d, 2, dtype=np.float32) / np.float32(d))).astype(np.float32)
        ang = p[:, None].astype(np.float32) * inv[None, :]
        return np.cos(ang).astype(np.float32), np.sin(ang).astype(np.float32)
    co, si = tabs(32, pos)
    c["cos32"] = np.concatenate([co, co], 1)
    c["sin32"] = np.concatenate([-si, si], 1)
    c96 = np.ones((96, SEQ), np.float32)
    s96 = np.zeros((96, SEQ), np.float32)
    c96[64:80] = co.T
    c96[80:96] = co.T
    s96[64:80] = -si.T
    s96[80:96] = si.T
    c["c96"], c["s96"] = c96, s96
    co, si = tabs(64, pos)
    c["cos64"] = np.concatenate([co, co], 1)
    c["sin64"] = np.concatenate([-si, si], 1)
    c["onehot8"] = (np.arange(SEQ)[None, :] // 256 == np.arange(8)[:, None]).astype(np.float32)
    ps = np.full((4,), 16384.0, np.float32)
    co, si = tabs(32, ps)
    c["scos32"] = np.concatenate([co, co], 1)
    c["ssin32"] = np.concatenate([-si, si], 1)
    co, si = tabs(64, ps)
    c["scos64"] = np.concatenate([co, co], 1)
    c["ssin64"] = np.concatenate([-si, si], 1)
    return c


def sample_shared_inputs(inp, c):
    w_uk = np.ascontiguousarray(np.asarray(inp["w_uk"][0], dtype=np.float32))
    return {
        "p_ckv": np.asarray(inp["cache_mla_ckv"][0]).reshape(NPOOL * 128, 256),
        "p_kr": np.asarray(inp["cache_mla_krope"][0]).reshape(NPOOL * 128, 32),
        "p_k": np.asarray(inp["cache_moba_k"][0]).reshape(NPOOL * 128, 256),
        "p_v": np.asarray(inp["cache_moba_v"][0]).reshape(NPOOL * 128, 256),
        "w_ukT": np.ascontiguousarray(w_uk.transpose(2, 1, 0).reshape(64, 12 * 256)),
        "scos32": c["scos32"], "ssin32": c["ssin32"], "scos64": c["scos64"], "ssin64": c["ssin64"],
    }


def sample_core_inputs(inp, core):
    sl = slice(4 * core, 4 * core + 4)
    sc = np.asarray(inp["state_conv"], dtype=np.float32)[:, sl]
    sconv = sc.reshape(2, 4, 2, NFC, 128).transpose(0, 4, 3, 1, 2).reshape(2, 128, NFC * 8)
    return {
        "xs": np.ascontiguousarray(np.asarray(inp["x_sample"], dtype=np.float32)[sl, 0]),
        "ptab": np.ascontiguousarray(np.asarray(inp["page_table"], dtype=np.int32)[sl].reshape(1, 4 * 128)),
        "cmk": np.ascontiguousarray(np.asarray(inp["cache_mem_k"], dtype=np.float32)[:, sl].reshape(2, 4, 256, 256)),
        "cmv": np.ascontiguousarray(np.asarray(inp["cache_mem_v"], dtype=np.float32)[:, sl].reshape(2, 4, 256, 256)),
        "sconv": np.ascontiguousarray(sconv),
    }


def sample_outputs(R):
    n = len(R)
    st = lambda name: np.stack([np.asarray(R[i][name]) for i in range(n)])
    cs = st("conv_s").reshape(n, 2, 128, NFC, 4, 2)
    conv_s = np.ascontiguousarray(cs.transpose(1, 0, 4, 5, 3, 2).reshape(2, n * 4, 2, DFF))
    return dict(y_s=st("y_s").reshape(n * 4, 1, D), ckv_s=st("ckv_s").reshape(1, n * 4, 1, 256),
                krope_s=st("krope_s").reshape(1, n * 4, 1, 32), mobak_s=st("mobak_s").reshape(1, n * 4, 1, 4, 64),
                mobav_s=st("mobav_s").reshape(1, n * 4, 1, 4, 64), conv_s=conv_s)


_PROG = {}


def kernel(**inp):
    flags = dict(FLAGS)
    key = (flags["sample"], flags["layers"], flags.get("stop"), flags.get("prompt", True))
    if key not in _PROG:
        _PROG[key] = build_program(flags)
    nc = _PROG[key]
    f32 = lambda a: np.ascontiguousarray(np.asarray(a), dtype=np.float32)
    c = _consts()
    w_q_b = f32(inp["w_q_b"][0])
    sw = w_q_b.reshape(384, 12, 96).copy()
    sw[:, :, 64:80] = w_q_b.reshape(384, 12, 96)[:, :, 80:96]
    sw[:, :, 80:96] = w_q_b.reshape(384, 12, 96)[:, :, 64:80]
    conv_w = f32(inp["conv_w"])
    conv_b = f32(inp["conv_b"])
    shared = {
        "ident": c["ident"], "masks": c["masks"], "c96": c["c96"], "s96": c["s96"], "cos32": c["cos32"], "sin32": c["sin32"],
        "cos64": c["cos64"], "sin64": c["sin64"], "onehot8": c["onehot8"],
        "w_in_a": f32(inp["w_in_a"][0]), "g_q": f32(inp["g_q"]), "w_q_b": w_q_b, "w_q_b_sw": sw.reshape(384, 1152),
        "g_kv": f32(inp["g_kv"]), "w_uk": f32(inp["w_uk"][0]).reshape(256, 768), "w_uv": f32(inp["w_uv"][0]).reshape(256, 768),
        "w_in_b": f32(inp["w_in_b"][0]), "w_mem_k": f32(inp["w_mem_k"]), "w_mem_v": f32(inp["w_mem_v"]), "w_o": f32(inp["w_o"]),
        "ln1_g": f32(inp["ln1_g"]), "ln1_b": f32(inp["ln1_b"]), "ln2_g": f32(inp["ln2_g"]), "ln2_b": f32(inp["ln2_b"]),
        "w_up": f32(inp["w_up"]), "w_down": f32(inp["w_down"]),
        "convw": np.ascontiguousarray(conv_w.reshape(2, 3, NFC, 128).transpose(0, 3, 2, 1).reshape(2, 128, NFC * 3)),
        "convb": np.ascontiguousarray(conv_b.reshape(2, NFC, 128).transpose(0, 2, 1)),
    }
    if flags["sample"]:
        shared.update(sample_shared_inputs(inp, c))
    xp = f32(inp["x_prompt"])
    memp = f32(inp["mem_prompt"])
    in_maps = []
    for core in range(8):
        m = dict(shared)
        m["xp"] = xp[core]
        m["memp"] = memp[core]
        if flags["sample"]:
            m.update(sample_core_inputs(inp, core))
        in_maps.append(m)
    res = run_bass_kernel_spmd(nc, in_maps, core_ids=list(range(8)))
    R = res.results
    st = lambda name: np.stack([np.asarray(R[i][name]) for i in range(8)])
    y_p = st("y_p")
    ckv_p = st("ckv_p")[None]
    krope_p = st("krope_p")[None]
    mobak_p = st("mobak_p").reshape(8, SEQ, 4, 64)[None]
    mobav_p = st("mobav_p").reshape(8, SEQ, 4, 64)[None]
    memk = st("memk_p").transpose(1, 0, 2, 3).reshape(2, 8, 256, 4, 64)
    memv = st("memv_p").transpose(1, 0, 2, 3).reshape(2, 8, 256, 4, 64)
    cp = st("conv_p").reshape(8, 2, 128, NFC, 2)
    conv_p = np.ascontiguousarray(cp.transpose(1, 0, 4, 3, 2).reshape(2, 8, 2, DFF))
    if flags["sample"]:
        so = sample_outputs(R)
    else:
        z = lambda *s: np.zeros(s, np.float32)
        so = dict(y_s=z(32, 1, D), ckv_s=z(1, 32, 1, 256), krope_s=z(1, 32, 1, 32), mobak_s=z(1, 32, 1, 4, 64),
                  mobav_s=z(1, 32, 1, 4, 64), conv_s=z(2, 32, 2, DFF))
    return (y_p, so["y_s"], ckv_p, krope_p, mobak_p, mobav_p, memk, memv, conv_p,
            so["ckv_s"], so["krope_s"], so["mobak_s"], so["mobav_s"], so["conv_s"])
```

```python
from contextlib import ExitStack
import numpy as np
import concourse.bass as bass
import concourse.mybir as mybir

F32 = mybir.dt.float32
BF16 = mybir.dt.bfloat16
I32 = mybir.dt.int32
ALU = mybir.AluOpType
AF = mybir.ActivationFunctionType
AX = mybir.AxisListType
import os as _os
NO_SELF_WAIT = _os.environ.get("NOSELF", "0") == "1"


class Tile:
    __slots__ = ("t", "name", "lw", "readers", "dsem", "dcount", "sch", "excl")

    def __init__(self, sch, t, name):
        self.sch = sch
        self.t = t
        self.name = name
        self.lw = None
        self.readers = {}
        self.dsem = None
        self.dcount = 0
        self.excl = False

    def __getitem__(self, k):
        return self.t[k]


class Eng:
    def __init__(self, name, h, sem):
        self.name = name
        self.h = h
        self.sem = sem
        self.count = 0
        self.waited = {}
        self.prog = []


class Sched:
    def __init__(self, nc, es):
        self.nc = nc
        self.es = es
        self.eng = {}
        self.sb_bytes = 0
        self.nsem = 0
        self.dma_tiles = []
        for name, h in (("pe", nc.tensor), ("act", nc.scalar), ("dve", nc.vector),
                        ("pool", nc.gpsimd), ("sp", nc.sync)):
            self.eng[name] = Eng(name, h, self.new_sem("e_" + name))

    def new_sem(self, name):
        self.nsem += 1
        return self.es.enter_context(self.nc.semaphore(name))

    def sbuf(self, name, shape, dtype):
        t = self.es.enter_context(self.nc.sbuf_tensor("sb_" + name, list(shape), dtype))
        n = 1
        for d in shape[1:]:
            n *= d
        self.sb_bytes += n * (4 if dtype in (F32, I32) else 2)
        return Tile(self, t, name)

    def psum(self, name, shape, dtype):
        t = self.es.enter_context(self.nc.psum_tensor("pp_" + name, list(shape), dtype))
        tl = Tile(self, t, name)
        tl.excl = True
        return tl

    def view(self, tile, name=None):
        return Tile(self, tile.t, name or tile.name)

    def _deps(self, E, reads, writes):
        deps = []
        for t in reads:
            if t.lw is not None:
                deps.append(t.lw)
            if t.excl:
                deps.extend(ev for k, ev in t.readers.items() if k != id(E.sem))
        for t in writes:
            if t.lw is not None:
                deps.append(t.lw)
            deps.extend(t.readers.values())
        for sem, val in deps:
            if sem is E.sem and (E.name == "pe" or NO_SELF_WAIT):
                continue
            k = id(sem)
            if E.waited.get(k, 0) < val:
                E.waited[k] = val
                E.prog.append(("w", sem, val))

    def op(self, eng, fn, reads=(), writes=()):
        E = self.eng[eng]
        self._deps(E, reads, writes)
        E.count += 1
        E.prog.append(("op", fn))
        ev = (E.sem, E.count)
        for t in reads:
            t.readers[id(E.sem)] = ev
        for t in writes:
            t.lw = ev
            t.readers = {}

    def dma(self, eng, out, in_, reads=(), writes=(), fn=None, **kw):
        E = self.eng[eng]
        self._deps(E, reads, writes)
        tl = list(writes) + list(reads)
        t = tl[0]
        if t.dsem is None:
            t.dsem = self.new_sem("d_" + t.name)
            self.dma_tiles.append(t)
        t.dcount += 1
        ev = (t.dsem, 16 * t.dcount)
        E.prog.append(("dma", out, in_, t.dsem, kw, fn))
        for x in reads:
            x.readers[id(t.dsem)] = ev
        for x in writes:
            x.lw = ev
            x.readers = {}

    def finish(self):
        E = self.eng["sp"]
        for t in self.dma_tiles:
            E.prog.append(("w", t.dsem, 16 * t.dcount))

    def emit(self):
        self.finish()
        with self.nc.Block() as block:
            def run(E):
                def body(e):
                    for it in E.prog:
                        if it[0] == "w":
                            e.wait_ge(it[1], it[2])
                        elif it[0] == "op":
                            it[1](e).then_inc(E.sem, 1)
                        else:
                            _, out, in_, dsem, kw, fn = it
                            if fn is not None:
                                fn(e).then_inc(dsem, 16)
                            else:
                                e.dma_start(out=out, in_=in_, **kw).then_inc(dsem, 16)
                return body
            block.tensor(run(self.eng["pe"]))
            block.scalar(run(self.eng["act"]))
            block.vector(run(self.eng["dve"]))
            block.gpsimd(run(self.eng["pool"]))
            block.sync(run(self.eng["sp"]))

from concourse.bass_utils import run_bass_kernel_spmd
import os

D = 1024
SEQ = 2048
NT = 16
DFF = 2816
NFC = 22
ALPHA = float((2 * 2) ** 0.25)
LN_EPS = 1e-5
RMS_EPS = 1e-6
BIG = 30000.0
NPOOL = 5120
FLAGS = {"sample": True, "layers": 2, "stop": "", "prompt": True}


class StopBuild(Exception):
    pass


class K:
    pass


def build_program(flags):
    nc = bass.Bass("TRN2", target_bir_lowering=False)
    g = K()
    g.nc = nc

    def din(name, shape, dt=F32):
        return nc.dram_tensor(name, list(shape), dt, kind="ExternalInput").ap()

    def dout(name, shape, dt=F32):
        return nc.dram_tensor(name, list(shape), dt, kind="ExternalOutput").ap()

    I = {}
    for name, shape in [
        ("xp", [SEQ, D]), ("memp", [256, D]), ("ident", [128, 128]), ("masks", [128, 4 * 512]),
        ("c96", [96, SEQ]), ("s96", [96, SEQ]), ("cos32", [SEQ, 32]), ("sin32", [SEQ, 32]),
        ("cos64", [SEQ, 64]), ("sin64", [SEQ, 64]), ("onehot8", [8, SEQ]),
        ("w_in_a", [D, 928]), ("g_q", [1, 384]), ("w_q_b", [384, 1152]), ("w_q_b_sw", [384, 1152]),
        ("g_kv", [1, 256]), ("w_uk", [256, 768]), ("w_uv", [256, 768]), ("w_in_b", [D, 1536]),
        ("w_mem_k", [2, D, 256]), ("w_mem_v", [2, D, 256]), ("w_o", [2, D, D]),
        ("ln1_g", [2, D]), ("ln1_b", [2, D]), ("ln2_g", [2, D]), ("ln2_b", [2, D]),
        ("w_up", [2, D, 2 * DFF]), ("w_down", [2, DFF, D]),
        ("convw", [2, 128, NFC * 3]), ("convb", [2, 128, NFC]),
    ]:
        I[name] = din(name, shape)
    O = {}
    for name, shape in [
        ("y_p", [SEQ, D]), ("ckv_p", [SEQ, 256]), ("krope_p", [SEQ, 32]), ("mobak_p", [SEQ, 256]),
        ("mobav_p", [SEQ, 256]), ("memk_p", [2, 256, 256]), ("memv_p", [2, 256, 256]),
        ("conv_p", [2, 128, NFC * 2]),
    ]:
        O[name] = dout(name, shape)
    if flags["sample"]:
        declare_sample_io(g, I, O, din, dout)
    X1 = nc.dram_tensor("x1_scr", [SEQ, D], F32, kind="Internal").ap()
    X2 = nc.dram_tensor("x2_scr", [SEQ, D], F32, kind="Internal").ap()

    with ExitStack() as es:
        S = Sched(nc, es)
        g.S = S
        g.I = I
        g.O = O
        g.ps = [S.psum("ps%d" % i, [128, 512], F32) for i in range(4)]
        g.po = [S.psum("po%d" % i, [128, 512], F32) for i in range(2)]
        g.pt = [S.psum("pt%d" % i, [128, 1024], BF16) for i in range(2)]
        g.ips = 0
        g.ipo = 0
        g.ipt = 0

        def nps():
            g.ips = (g.ips + 1) % 4
            return g.ps[g.ips]

        def npo():
            g.ipo = (g.ipo + 1) % 2
            return g.po[g.ipo]

        def npt():
            g.ipt = (g.ipt + 1) % 2
            return g.pt[g.ipt]
        g.nps, g.npo, g.npt = nps, npo, npt

        ident = S.sbuf("ident", [128, 128], BF16)
        masks = S.sbuf("masks", [128, 4, 512], BF16)
        g.ident = ident
        S.dma("pool", ident[:], I["ident"][:, :], writes=[ident])
        S.dma("pool", masks[:], I["masks"].rearrange("p (j q) -> p j q", j=4), writes=[masks])
        XT = S.sbuf("XT", [128, 8, SEQ], BF16)
        MIX = S.sbuf("MIX", [128, 8, SEQ], BF16)
        BUF = S.sbuf("BUF", [128, 10240], F32)
        WA = S.sbuf("WA", [128, 8 * 1536], BF16)
        WB = S.sbuf("WB", [128, 10240], BF16)
        TAB = S.sbuf("TAB", [128, 4096], F32)
        lnp = S.sbuf("lnp", [128, 2, D], F32)
        gqb = S.sbuf("gqb", [128, 384 + 256], F32)
        cvw = S.sbuf("cvw", [128, NFC, 3], F32)
        cvb = S.sbuf("cvb", [128, NFC], F32)
        cst = S.sbuf("cst", [128, NFC, 2], F32)
        xf = [S.sbuf("xf%d" % i, [128, D], F32) for i in range(1)]
        xb = [S.sbuf("xb%d" % i, [128, D], BF16) for i in range(1)]
        zt = [S.sbuf("zt%d" % i, [128, D], F32) for i in range(1)]
        sq = S.sbuf("sq", [128, D], F32)
        st = S.sbuf("st", [128, 16], F32)
        pT = [S.sbuf("pT%d" % i, [128, 512], BF16) for i in range(3)]
        rc = [S.sbuf("rc%d" % i, [128, 512], F32) for i in range(1)]
        stg = [S.sbuf("stg%d" % i, [128, 512], F32) for i in range(2)]
        sgb = [S.sbuf("sgb%d" % i, [128, 1024], BF16) for i in range(1)]
        g.ctr = {}

        def rot(lst, key):
            g.ctr[key] = (g.ctr.get(key, -1) + 1) % len(lst)
            return lst[g.ctr[key]]

        def MM(ps, out, lhsT, rhs, start, stop, reads):
            S.op("pe", lambda e, o=out, l=lhsT, r=rhs, a=start, b=stop: e.matmul(o, lhsT=l, rhs=r, start=a, stop=b),
                 reads=reads, writes=[ps])

        def TR(pt_t, out, in_, reads, npart=128):
            S.op("pe", lambda e, o=out, i=in_, n=npart: e.transpose(out=o, in_=i, identity=ident[0:n, 0:n]),
                 reads=list(reads) + [ident], writes=[pt_t])

        def ACT(out, in_, func, reads, writes, bias=None, scale=None):
            kw = {}
            if bias is not None:
                kw["bias"] = bias
            if scale is not None:
                kw["scale"] = scale
            S.op("act", lambda e, o=out, i=in_, f=func, k=kw: e.activation(out=o, in_=i, func=f, **k),
                 reads=reads, writes=writes)

        def TT(eng, out, in0, in1, op, reads, writes):
            S.op(eng, lambda e, o=out, a=in0, b=in1, p=op: e.tensor_tensor(out=o, in0=a, in1=b, op=p),
                 reads=reads, writes=writes)

        def TS(eng, out, in0, s1, s2, op0, op1, reads, writes):
            if op1 is None:
                S.op(eng, lambda e, o=out, a=in0, x=s1, p=op0: e.tensor_scalar(out=o, in0=a, scalar1=x, scalar2=None, op0=p),
                     reads=reads, writes=writes)
            else:
                S.op(eng, lambda e, o=out, a=in0, x=s1, y=s2, p=op0, q=op1:
                     e.tensor_scalar(out=o, in0=a, scalar1=x, scalar2=y, op0=p, op1=q), reads=reads, writes=writes)

        def STT(eng, out, in0, sc, in1, op0, op1, reads, writes):
            S.op(eng, lambda e, o=out, a=in0, s=sc, b=in1, p=op0, q=op1:
                 e.scalar_tensor_tensor(out=o, in0=a, scalar=s, in1=b, op0=p, op1=q), reads=reads, writes=writes)

        def CP(eng, out, in_, reads, writes):
            S.op(eng, lambda e, o=out, i=in_: e.tensor_copy(out=o, in_=i), reads=reads, writes=writes)

        def MS(eng, ap, val, writes):
            S.op(eng, lambda e, a=ap, v=val: e.memset(a, v), writes=writes)

        def RS(out, in_, reads, writes):
            S.op("dve", lambda e, o=out, i=in_: e.reduce_sum(out=o, in_=i, axis=AX.X), reads=reads, writes=writes)

        def RCP(out, in_, reads, writes):
            S.op("dve", lambda e, o=out, i=in_: e.reciprocal(out=o, in_=i), reads=reads, writes=writes)

        def barrier():
            evs = [(E.sem, E.count) for E in S.eng.values() if E.count > 0]
            evs += [(t.dsem, 16 * t.dcount) for t in S.dma_tiles]
            for E in S.eng.values():
                for sem, val in evs:
                    if sem is E.sem:
                        continue
                    k = id(sem)
                    if E.waited.get(k, 0) < val:
                        E.waited[k] = val
                        E.prog.append(("w", sem, val))
        g.MM, g.TR, g.ACT, g.TT, g.TS, g.STT, g.CP, g.MS, g.RS, g.RCP, g.barrier, g.rot = \
            MM, TR, ACT, TT, TS, STT, CP, MS, RS, RCP, barrier, rot
        g.stg, g.sgb, g.st, g.sq, g.zt, g.xf, g.xb, g.pT, g.rc = stg, sgb, st, sq, zt, xf, xb, pT, rc
        g.lnp, g.gqb, g.cvw, g.cvb = lnp, gqb, cvw, cvb

        import os
        def wload(dst_tile, dst_ap, src_ap):
            if os.environ.get("SKIPW") == "1":
                return
            for kc in range(dst_ap.shape[1]):
                S.dma("pool", dst_ap[:, kc, :], src_ap[:, kc, :], writes=[dst_tile])
        g.wload = wload

        def kcw(w2d):
            return w2d.rearrange("(kc p) n -> p kc n", p=128)

        def rstd_of(ps_slice, n, np_, eps, reads):
            ACT(sq[0:np_, 0:n], ps_slice, AF.Square, reads, [sq])
            RS(st[0:np_, 0:1], sq[0:np_, 0:n], [sq], [st])
            ACT(st[0:np_, 1:2], st[0:np_, 0:1], AF.Sqrt, [st], [st], bias=eps, scale=1.0 / n)
            RCP(st[0:np_, 2:3], st[0:np_, 1:2], [st], [st])
            return st[0:np_, 2:3]
        g.rstd_of = rstd_of

        def layernorm(z, np_, li, out_tile, out_ap):
            zz = z[0:np_, :]
            RS(st[0:np_, 4:5], zz, [z], [st])
            ACT(sq[0:np_, :], zz, AF.Square, [z], [sq])
            RS(st[0:np_, 5:6], sq[0:np_, :], [sq], [st])
            TS("dve", st[0:np_, 6:7], st[0:np_, 4:5], 1.0 / D, None, ALU.mult, None, [st], [st])
            TT("dve", st[0:np_, 7:8], st[0:np_, 6:7], st[0:np_, 6:7], ALU.mult, [st], [st])
            STT("dve", st[0:np_, 8:9], st[0:np_, 5:6], 1.0 / D, st[0:np_, 7:8], ALU.mult, ALU.subtract, [st], [st])
            ACT(st[0:np_, 9:10], st[0:np_, 8:9], AF.Sqrt, [st], [st], bias=LN_EPS, scale=1.0)
            RCP(st[0:np_, 10:11], st[0:np_, 9:10], [st], [st])
            TS("dve", zz, zz, st[0:np_, 6:7], st[0:np_, 10:11], ALU.subtract, ALU.mult, [z, st], [z])
            TT("pool", zz, zz, lnp[0:np_, 0, :], ALU.mult, [z, lnp], [z])
            TT("pool", out_ap, zz, lnp[0:np_, 1, :], ALU.add, [z, lnp], [out_tile])
        g.layernorm = layernorm

        def to_xT(src_bf, dst, t):
            pt_t = npt()
            for kc in range(8):
                TR(pt_t, pt_t[:, kc * 128:(kc + 1) * 128], src_bf[:, kc * 128:(kc + 1) * 128], [src_bf])
            CP("dve", dst[:, :, t * 128:(t + 1) * 128], pt_t[:, :].rearrange("p (k n) -> p k n", k=8), [pt_t], [dst])
        g.to_xT = to_xT

        def attention(qT, qtile, kT, ktile, Kp0, Kp1, vaug_fn, vtile, nkt, causal, scale, den_lo, out_chunk):
            num0 = 64 if den_lo else 0
            den0 = 0 if den_lo else 64
            for qb in range(4):
                po = npo()
                kts = list(range(4 * qb + 4)) if causal else list(range(nkt))
                LOOK = 2
                pss = {}

                def issue_score(i, qb=qb, kts=kts, pss=pss):
                    kt = kts[i]
                    ps = nps()
                    MM(ps, ps[:, :], kT[Kp0:Kp1, kt * 128:(kt + 1) * 128], qT[Kp0:Kp1, qb * 512:(qb + 1) * 512],
                       True, True, [ktile, qtile])
                    pss[i] = ps
                for i in range(min(LOOK, len(kts))):
                    issue_score(i)
                for i, kt in enumerate(kts):
                    ps = pss.pop(i)
                    p = rot(pT, "pT")
                    ACT(p[:, :], ps[:, :], AF.Exp, [ps], [p], scale=scale)
                    if causal and kt >= 4 * qb:
                        TT("pool", p[:, :], p[:, :], masks[:, kt - 4 * qb, :], ALU.mult, [p, masks], [p])
                    if i + LOOK < len(kts):
                        issue_score(i + LOOK)
                    MM(po, po[:, :], vaug_fn(kt), p[:, :], i == 0, i == len(kts) - 1, [vtile, p])
                r = rot(rc, "rc")
                RCP(r[den0:den0 + 64, :], po[den0:den0 + 64, :], [po], [r])
                TT("dve", MIX[num0:num0 + 64, out_chunk, qb * 512:(qb + 1) * 512], po[num0:num0 + 64, :],
                   r[den0:den0 + 64, :], ALU.mult, [po, r], [MIX])
        g.attention = attention

        BUFb = S.view(BUF)
        Gk = [S.view(XT, "Gk0"), S.view(XT, "Gk1")]
        Gq = [S.view(XT, "Gq0"), S.view(XT, "Gq1")]
        Gv = [S.view(XT, "Gv0"), S.view(XT, "Gv1")]
        Gt = [S.view(BUF, "Gt0"), S.view(BUF, "Gt1")]
        g.mkT_sb = S.sbuf("mkT_sb", [128, 2, 256], BF16)
        g.vmem_sb = S.sbuf("vmem_sb", [128, 2, 4, 128], BF16)
        g.X1t = Tile(S, None, "X1t")
        g.X2t = Tile(S, None, "X2t")
        g.wtile = [S.view(WB, "wt0"), S.view(WB, "wt1")]
        g.wdtile = [S.view(WA, "wd0"), S.view(WA, "wd1")]
        g.gftile = [S.view(MIX, "gf0"), S.view(MIX, "gf1")]
        g.fttile = [S.view(MIX, "ft0"), S.view(MIX, "ft1")]
        g.hTtile = S.view(MIX, "hTt")

        def bview(off_words, nwords, pattern=None, **kw):
            ap = BUF[:, off_words:off_words + nwords].bitcast(BF16)
            if pattern:
                ap = ap.rearrange(pattern, **kw)
            return ap

        xsrc = I["xp"]
        def stop_if(tag):
            if flags.get("stop") == tag:
                raise StopBuild()
        try:
          for L in range(flags["layers"] if flags.get("prompt", True) else 0):
              mla = (L == 0)
              barrier()
              def load_ln(which):
                  for i, nm in enumerate([which + "_g", which + "_b"]):
                      S.dma("sp", lnp[:, i, :], I[nm][L:L + 1, :].partition_broadcast(128)[:, 0, :], writes=[lnp])
              load_ln("ln1")
              S.dma("sp", cvw[:], I["convw"][L].rearrange("p (c t) -> p c t", t=3), writes=[cvw])
              S.dma("sp", cvb[:], I["convb"][L], writes=[cvb])
              WMt = WA if mla else WB
              WM = WMt[:, 8192:12288] if mla else WMt[:, 0:4096]
              wmk = WM[:, 0:2048].rearrange("p (k n) -> p k n", k=8)
              wmv = WM[:, 2048:4096].rearrange("p (k n) -> p k n", k=8)
              wload(WMt, wmk, kcw(I["w_mem_k"][L]))
              wload(WMt, wmv, kcw(I["w_mem_v"][L]))
              if mla:
                  NIN = 928
                  wload(WA, WA[:, 0:8 * NIN].rearrange("p (k n) -> p k n", k=8), kcw(I["w_in_a"]))
                  S.dma("sp", gqb[:, 0:384], I["g_q"].partition_broadcast(128)[:, 0, :], writes=[gqb])
                  S.dma("sp", gqb[:, 384:640], I["g_kv"].partition_broadcast(128)[:, 0, :], writes=[gqb])
                  S.dma("sp", TAB[:, 0:512].rearrange("p (t d) -> p t d", d=32), I["cos32"].rearrange("(t p) d -> p t d", p=128), writes=[TAB])
                  S.dma("sp", TAB[:, 512:1024].rearrange("p (t d) -> p t d", d=32), I["sin32"].rearrange("(t p) d -> p t d", p=128), writes=[TAB])
                  cosT = TAB[:, 0:512].rearrange("p (t d) -> p t d", d=32)
                  sinT = TAB[:, 512:1024].rearrange("p (t d) -> p t d", d=32)
              else:
                  NIN = 1536
                  wload(WA, WA[:, 0:8 * NIN].rearrange("p (k n) -> p k n", k=8), kcw(I["w_in_b"]))
                  S.dma("sp", TAB[:, 0:1024].rearrange("p (t d) -> p t d", d=64), I["cos64"].rearrange("(t p) d -> p t d", p=128), writes=[TAB])
                  S.dma("sp", TAB[:, 1024:2048].rearrange("p (t d) -> p t d", d=64), I["sin64"].rearrange("(t p) d -> p t d", p=128), writes=[TAB])
                  cosT = TAB[:, 0:1024].rearrange("p (t d) -> p t d", d=64)
                  sinT = TAB[:, 1024:2048].rearrange("p (t d) -> p t d", d=64)
              win = WA[:, 0:8 * NIN].rearrange("p (k n) -> p k n", k=8)

              if mla:
                  qnT = bview(0, 3072, "p (k n) -> p k n", k=3)
                  cnT = bview(3072, 2048, "p (k n) -> p k n", k=2)
                  krT = bview(5120, 1024)
                  qmT = bview(6144, 2048, "p (k n) -> p k n", k=2)
                  XT2 = XT[:, :, :].rearrange("p k n -> p (k n)")
                  kTh = [XT2[:, 0:2048], XT2[:, 2048:4096]]
                  qTh = [XT2[:, 4096:6144], XT2[:, 6144:8192]]
                  vaug = [XT2[:, 8192:10240].rearrange("p (t n) -> p t n", t=16), XT2[:, 10240:12288].rearrange("p (t n) -> p t n", t=16)]
                  tmpf = [BUF[:, 8960:9472], BUF[:, 9472:9984]]
              elif flags.get('stop') != 'P0all':
                  XT2 = XT[:, :, :].rearrange("p k n -> p (k n)")
                  qtok = XT2[:, 0:12288].rearrange("p (t h d) -> p t h d", t=16, h=12)
                  qTa = XT2[:, 12288:14336]
                  qa = XT2[:, 14336:15488].rearrange("p (t c) -> p t c", t=16)
                  kTa = bview(0, 4096, "p (k n) -> p k n", k=4)
                  vtok = bview(4096, 2048, "p (t n) -> p t n", t=16)
                  qmT = bview(6144, 2048, "p (k n) -> p k n", k=2)
                  vaug = [bview(8192, 1024, "p (t n) -> p t n", t=16), bview(9216, 1024, "p (t n) -> p t n", t=16)]
                  xTt = WB[:, 4096:5120].rearrange("p (k n) -> p k n", k=8)
                  biasb = WB[:, 5120:6656].rearrange("p (t h b) -> p t h b", t=16, h=12)
                  qTt = WB[:, 6656:8192].rearrange("p (h n) -> p h n", h=12)
                  sbm = WB[:, 8192:8704]
                  kmT = WB[:, 8704:8736].rearrange("p (k b) -> p k b", k=4)
                  G_qtok, G_qTa, G_qa = S.view(XT, "G_qtok"), S.view(XT, "G_qTa"), S.view(XT, "G_qa")
                  G_kTa, G_vtok, G_qmT = S.view(BUF, "G_kTa"), S.view(BUF, "G_vtok"), S.view(BUF, "G_qmT")
                  G_xTt, G_bias, G_qTt, G_sbm, G_kmT = (S.view(WB, "G_xTt"), S.view(WB, "G_bias"), S.view(WB, "G_qTt"),
                                                       S.view(WB, "G_sbm"), S.view(WB, "G_kmT"))
              stop_if('S0')
              memT = MIX
              for mt in range(2):
                  f = rot(xf, "xf")
                  S.dma("sp", f[:], I["memp"][mt * 128:(mt + 1) * 128, :], writes=[f])
                  b = rot(xb, "xb")
                  ACT(b[:], f[:], AF.Copy, [f], [b])
                  to_xT(b, memT, mt)
              skipviews = flags.get("stop") == "P0all"
              mkT_ap, vmem_ap = g.mkT_sb[:, :, :], g.vmem_sb[:, :, :, :]
              mkT_tile, vmem_tile = g.mkT_sb, g.vmem_sb
              if not skipviews:
                  MS("pool", vmem_ap, 1.0, [vmem_tile])
              for mt in range(2):
                  for wi, (w, oname) in enumerate([(wmk, "memk_p"), (wmv, "memv_p")]):
                      ps = nps()
                      for kc in range(8):
                          MM(ps, ps[:, 0:256], memT[:, kc, mt * 128:(mt + 1) * 128], w[:, kc, :], kc == 0, kc == 7, [MIX, WMt])
                      sg = rot(stg, "stg")
                      CP("dve", sg[:, 0:256], ps[:, 0:256], [ps], [sg])
                      if True:
                          S.dma("sp", O[oname][L, mt * 128:(mt + 1) * 128, :], sg[:, 0:256], reads=[sg])
                      if wi == 1 and not skipviews:
                          for j in range(4):
                              c0 = 0 if j % 2 == 0 else 64
                              if os.environ.get("DBG") == "dvecp":
                                  CP("dve", vmem_ap[:, mt, j, c0:c0 + 64], ps[:, j * 64:(j + 1) * 64], [ps], [vmem_tile])
                              else:
                                  ACT(vmem_ap[:, mt, j, c0:c0 + 64], ps[:, j * 64:(j + 1) * 64], AF.Copy, [ps], [vmem_tile])
              for pr in range(2):
                  ps = nps()
                  for kc in range(8):
                      MM(ps, ps[:, 0:256], wmk[:, kc, pr * 128:(pr + 1) * 128], memT[:, kc, 0:256], kc == 0, kc == 7, [MIX, WMt])
                  if not skipviews:
                      if os.environ.get("DBG") == "dvecp":
                          CP("dve", mkT_ap[:, pr, :], ps[:, 0:256], [ps], [mkT_tile])
                      else:
                          ACT(mkT_ap[:, pr, :], ps[:, 0:256], AF.Copy, [ps], [mkT_tile])

              stop_if('P0')
              if flags.get('stop') == 'P0all':
                  continue
              if not mla:
                  MS("pool", biasb, -BIG, [G_bias])
                  for kvh in range(4):
                      S.dma("pool", kTa[64:72, kvh, :], I["onehot8"][:, :], writes=[G_kTa])
              for t in range(NT):
                  f = rot(xf, "xf")
                  S.dma("sp", f[:], xsrc[t * 128:(t + 1) * 128, :], writes=[f])
                  b = rot(xb, "xb")
                  ACT(b[:], f[:], AF.Copy, [f], [b])
                  if mla:
                      to_xT(b, XT, t)
                      xts = [XT[:, kc, t * 128:(t + 1) * 128] for kc in range(8)]
                      xtile = XT
                  else:
                      pt_x = npt()
                      for kc in range(8):
                          TR(pt_x, pt_x[:, kc * 128:(kc + 1) * 128], b[:, kc * 128:(kc + 1) * 128], [b])
                      CP("dve", xTt, pt_x[:, :].rearrange("p (k n) -> p k n", k=8), [pt_x], [G_xTt])
                      xts = [xTt[:, kc, :] for kc in range(8)]
                      xtile = G_xTt

                  def proj(c0, c1):
                      ps = nps()
                      for kc in range(8):
                          MM(ps, ps[:, 0:c1 - c0], xts[kc], win[:, kc, c0:c1], kc == 0, kc == 7, [xtile, WA])
                      return ps
                  if mla:
                      psA = proj(0, 384)
                      r = rstd_of(psA[:, 0:384], 384, 128, RMS_EPS, [psA])
                      sb = rot(sgb, "sgb")
                      STT("dve", sb[:, 0:384], psA[:, 0:384], r, gqb[:, 0:384], ALU.mult, ALU.mult, [psA, st, gqb], [sb])
                      psB = proj(384, 672)
                      r = rstd_of(psB[:, 0:256], 256, 128, RMS_EPS, [psB])
                      sg = rot(stg, "stg")
                      STT("dve", sg[:, 0:256], psB[:, 0:256], r, gqb[:, 384:640], ALU.mult, ALU.mult, [psB, st, gqb], [sg])
                      ACT(sb[:, 384:640], sg[:, 0:256], AF.Copy, [sg], [sb])
                      kx = psB[:, 256:288]
                      TT("dve", sg[:, 256:288], kx, cosT[:, t, :], ALU.mult, [psB, TAB], [sg])
                      TT("dve", sg[:, 288:304], psB[:, 272:288], sinT[:, t, 0:16], ALU.mult, [psB, TAB], [sg])
                      TT("dve", sg[:, 304:320], psB[:, 256:272], sinT[:, t, 16:32], ALU.mult, [psB, TAB], [sg])
                      TT("dve", sg[:, 256:288], sg[:, 256:288], sg[:, 288:320], ALU.add, [sg], [sg])
                      S.dma("sp", O["ckv_p"][t * 128:(t + 1) * 128, :], sg[:, 0:256], reads=[sg])
                      S.dma("sp", O["krope_p"][t * 128:(t + 1) * 128, :], sg[:, 256:288], reads=[sg])
                      MS("pool", sb[:, 640:704], 0.0, [sb])
                      ACT(sb[:, 704:736], sg[:, 256:288], AF.Copy, [sg], [sb])
                      psC = proj(672, 928)
                      ACT(sb[:, 768:1024], psC[:, 0:256], AF.Copy, [psC], [sb])
                      pt_t = npt()
                      for j in range(3):
                          TR(pt_t, pt_t[:, j * 128:(j + 1) * 128], sb[:, j * 128:(j + 1) * 128], [sb])
                      for j in range(2):
                          TR(pt_t, pt_t[:, (3 + j) * 128:(4 + j) * 128], sb[:, 384 + j * 128:384 + (j + 1) * 128], [sb])
                      TR(pt_t, pt_t[0:96, 5 * 128:6 * 128], sb[:, 640:736], [sb])
                      for j in range(2):
                          TR(pt_t, pt_t[:, (6 + j) * 128:(7 + j) * 128], sb[:, 768 + j * 128:768 + (j + 1) * 128], [sb])
                      tsl = slice(t * 128, (t + 1) * 128)
                      CP("dve", qnT[:, :, tsl], pt_t[:, 0:384].rearrange("p (k n) -> p k n", k=3), [pt_t], [BUFb])
                      CP("dve", cnT[:, :, tsl], pt_t[:, 384:640].rearrange("p (k n) -> p k n", k=2), [pt_t], [BUFb])
                      ACT(krT[0:96, tsl], pt_t[0:96, 640:768], AF.Copy, [pt_t], [BUFb])
                      ACT(qmT[:, :, tsl], pt_t[:, 768:1024].rearrange("p (k n) -> p k n", k=2), AF.Copy, [pt_t], [BUFb])
                  else:
                      tsl = slice(t * 128, (t + 1) * 128)
                      own = t // 2
                      cosB6 = cosT[:, t:t + 1, :].to_broadcast([128, 6, 64])
                      sinB6 = sinT[:, t:t + 1, :].to_broadcast([128, 6, 64])
                      cosB4 = cosT[:, t:t + 1, :].to_broadcast([128, 4, 64])
                      sinB4 = sinT[:, t:t + 1, :].to_broadcast([128, 4, 64])
                      t1 = sq[:, 0:384].rearrange("p (h d) -> p h d", h=6)
                      t2 = sq[:, 384:768].rearrange("p (h d) -> p h d", h=6)

                      def rope(src3, nh, cB, sB, dst3, dst_tile, src_tile):
                          a, b2 = t1[:, 0:nh, :], t2[:, 0:nh, :]
                          TT("dve", a, src3, cB, ALU.mult, [src_tile, TAB], [sq])
                          TT("dve", b2[:, :, 0:32], src3[:, :, 32:64], sB[:, :, 0:32], ALU.mult, [src_tile, TAB], [sq])
                          TT("dve", b2[:, :, 32:64], src3[:, :, 0:32], sB[:, :, 32:64], ALU.mult, [src_tile, TAB], [sq])
                          TT("dve", dst3, a, b2, ALU.add, [sq], [dst_tile])
                      for qh in range(2):
                          psq = proj(qh * 384, (qh + 1) * 384)
                          rope(psq[:, 0:384].rearrange("p (h d) -> p h d", h=6), 6, cosB6, sinB6,
                               qtok[:, t, qh * 6:(qh + 1) * 6, :], G_qtok, psq)
                      pskv = proj(768, 1280)
                      sgk = rot(stg, "stg")
                      rope(pskv[:, 0:256].rearrange("p (h d) -> p h d", h=4), 4, cosB4, sinB4,
                           sgk[:, 0:256].rearrange("p (h d) -> p h d", h=4), sgk, pskv)
                      S.dma("sp", O["mobak_p"][tsl, :], sgk[:, 0:256], reads=[sgk])
                      ACT(sbm[:, 0:256], sgk[:, 0:256], AF.Copy, [sgk], [G_sbm])
                      sgv = rot(stg, "stg")
                      CP("dve", sgv[:, 0:256], pskv[:, 256:512], [pskv], [sgv])
                      S.dma("sp", O["mobav_p"][tsl, :], sgv[:, 0:256], reads=[sgv])
                      ACT(vtok[:, t, :], sgv[:, 0:256], AF.Copy, [sgv], [G_vtok])
                      psm = proj(1280, 1536)
                      ACT(sbm[:, 256:512], psm[:, 0:256], AF.Copy, [psm], [G_sbm])
                      pt_t = npt()
                      for kvh in range(4):
                          TR(pt_t, pt_t[0:64, kvh * 128:(kvh + 1) * 128], sbm[:, kvh * 64:(kvh + 1) * 64], [G_sbm])
                      for j in range(2):
                          TR(pt_t, pt_t[:, (4 + j) * 128:(5 + j) * 128], sbm[:, 256 + j * 128:256 + (j + 1) * 128], [G_sbm])
                      CP("dve", kTa[0:64, :, tsl], pt_t[0:64, 0:512].rearrange("p (k n) -> p k n", k=4), [pt_t], [G_kTa])
                      ACT(qmT[:, :, tsl], pt_t[:, 512:768].rearrange("p (k n) -> p k n", k=2), AF.Copy, [pt_t], [G_qmT])
                      if own >= 1:
                          if own >= 4:
                              for h0, nh in ((0, 8), (8, 4)):
                                  ptq = npt()
                                  for hh in range(nh):
                                      TR(ptq, ptq[0:64, hh * 128:(hh + 1) * 128], qtok[:, t, h0 + hh, :], [G_qtok])
                                  CP("dve", qTt[0:64, h0:h0 + nh, :], ptq[0:64, 0:nh * 128].rearrange("p (h n) -> p h n", h=nh), [ptq], [G_qTt])
                              psg = nps()
                              for h in range(12):
                                  MM(psg, psg[:, h * 8:h * 8 + own], qTt[0:64, h, :], kmT[0:64, h // 3, 0:own], True, True, [G_qTt, G_kmT])
                              gt = sq[:, 768:864].rearrange("p (h b) -> p h b", h=12)
                              top8 = sq[:, 864:960].rearrange("p (h b) -> p h b", h=12)
                              selt = sq[:, 0:96].rearrange("p (h b) -> p h b", h=12)
                              MS("dve", gt, -1.0e30, [sq])
                              CP("dve", gt[:, :, 0:own], psg[:, 0:96].rearrange("p (h b) -> p h b", h=12)[:, :, 0:own], [psg], [sq])
                              for h in range(12):
                                  S.op("dve", lambda e, o=top8[:, h, :], i=gt[:, h, :]: e.max(out=o, in_=i), reads=[sq], writes=[sq])
                                  TS("dve", selt[:, h, 0:own], gt[:, h, 0:own], top8[:, h, 2:3], None, ALU.is_ge, None, [sq], [sq])
                              TS("dve", biasb[:, t, :, 0:own], selt[:, :, 0:own], -1.0, BIG, ALU.add, ALU.mult, [sq], [G_bias])
                          else:
                              MS("pool", biasb[:, t, :, 0:own], 0.0, [G_bias])
                      MS("pool", biasb[:, t, :, own:own + 1], 0.0, [G_bias])
                      if t % 2 == 1:
                          bb = t // 2
                          kmf = sq[0:64, 960:964]
                          S.op("dve", lambda e, o=kmf, i=kTa[0:64, :, bb * 256:(bb + 1) * 256]: e.reduce_sum(out=o, in_=i, axis=AX.X),
                               reads=[G_kTa], writes=[sq])
                          TS("dve", kmT[0:64, :, bb], kmf, 1.0 / 256.0, None, ALU.mult, None, [sq], [G_kmT])

              stop_if('P1')
              if mla:
                  barrier()
                  wq = WB[:, 0:3456].rearrange("p (k n) -> p k n", k=3)
                  wqs = WB[:, 3456:6912].rearrange("p (k n) -> p k n", k=3)
                  wuk = WB[:, 6912:8448].rearrange("p (k n) -> p k n", k=2)
                  wuv = WB[:, 8448:9984].rearrange("p (k n) -> p k n", k=2)
                  wload(WB, wq, kcw(I["w_q_b"]))
                  wload(WB, wqs, kcw(I["w_q_b_sw"]))
                  wload(WB, wuk, kcw(I["w_uk"]))
                  wload(WB, wuv, kcw(I["w_uv"]))
                  S.dma("sp", TAB[0:96, 0:2048], I["c96"][:, :], writes=[TAB])
                  S.dma("sp", TAB[0:96, 2048:4096], I["s96"][:, :], writes=[TAB])
                  wload(WA, WA[:, 0:8192].rearrange("p (k n) -> p k n", k=8), kcw(I["w_o"][L]))
                  MS("pool", vaug[0][:, :, 64:128], 1.0, [Gv[0]])
                  MS("pool", vaug[1][:, :, 0:64], 1.0, [Gv[1]])
                  for h in range(12):
                      kT_h = kTh[h % 2]
                      qT_h = qTh[h % 2]
                      va = vaug[h % 2]
                      gk, gq, gv = Gk[h % 2], Gq[h % 2], Gv[h % 2]
                      v0 = 0 if h % 2 == 0 else 64
                      for tb in range(4):
                          bs = slice(tb * 512, (tb + 1) * 512)
                          ps = nps()
                          for kc in range(2):
                              MM(ps, ps[0:64, :], wuk[:, kc, h * 64:(h + 1) * 64], cnT[:, kc, bs], kc == 0, kc == 1, [WB, BUFb])
                          ACT(kT_h[0:64, bs], ps[0:64, :], AF.Copy, [ps], [gk])
                          ps1 = nps()
                          for kc in range(3):
                              MM(ps1, ps1[0:96, :], wq[:, kc, h * 96:(h + 1) * 96], qnT[:, kc, bs], kc == 0, kc == 2, [WB, BUFb])
                          ps2 = nps()
                          for kc in range(3):
                              MM(ps2, ps2[0:96, :], wqs[:, kc, h * 96:(h + 1) * 96], qnT[:, kc, bs], kc == 0, kc == 2, [WB, BUFb])
                          TT("dve", tmpf[0][0:96, :], ps1[0:96, :], TAB[0:96, tb * 512:(tb + 1) * 512], ALU.mult, [ps1, TAB], [Gt[0]])
                          TT("dve", tmpf[1][0:96, :], ps2[0:96, :], TAB[0:96, 2048 + tb * 512:2048 + (tb + 1) * 512], ALU.mult, [ps2, TAB], [Gt[1]])
                          TT("pool", qT_h[0:96, bs], tmpf[0][0:96, :], tmpf[1][0:96, :], ALU.add, [Gt[0], Gt[1]], [gq])
                      CP("pool", kT_h[64:96, :], krT[64:96, :], [BUFb], [gk])
                      for half in range(2):
                          ps = nps()
                          for tt in range(8):
                              t = half * 8 + tt
                              for kc in range(2):
                                  MM(ps, ps[:, tt * 64:(tt + 1) * 64], cnT[:, kc, t * 128:(t + 1) * 128], wuv[:, kc, h * 64:(h + 1) * 64],
                                     kc == 0, kc == 1, [WB, BUFb])
                          ACT(va[:, half * 8:(half + 1) * 8, v0:v0 + 64], ps[:, :].rearrange("p (t d) -> p t d", t=8), AF.Copy, [ps], [gv])
                      attention(qT_h, gq, kT_h, gk, 0, 96, (lambda kt, va=va: va[:, kt, :]), gv, 16, True,
                                float(96 ** -0.5), h % 2 == 1, h // 2)
              else:
                  barrier()
                  wload(WA, WA[:, 0:8192].rearrange("p (k n) -> p k n", k=8), kcw(I["w_o"][L]))
                  Gv2 = [S.view(BUF, "Gv2_0"), S.view(BUF, "Gv2_1")]
                  MS("pool", vaug[0][:, :, 64:128], 1.0, [Gv2[0]])
                  MS("pool", vaug[1][:, :, 0:64], 1.0, [Gv2[1]])
                  for h in range(12):
                      kvh = h // 3
                      par = h % 2
                      va, gv = vaug[par], Gv2[par]
                      v0 = 0 if par == 0 else 64
                      CP("pool", qa[:, :, 0:64], qtok[:, :, h, :], [G_qtok], [G_qa])
                      CP("pool", qa[:, :, 64:72], biasb[:, :, h, :], [G_bias], [G_qa])
                      for half in range(2):
                          ptq = npt()
                          for tt in range(8):
                              TR(ptq, ptq[0:72, tt * 128:(tt + 1) * 128], qa[:, half * 8 + tt, :], [G_qa])
                          CP("dve", qTa[0:72, half * 1024:(half + 1) * 1024], ptq[0:72, :], [ptq], [G_qTa])
                      CP("pool", va[:, :, v0:v0 + 64], vtok[:, :, kvh * 64:(kvh + 1) * 64], [G_vtok], [gv])
                      attention(qTa, G_qTa, kTa[:, kvh, :], G_kTa, 0, 72, (lambda kt, va=va: va[:, kt, :]), gv, 16, True,
                                0.125, par == 1, h // 2)
                  BUFb_q = G_qmT
              for j in range(4):
                  p0 = (j % 2) * 64
                  attention(qmT[:, j // 2, :], (BUFb if mla else G_qmT), mkT_ap[:, j // 2, :], mkT_tile, p0, p0 + 64,
                            (lambda kt, j=j: vmem_ap[:, kt, j, :]), vmem_tile, 2, False, 0.125, j % 2 == 1, 6 + j // 2)

              stop_if('P2')
              barrier()
              wo = WA[:, 0:8192].rearrange("p (k n) -> p k n", k=8)
              for t in range(NT):
                  pa, pb = nps(), nps()
                  for hf, ps in enumerate((pa, pb)):
                      for kc in range(8):
                          MM(ps, ps[:, :], MIX[:, kc, t * 128:(t + 1) * 128], wo[:, kc, hf * 512:(hf + 1) * 512], kc == 0, kc == 7, [MIX, WA])
                  f = rot(xf, "xf")
                  S.dma("sp", f[:], xsrc[t * 128:(t + 1) * 128, :], writes=[f])
                  z = rot(zt, "zt")
                  for hf, ps in enumerate((pa, pb)):
                      STT("dve", z[:, hf * 512:(hf + 1) * 512], f[:, hf * 512:(hf + 1) * 512], ALPHA, ps[:, :], ALU.mult, ALU.add, [f, ps], [z])
                  layernorm(z, 128, 0, z, z[:, :])
                  S.dma("sp", X1[t * 128:(t + 1) * 128, :], z[:, :], reads=[z], writes=[g.X1t])
                  b = rot(xb, "xb")
                  ACT(b[:], z[:], AF.Copy, [z], [b])
                  to_xT(b, XT, t)

              stop_if('P3')
              barrier()
              load_ln("ln2")
              acc = BUF[:, 0:8192].rearrange("p (t n) -> p t n", t=8)
              MIXf = MIX[:, :, :].rearrange("p k n -> p (k n)").bitcast(F32)
              g.gfull = [MIXf[:, 0:1026], MIXf[:, 1026:2052]]
              g.ftmp = [MIXf[:, 2052:2564], MIXf[:, 2564:3076]]
              g.hT = MIXf[:, 3076:4100].bitcast(BF16).rearrange("p (j n) -> p j n", j=2)
              g.halo = MIXf[:, 4100:4144].rearrange("p (c t) -> p c t", t=2)
              g.wdn = [WA[:, 0:2048].rearrange("p (j n) -> p j n", j=2), WA[:, 2048:4096].rearrange("p (j n) -> p j n", j=2)]
              g.MIXt, g.WAt, g.WBt, g.XTt, g.BUFb = MIX, WA, WB, XT, BUFb
              MS("pool", g.halo, 0.0, [MIX])
              xdst = X2 if L == 0 else O["y_p"]
              for half in range(2):
                  for t in range(8):
                      tt = half * 8 + t
                      S.dma("sp", acc[:, t, :], X1[tt * 128:(tt + 1) * 128, :], reads=[g.X1t], writes=[BUFb])
                  ffn(g, L, half, acc, cst)
                  for t in range(8):
                      tt = half * 8 + t
                      z = rot(zt, "zt")
                      ACT(z[:], acc[:, t, :], AF.Copy, [BUFb], [z])
                      layernorm(z, 128, 2, z, z[:, :])
                      S.dma("sp", xdst[tt * 128:(tt + 1) * 128, :], z[:, :], reads=[z], writes=[g.X2t])
              S.dma("sp", O["conv_p"][L].rearrange("p (c t) -> p c t", t=2), cst[:], reads=[cst])
              xsrc = X2
        except StopBuild:
            pass
        if flags["sample"]:
            g.XT, g.MIX, g.BUF, g.WA, g.WB, g.TAB, g.flags = XT, MIX, BUF, WA, WB, TAB, flags
            g.wdn = [WA[:, 0:2048].rearrange("p (j n) -> p j n", j=2), WA[:, 2048:4096].rearrange("p (j n) -> p j n", j=2)]
            sample_path(g)
        S.emit()
        print("SBUF KB/partition:", S.sb_bytes / 1024.0, "sems:", S.nsem, {k: len(E.prog) for k, E in S.eng.items()})
    return nc


def ffn(g, L, half, acc, cst):
    S, I = g.S, g.I
    MM, ACT, TT, TS, STT, CP, MS, rot, nps = g.MM, g.ACT, g.TT, g.TS, g.STT, g.CP, g.MS, g.rot, g.nps
    XT, MIX, WA, WB, BUFb = g.XTt, g.MIXt, g.WAt, g.WBt, g.BUFb
    tok0 = half * 1024
    for t in range(8):
        S.op("act", lambda e, o=acc[:, t, :]: e.mul(out=o, in_=o, mul=ALPHA), reads=[BUFb], writes=[BUFb])
    wup = I["w_up"][L].rearrange("(kc p) n -> p kc n", p=128)
    wdn = I["w_down"][L]
    for gi in range(11):
        w = WB[:, (gi % 2) * 4096:(gi % 2) * 4096 + 4096]
        wu = w[:, 0:2048].rearrange("p (k n) -> p k n", k=8)
        wg = w[:, 2048:4096].rearrange("p (k n) -> p k n", k=8)
        wt = g.wtile[gi % 2]
        if os.environ.get("DMA3D", "1") == "1":
            S.dma("pool", wu, wup[:, :, gi * 256:(gi + 1) * 256], writes=[wt])
            S.dma("pool", wg, wup[:, :, DFF + gi * 256:DFF + (gi + 1) * 256], writes=[wt])
        else:
            for kc in range(8):
                S.dma("pool", wu[:, kc, :], wup[:, kc, gi * 256:(gi + 1) * 256], writes=[wt])
                S.dma("pool", wg[:, kc, :], wup[:, kc, DFF + gi * 256:DFF + (gi + 1) * 256], writes=[wt])
        wd = g.wdn[gi % 2]
        wdt = g.wdtile[gi % 2]
        S.dma("pool", wd, wdn[gi * 256:(gi + 1) * 256, :].rearrange("(j p) n -> p j n", p=128), writes=[wdt])
        for j in range(2):
            fc = gi * 2 + j
            gf = g.gfull[j]
            gft = g.gftile[j]
            CP("pool", gf[:, 0:2], g.halo[:, fc, :], [MIX], [gft])
            for tb in range(2):
                bs = slice(tok0 + tb * 512, tok0 + (tb + 1) * 512)
                pu, pg = nps(), nps()
                for kc in range(8):
                    MM(pu, pu[:, :], wu[:, kc, j * 128:(j + 1) * 128], XT[:, kc, bs], kc == 0, kc == 7, [wt, XT])
                for kc in range(8):
                    MM(pg, pg[:, :], wg[:, kc, j * 128:(j + 1) * 128], XT[:, kc, bs], kc == 0, kc == 7, [wt, XT])
                o = 2 + tb * 512
                ACT(gf[:, o:o + 512], pg[:, :], AF.Copy, [pg], [gft])
                tm = g.ftmp[tb]
                tmt = g.fttile[tb]
                TS("dve", tm, gf[:, o:o + 512], g.cvw[:, fc, 2:3], g.cvb[:, fc:fc + 1], ALU.mult, ALU.add, [gft, g.cvw, g.cvb], [tmt])
                STT("dve", tm, gf[:, o - 1:o + 511], g.cvw[:, fc, 1:2], tm, ALU.mult, ALU.add, [gft, g.cvw, tmt], [tmt])
                STT("dve", tm, gf[:, o - 2:o + 510], g.cvw[:, fc, 0:1], tm, ALU.mult, ALU.add, [gft, g.cvw, tmt], [tmt])
                ACT(tm, tm, AF.Gelu, [tmt], [tmt])
                TT("dve", g.hT[:, j, tb * 512:(tb + 1) * 512], tm, pu[:, :], ALU.mult, [tmt, pu], [g.hTtile])
            CP("pool", g.halo[:, fc, :], gf[:, 1024:1026], [gft], [MIX])
            if half == 1:
                CP("pool", cst[:, fc, :], gf[:, 1024:1026], [gft], [cst])
        for t in range(8):
            pa, pb = nps(), nps()
            for hf, ps in enumerate((pa, pb)):
                for j in range(2):
                    MM(ps, ps[:, :], g.hT[:, j, t * 128:(t + 1) * 128], wd[:, j, hf * 512:(hf + 1) * 512], j == 0, j == 1, [g.hTtile, wdt])
                TT("dve", acc[:, t, hf * 512:(hf + 1) * 512], acc[:, t, hf * 512:(hf + 1) * 512], ps[:, :], ALU.add, [BUFb, ps], [BUFb])


def moba_inproj_tile(*a, **k):
    raise NotImplementedError("MoBA layer not implemented in this revision")


def moba_attention(*a, **k):
    raise NotImplementedError("MoBA layer not implemented in this revision")


NPG = 128
SPOS_SCALE_A = float(96 ** -0.5)


def declare_sample_io(g, I, O, din, dout):
    for name, shape, dt in [
        ("xs", [4, D], F32), ("ptab", [1, 4 * NPG], I32),
        ("p_ckv", [NPOOL * 128, 256], F32), ("p_kr", [NPOOL * 128, 32], F32),
        ("p_k", [NPOOL * 128, 256], F32), ("p_v", [NPOOL * 128, 256], F32),
        ("cmk", [2, 4, 256, 256], F32), ("cmv", [2, 4, 256, 256], F32),
        ("sconv", [2, 128, NFC * 8], F32), ("w_ukT", [64, 12 * 256], F32),
        ("scos32", [4, 32], F32), ("ssin32", [4, 32], F32), ("scos64", [4, 64], F32), ("ssin64", [4, 64], F32),
    ]:
        I[name] = din(name, shape, dt)
    for name, shape in [("y_s", [4, D]), ("ckv_s", [4, 256]), ("krope_s", [4, 32]), ("mobak_s", [4, 256]),
                        ("mobav_s", [4, 256]), ("conv_s", [2, 128, NFC * 8])]:
        O[name] = dout(name, shape)


def sample_path(g):
    S, I, O = g.S, g.I, g.O
    MM, TR, ACT, TT, TS, STT, CP, MS, RS, RCP = g.MM, g.TR, g.ACT, g.TT, g.TS, g.STT, g.CP, g.MS, g.RS, g.RCP
    rot, nps, npo, npt, barrier, wload = g.rot, g.nps, g.npo, g.npt, g.barrier, g.wload
    XT, MIX, BUF, WA, WB, TAB = g.XT, g.MIX, g.BUF, g.WA, g.WB, g.TAB
    st, sq, lnp, gqb, cvw, cvb, ident = g.st, g.sq, g.lnp, g.gqb, g.cvw, g.cvb, g.ident
    flags = g.flags
    barrier()

    def kcw(w2d):
        return w2d.rearrange("(kc p) n -> p kc n", p=128)

    XT2 = XT[:, :, :].rearrange("p k n -> p (k n)")
    MX2 = MIX[:, :, :].rearrange("p k n -> p (k n)")
    G = lambda base, nm: S.view(base, nm)
    xsT = XT2[:, 0:32].rearrange("p (k n) -> p k n", k=8); G_xsT = G(XT, "s_xsT")
    mixT = XT2[:, 32:64].rearrange("p (k n) -> p k n", k=8); G_mix = G(XT, "s_mix")
    x1T = XT2[:, 64:96].rearrange("p (k n) -> p k n", k=8); G_x1T = G(XT, "s_x1T")
    qnT = XT2[:, 96:108].rearrange("p (k n) -> p k n", k=3); G_qnT = G(XT, "s_qnT")
    qmTs = XT2[:, 108:116].rearrange("p (k n) -> p k n", k=2); G_qmT = G(XT, "s_qmT")
    qlT = XT2[:, 116:212].rearrange("p (c h n) -> p c h n", c=2, h=12); G_qlT = G(XT, "s_qlT")
    qrT = XT2[:, 212:260].rearrange("p (h n) -> p h n", h=12); G_qrT = G(XT, "s_qrT")
    qhT = XT2[:, 260:308].rearrange("p (h n) -> p h n", h=12); G_qhT = G(XT, "s_qhT")
    olT = XT2[:, 308:404].rearrange("p (c h n) -> p c h n", c=2, h=12); G_olT = G(XT, "s_olT")
    qTs = XT2[:, 404:428].rearrange("p (c g n) -> p c g n", c=2, g=3); G_qTs = G(XT, "s_qTs")
    sbs = XT2[:, 512:2560]; G_sbs = G(XT, "s_sbs")
    hTs = XT2[:, 2560:2568].rearrange("p (j n) -> p j n", j=2); G_hTs = G(XT, "s_hTs")
    pTs = [XT2[:, 2576 + i * 16:2576 + i * 16 + 12] for i in range(4)]; G_pTs = [G(XT, "s_pT%d" % i) for i in range(4)]
    cT = [XT2[:, 4096 + i * 512:4096 + i * 512 + 384] for i in range(2)]; G_cT = [G(XT, "s_cT%d" % i) for i in range(2)]
    PTall = XT2[:, 6144:6144 + 1548].rearrange("p (g h) -> p g h", h=12); G_PT = G(XT, "s_PT")
    Yb = XT2[:, 8192:8960]; G_Yb = G(XT, "s_Yb")
    ones1 = XT2[:, 8960:9088]; G_on = G(XT, "s_ones")
    NB = 4
    cpg = [MX2[:, i * 264:i * 264 + 257] for i in range(NB)]; G_cpg = [G(MIX, "s_cpg%d" % i) for i in range(NB)]
    krp = [MX2[:, 1056 + i * 32:1056 + (i + 1) * 32] for i in range(NB)]; G_krp = [G(MIX, "s_krp%d" % i) for i in range(NB)]
    vpg = [MX2[:, 2048 + i * 264:2048 + i * 264 + 260].rearrange("p (k n) -> p k n", k=4) for i in range(NB)]
    G_vpg = [G(MIX, "s_vpg%d" % i) for i in range(NB)]
    lpg = MX2[:, 4096:4096 + 264]; G_lpg = G(MIX, "s_lpg")
    lkr = MX2[:, 4360:4392]
    lvp = MX2[:, 4400:4660].rearrange("p (k n) -> p k n", k=4)
    wukT = MX2[:, 8192:8192 + 3072].rearrange("p (h r) -> p h r", h=12); G_wukT = G(MIX, "s_wukT")
    xcur = BUF[:, 0:1024]; G_xcur = G(BUF, "s_xcur")
    zs = BUF[:, 1024:2048]; G_zs = G(BUF, "s_zs")
    accs = BUF[:, 2048:3072]; G_acc = G(BUF, "s_acc")
    sgs = BUF[:, 3072:4608]; G_sgs = G(BUF, "s_sgs")
    Sall = BUF[:, 4608:4608 + 1548].rearrange("p (g h) -> p g h", h=12); G_Sall = G(BUF, "s_Sall")
    biasB = BUF[:, 6400:7168]; G_bB = G(BUF, "s_biasB")
    gts = BUF[:, 7168:7232]; top8s = BUF[:, 7232:7240]; bsel = BUF[:, 7240:7304]; G_gt = G(BUF, "s_gt")
    olat = BUF[:, 7424:7424 + 264]; G_olat = G(BUF, "s_olat")
    gst = BUF[:, 7700:7700 + 3 * 88].rearrange("p (a n) -> p a n", a=3); G_gst = G(BUF, "s_gst")
    sct = BUF[:, 8192:8192 + NFC * 8].rearrange("p (c s t) -> p c s t", c=NFC, s=4); G_sct = G(BUF, "s_sct")
    cso = BUF[:, 8400:8400 + NFC * 8].rearrange("p (c s t) -> p c s t", c=NFC, s=4); G_cso = G(BUF, "s_cso")
    rmask = BUF[:, 8600:8601]; G_rm = G(BUF, "s_rmask")
    io = TAB[:, 0:1].bitcast(I32); ptb = TAB[:, 16:16 + 512].bitcast(I32); idx = TAB[:, 544:544 + 512].bitcast(I32)
    G_idx = G(TAB, "s_idx")
    stab = TAB[:, 1100:1100 + 192]; G_stab = G(TAB, "s_stab")
    identf = TAB[0:64, 2048:2112]; G_idf = G(TAB, "s_identf")
    S.dma("sp", identf, I["ident"][0:64, 0:64], writes=[G_idf])

    S.op("pool", lambda e: e.iota(io, pattern=[[1, 1]], base=0, channel_multiplier=1), writes=[G_idx])
    S.dma("sp", ptb, I["ptab"].partition_broadcast(128)[:, 0, :], writes=[G_idx])
    TS("dve", idx, ptb, 128, io[:, 0:1], ALU.mult, ALU.add, [G_idx], [G_idx])
    S.dma("sp", stab[0:4, 0:32], I["scos32"][:, :], writes=[G_stab])
    S.dma("sp", stab[0:4, 32:64], I["ssin32"][:, :], writes=[G_stab])
    S.dma("sp", stab[0:4, 64:128], I["scos64"][:, :], writes=[G_stab])
    S.dma("sp", stab[0:4, 128:192], I["ssin64"][:, :], writes=[G_stab])
    MS("pool", ones1, 1.0, [G_on])
    maskk = BUF[:, 8610:8614]; G_mk = G(BUF, "s_maskk")
    for kvh in range(4):
        S.op("dve", lambda e, o=maskk[0:12, kvh:kvh + 1], i=ident[0:12, kvh * 3:(kvh + 1) * 3]: e.reduce_sum(out=o, in_=i, axis=AX.X),
             reads=[ident], writes=[G_mk])
    MS("pool", rmask, 0.0, [G_rm])
    MS("pool", rmask[0:1, :], 1.0, [G_rm])
    for i in range(NB):
        MS("pool", cpg[i][:, 256:257], 1.0, [G_cpg[i]])
        MS("pool", vpg[i][:, :, 64:65], 1.0, [G_vpg[i]])
    MS("pool", lpg, 0.0, [G_lpg])
    MS("pool", lpg[:, 256:257], 1.0, [G_lpg])
    MS("pool", lkr, 0.0, [G_lpg])
    MS("pool", lvp, 0.0, [G_lpg])
    MS("pool", lvp[:, :, 64:65], 1.0, [G_lpg])
    S.dma("sp", xcur[0:4, :], I["xs"][:, :], writes=[G_xcur])
    S.dma("pool", wukT[0:64, :, :], I["w_ukT"].rearrange("p (h r) -> p h r", h=12), writes=[G_wukT])

    def tok_T(src_tile, src_ap, ncols_blocks, dst_ap, dst_tile, rows=128):
        pt_t = npt()
        nb = len(ncols_blocks)
        for i, (c0, w) in enumerate(ncols_blocks):
            TR(pt_t, pt_t[0:w, i * 4:(i + 1) * 4], src_ap[0:4, c0:c0 + w], [src_tile], npart=4)
        CP("dve", dst_ap, pt_t[0:rows, 0:nb * 4].rearrange("p (k n) -> p k n", k=nb), [pt_t], [dst_tile])

    def x_to_T(xf32, x_tile, dstT, dst_tile):
        ACT(sbs[0:4, 0:1024], xf32[0:4, :], AF.Copy, [x_tile], [G_sbs])
        tok_T(G_sbs, sbs, [(k * 128, 128) for k in range(8)], dstT, dst_tile)

    def proj_s(win, wtile, c0, c1):
        ps = nps()
        for kc in range(8):
            MM(ps, ps[0:4, 0:c1 - c0], xsT[:, kc, :], win[:, kc, c0:c1], kc == 0, kc == 7, [G_xsT, wtile])
        return ps

    def rope_s(src, nh, dh, ctab, stab_, dst, src_tile, dst_tile):
        hd = dh // 2
        a = sq[0:4, 0:nh * dh].rearrange("p (h d) -> p h d", h=nh)
        b2 = sq[0:4, 768:768 + nh * dh].rearrange("p (h d) -> p h d", h=nh) if nh * dh <= 256 else sq[0:4, 512:512 + nh * dh].rearrange("p (h d) -> p h d", h=nh)
        cB = ctab.to_broadcast([4, nh, dh]) if nh > 1 else ctab
        sB = stab_.to_broadcast([4, nh, dh]) if nh > 1 else stab_
        TT("dve", a, src, cB, ALU.mult, [src_tile, G_stab], [sq])
        TT("dve", b2[:, :, 0:hd], src[:, :, hd:dh], sB[:, :, 0:hd], ALU.mult, [src_tile, G_stab], [sq])
        TT("dve", b2[:, :, hd:dh], src[:, :, 0:hd], sB[:, :, hd:dh], ALU.mult, [src_tile, G_stab], [sq])
        TT("dve", dst, a, b2, ALU.add, [sq], [dst_tile])

    def gather(pool_ap, dst_ap, dst_tile, s, p):
        col = s * NPG + p
        S.dma("pool", None, None, reads=[G_idx], writes=[dst_tile],
              fn=lambda e, o=dst_ap, c=col, src=pool_ap: e.indirect_dma_start(
                  out=o, out_offset=None, in_=src[:, :], in_offset=bass.IndirectOffsetOnAxis(ap=idx[:, c:c + 1], axis=0)))

    def mem_attn_s(L):
        mkT_ap, vmem_ap = g.mkT_sb[:, :, :], g.vmem_sb[:, :, :, :]
        for s in range(4):
            MS("pool", vmem_ap, 1.0, [g.vmem_sb])
            for mt in range(2):
                f = rot(g.xf, "xf")
                S.dma("sp", f[:, 0:256], I["cmk"][L, s, mt * 128:(mt + 1) * 128, :], writes=[f])
                S.dma("sp", f[:, 256:512], I["cmv"][L, s, mt * 128:(mt + 1) * 128, :], writes=[f])
                b = rot(g.xb, "xb")
                ACT(b[:, 0:256], f[:, 0:256], AF.Copy, [f], [b])
                pt_t = npt()
                for pr in range(2):
                    TR(pt_t, pt_t[:, pr * 128:(pr + 1) * 128], b[:, pr * 128:(pr + 1) * 128], [b])
                CP("dve", mkT_ap[:, :, mt * 128:(mt + 1) * 128], pt_t[:, 0:256].rearrange("p (k n) -> p k n", k=2), [pt_t], [g.mkT_sb])
                for j in range(4):
                    c0 = 0 if j % 2 == 0 else 64
                    ACT(vmem_ap[:, mt, j, c0:c0 + 64], f[:, 256 + j * 64:256 + (j + 1) * 64], AF.Copy, [f], [g.vmem_sb])
            for j in range(4):
                p0 = (j % 2) * 64
                num0 = 64 if j % 2 == 1 else 0
                den0 = 64 - num0
                po = npo()
                for mt in range(2):
                    ps = nps()
                    MM(ps, ps[:, 0:1], mkT_ap[p0:p0 + 64, j // 2, mt * 128:(mt + 1) * 128], qmTs[p0:p0 + 64, j // 2, s:s + 1], True, True,
                       [g.mkT_sb, G_qmT])
                    p = rot(g.pT, "pT")
                    ACT(p[:, 0:1], ps[:, 0:1], AF.Exp, [ps], [p], scale=0.125)
                    MM(po, po[:, 0:1], vmem_ap[:, mt, j, :], p[:, 0:1], mt == 0, mt == 1, [g.vmem_sb, p])
                r = rot(g.rc, "rc")
                RCP(r[den0:den0 + 64, 0:1], po[den0:den0 + 64, 0:1], [po], [r])
                TT("dve", mixT[num0:num0 + 64, 6 + j // 2, s:s + 1], po[num0:num0 + 64, 0:1], r[den0:den0 + 64, 0:1], ALU.mult, [po, r], [G_mix])

    def post_attention(L, wo, wotile):
        for hf in range(2):
            ps = nps()
            for kc in range(8):
                MM(ps, ps[0:4, :], mixT[:, kc, :], wo[:, kc, hf * 512:(hf + 1) * 512], kc == 0, kc == 7, [G_mix, wotile])
            STT("dve", zs[0:4, hf * 512:(hf + 1) * 512], xcur[0:4, hf * 512:(hf + 1) * 512], ALPHA, ps[0:4, :], ALU.mult, ALU.add, [G_xcur, ps], [G_zs])
        for i, nm in enumerate(["ln1_g", "ln1_b"]):
            S.dma("sp", lnp[:, i, :], I[nm][L:L + 1, :].partition_broadcast(128)[:, 0, :], writes=[lnp])
        g.layernorm(G_zs_t, 4, 0, G_zs_t, zs[0:4, :])
        x_to_T(zs, G_zs, x1T, G_x1T)
        S.op("act", lambda e: e.mul(out=accs[0:4, :], in_=zs[0:4, :], mul=ALPHA), reads=[G_zs], writes=[G_acc])
        S.dma("sp", sct, I["sconv"][L].rearrange("p (c s t) -> p c s t", c=NFC, s=4), writes=[G_sct])
        S.dma("sp", cvw[:], I["convw"][L].rearrange("p (c t) -> p c t", t=3), writes=[cvw])
        S.dma("sp", cvb[:], I["convb"][L], writes=[cvb])
        wup = I["w_up"][L].rearrange("(kc p) n -> p kc n", p=128)
        wdn = I["w_down"][L]
        for gi in range(11):
            w = WB[:, (gi % 2) * 4096:(gi % 2) * 4096 + 4096]
            wu = w[:, 0:2048].rearrange("p (k n) -> p k n", k=8)
            wg = w[:, 2048:4096].rearrange("p (k n) -> p k n", k=8)
            wt = g.wtile[gi % 2]
            for kc in range(8):
                S.dma("pool", wu[:, kc, :], wup[:, kc, gi * 256:(gi + 1) * 256], writes=[wt])
                S.dma("pool", wg[:, kc, :], wup[:, kc, DFF + gi * 256:DFF + (gi + 1) * 256], writes=[wt])
            wd = g.wdn[gi % 2]
            wdt = g.wdtile[gi % 2]
            for j in range(2):
                S.dma("pool", wd[:, j, :], wdn[gi * 256 + j * 128:gi * 256 + (j + 1) * 128, :], writes=[wdt])
            for j in range(2):
                fc = gi * 2 + j
                pu, pg = nps(), nps()
                for kc in range(8):
                    MM(pu, pu[:, 0:4], wu[:, kc, j * 128:(j + 1) * 128], x1T[:, kc, :], kc == 0, kc == 7, [wt, G_x1T])
                for kc in range(8):
                    MM(pg, pg[:, 0:4], wg[:, kc, j * 128:(j + 1) * 128], x1T[:, kc, :], kc == 0, kc == 7, [wt, G_x1T])
                gn = gst[:, 0, 0:4]
                tm = gst[:, 1, 0:4]
                ACT(gn, pg[:, 0:4], AF.Copy, [pg], [G_gst])
                TS("dve", tm, gn, cvw[:, fc, 2:3], cvb[:, fc:fc + 1], ALU.mult, ALU.add, [G_gst, cvw, cvb], [G_gst])
                STT("dve", tm, sct[:, fc, :, 1], cvw[:, fc, 1:2], tm, ALU.mult, ALU.add, [G_sct, cvw, G_gst], [G_gst])
                STT("dve", tm, sct[:, fc, :, 0], cvw[:, fc, 0:1], tm, ALU.mult, ALU.add, [G_sct, cvw, G_gst], [G_gst])
                ACT(tm, tm, AF.Gelu, [G_gst], [G_gst])
                TT("dve", hTs[:, j, :], tm, pu[:, 0:4], ALU.mult, [G_gst, pu], [G_hTs])
                CP("pool", cso[:, fc, :, 0], sct[:, fc, :, 1], [G_sct], [G_cso])
                CP("pool", cso[:, fc, :, 1], gn, [G_gst], [G_cso])
            for hf in range(2):
                ps = nps()
                for j in range(2):
                    MM(ps, ps[0:4, :], hTs[:, j, :], wd[:, j, hf * 512:(hf + 1) * 512], j == 0, j == 1, [G_hTs, wdt])
                TT("dve", accs[0:4, hf * 512:(hf + 1) * 512], accs[0:4, hf * 512:(hf + 1) * 512], ps[0:4, :], ALU.add, [G_acc, ps], [G_acc])
        S.dma("sp", O["conv_s"][L].rearrange("p (c s t) -> p c s t", c=NFC, s=4), cso, reads=[G_cso])
        for i, nm in enumerate(["ln2_g", "ln2_b"]):
            S.dma("sp", lnp[:, i, :], I[nm][L:L + 1, :].partition_broadcast(128)[:, 0, :], writes=[lnp])
        g.layernorm(G_acc_t, 4, 2, G_xcur, xcur[0:4, :])

    G_zs_t = _TileAP(G_zs, zs)
    G_acc_t = _TileAP(G_acc, accs)

    L = 0
    win = WA[:, 0:8 * 928].rearrange("p (k n) -> p k n", k=8)
    wload(WA, win, kcw(I["w_in_a"]))
    wq = WB[:, 0:3456].rearrange("p (k n) -> p k n", k=3)
    wuv = WB[:, 8448:9984].rearrange("p (k n) -> p k n", k=2)
    wload(WB, wq, kcw(I["w_q_b"]))
    wload(WB, wuv, kcw(I["w_uv"]))
    S.dma("sp", gqb[:, 0:384], I["g_q"].partition_broadcast(128)[:, 0, :], writes=[gqb])
    S.dma("sp", gqb[:, 384:640], I["g_kv"].partition_broadcast(128)[:, 0, :], writes=[gqb])
    x_to_T(xcur, G_xcur, xsT, G_xsT)
    psA = proj_s(win, WA, 0, 384)
    r = g.rstd_of(psA[0:4, 0:384], 384, 4, RMS_EPS, [psA])
    STT("dve", sbs[0:4, 0:384], psA[0:4, 0:384], r, gqb[0:4, 0:384], ALU.mult, ALU.mult, [psA, st, gqb], [G_sbs])
    tok_T(G_sbs, sbs, [(k * 128, 128) for k in range(3)], qnT, G_qnT)
    psB = proj_s(win, WA, 384, 672)
    r = g.rstd_of(psB[0:4, 0:256], 256, 4, RMS_EPS, [psB])
    STT("dve", sgs[0:4, 0:256], psB[0:4, 0:256], r, gqb[0:4, 384:640], ALU.mult, ALU.mult, [psB, st, gqb], [G_sgs])
    rope_s(psB[0:4, 256:288].rearrange("p (h d) -> p h d", h=1), 1, 32, stab[0:4, 0:32].rearrange("p (h d) -> p h d", h=1),
           stab[0:4, 32:64].rearrange("p (h d) -> p h d", h=1), sgs[0:4, 256:288].rearrange("p (h d) -> p h d", h=1), psB, G_sgs)
    S.dma("sp", O["ckv_s"][:, :], sgs[0:4, 0:256], reads=[G_sgs])
    S.dma("sp", O["krope_s"][:, :], sgs[0:4, 256:288], reads=[G_sgs])
    ACT(sbs[0:4, 1152:1440], sgs[0:4, 0:288], AF.Copy, [G_sgs], [G_sbs])
    psC = proj_s(win, WA, 672, 928)
    ACT(sbs[0:4, 1536:1792], psC[0:4, 0:256], AF.Copy, [psC], [G_sbs])
    tok_T(G_sbs, sbs, [(1536, 128), (1664, 128)], qmTs, G_qmT)
    for grp in range(3):
        ps = nps()
        for kc in range(3):
            MM(ps, ps[0:4, 0:384], qnT[:, kc, :], wq[:, kc, grp * 384:(grp + 1) * 384], kc == 0, kc == 2, [G_qnT, WB])
        q3 = ps[0:4, 0:384].rearrange("p (h d) -> p h d", h=4)
        dst = sgs[0:4, 384 + grp * 384:384 + (grp + 1) * 384].rearrange("p (h d) -> p h d", h=4)
        CP("dve", dst[:, :, 0:64], q3[:, :, 0:64], [ps], [G_sgs])
        rope_s(q3[:, :, 64:96], 4, 32, stab[0:4, 0:32].rearrange("p (h d) -> p h d", h=1), stab[0:4, 32:64].rearrange("p (h d) -> p h d", h=1),
               dst[:, :, 64:96], ps, G_sgs)
    ACT(sbs[0:4, 0:1152], sgs[0:4, 384:1536], AF.Copy, [G_sgs], [G_sbs])
    q12 = sbs[0:4, 0:1152].rearrange("p (h d) -> p h d", h=12)
    pt_t = npt()
    for h in range(12):
        TR(pt_t, pt_t[0:64, h * 4:(h + 1) * 4], q12[:, h, 0:64], [G_sbs], npart=4)
        TR(pt_t, pt_t[0:32, 64 + h * 4:64 + (h + 1) * 4], q12[:, h, 64:96], [G_sbs], npart=4)
    CP("dve", qhT[0:64, :, :], pt_t[0:64, 0:48].rearrange("p (h n) -> p h n", h=12), [pt_t], [G_qhT])
    CP("dve", qrT[0:32, :, :], pt_t[0:32, 64:112].rearrange("p (h n) -> p h n", h=12), [pt_t], [G_qrT])
    for c in range(2):
        ps = nps()
        for h in range(12):
            MM(ps, ps[:, h * 4:(h + 1) * 4], wukT[0:64, h, c * 128:(c + 1) * 128], qhT[0:64, h, :], True, True, [G_wukT, G_qhT])
        ACT(qlT[:, c, :, :], ps[:, 0:48].rearrange("p (h n) -> p h n", h=12), AF.Copy, [ps], [G_qlT])
    nbuf = [0]
    for s in range(4):
        po = npo()
        S.dma("sp", lpg[0:1, 0:256], sbs[s:s + 1, 1152:1408], reads=[G_sbs], writes=[G_lpg])
        S.dma("sp", lkr[0:1, 0:32], sbs[s:s + 1, 1408:1440], reads=[G_sbs], writes=[G_lpg])
        for p in range(NPG + 1):
            local = (p == NPG)
            bi = nbuf[0] % NB
            nbuf[0] += 1
            if local:
                cp_ap, cp_t, kr_ap, kr_t = lpg, G_lpg, lkr, G_lpg
            else:
                cp_ap, cp_t, kr_ap, kr_t = cpg[bi], G_cpg[bi], krp[bi], G_krp[bi]
                gather(I["p_ckv"], cp_ap[:, 0:256], cp_t, s, p)
                gather(I["p_kr"], kr_ap, kr_t, s, p)
            pt_t = npt()
            TR(pt_t, pt_t[:, 0:128], cp_ap[:, 0:128], [cp_t])
            TR(pt_t, pt_t[:, 128:256], cp_ap[:, 128:256], [cp_t])
            TR(pt_t, pt_t[0:32, 256:384], kr_ap, [kr_t])
            ci = p % 2
            CP("dve", cT[ci][:, 0:256], pt_t[:, 0:256], [pt_t], [G_cT[ci]])
            ACT(cT[ci][0:32, 256:384], pt_t[0:32, 256:384], AF.Copy, [pt_t], [G_cT[ci]])
            ps = nps()
            MM(ps, ps[:, 0:12], cT[ci][:, 0:128], qlT[:, 0, :, s], True, False, [G_cT[ci], G_qlT])
            MM(ps, ps[:, 0:12], cT[ci][:, 128:256], qlT[:, 1, :, s], False, False, [G_cT[ci], G_qlT])
            MM(ps, ps[:, 0:12], cT[ci][0:32, 256:384], qrT[0:32, :, s], False, True, [G_cT[ci], G_qrT])
            pi = p % 4
            ACT(pTs[pi], ps[:, 0:12], AF.Exp, [ps], [G_pTs[pi]], scale=SPOS_SCALE_A)
            if local:
                TS("dve", pTs[pi], pTs[pi], rmask[:, 0:1], None, ALU.mult, None, [G_pTs[pi], G_rm], [G_pTs[pi]])
            MM(po, po[0:12, 0:257], pTs[pi], cp_ap[:, 0:257], p == 0, local, [G_pTs[pi], cp_t])
        RCP(olat[0:12, 256:257], po[0:12, 256:257], [po], [G_olat])
        TS("dve", olat[0:12, 0:256], po[0:12, 0:256], olat[0:12, 256:257], None, ALU.mult, None, [po, G_olat], [G_olat])
        ACT(Yb[0:12, 0:256], olat[0:12, 0:256], AF.Copy, [G_olat], [G_Yb])
        pt_t = npt()
        for c in range(2):
            TR(pt_t, pt_t[:, c * 12:(c + 1) * 12], Yb[0:12, c * 128:(c + 1) * 128], [G_Yb], npart=12)
        CP("dve", olT[:, :, :, s], pt_t[:, 0:24].rearrange("p (c h) -> p c h", c=2), [pt_t], [G_olT])
    for pr in range(6):
        for hh in range(2):
            h = pr * 2 + hh
            ps = nps()
            for c in range(2):
                MM(ps, ps[:, 0:4], wuv[:, c, pr * 128:(pr + 1) * 128], olT[:, c, h, :], c == 0, c == 1, [WB, G_olT])
            r0 = hh * 64
            ACT(mixT[r0:r0 + 64, pr, :], ps[r0:r0 + 64, 0:4], AF.Copy, [ps], [G_mix])
    mem_attn_s(0)
    wo = WA[:, 0:8192].rearrange("p (k n) -> p k n", k=8)
    wload(WA, wo, kcw(I["w_o"][0]))
    post_attention(0, wo, WA)
    if flags["layers"] < 2:
        return
    L = 1
    win = WA[:, 0:8 * 1536].rearrange("p (k n) -> p k n", k=8)
    wload(WA, win, kcw(I["w_in_b"]))
    x_to_T(xcur, G_xcur, xsT, G_xsT)
    cos64 = stab[0:4, 64:128].rearrange("p (h d) -> p h d", h=1)
    sin64 = stab[0:4, 128:192].rearrange("p (h d) -> p h d", h=1)
    for qh in range(2):
        ps = proj_s(win, WA, qh * 384, (qh + 1) * 384)
        rope_s(ps[0:4, 0:384].rearrange("p (h d) -> p h d", h=6), 6, 64, cos64, sin64,
               sgs[0:4, qh * 384:(qh + 1) * 384].rearrange("p (h d) -> p h d", h=6), ps, G_sgs)
    pskv = proj_s(win, WA, 768, 1280)
    rope_s(pskv[0:4, 0:256].rearrange("p (h d) -> p h d", h=4), 4, 64, cos64, sin64,
           sgs[0:4, 768:1024].rearrange("p (h d) -> p h d", h=4), pskv, G_sgs)
    CP("dve", sgs[0:4, 1024:1280], pskv[0:4, 256:512], [pskv], [G_sgs])
    S.dma("sp", O["mobak_s"][:, :], sgs[0:4, 768:1024], reads=[G_sgs])
    S.dma("sp", O["mobav_s"][:, :], sgs[0:4, 1024:1280], reads=[G_sgs])
    ACT(sbs[0:4, 0:1280], sgs[0:4, 0:1280], AF.Copy, [G_sgs], [G_sbs])
    psm = proj_s(win, WA, 1280, 1536)
    ACT(sbs[0:4, 1280:1536], psm[0:4, 0:256], AF.Copy, [psm], [G_sbs])
    tok_T(G_sbs, sbs, [(1280, 128), (1408, 128)], qmTs, G_qmT)
    q12 = sbs[0:4, 0:768].rearrange("p (h d) -> p h d", h=12)
    pt_t = npt()
    for h in range(12):
        TR(pt_t, pt_t[0:64, h * 4:(h + 1) * 4], q12[:, h, :], [G_sbs], npart=4)
    CP("dve", qhT[0:64, :, :], pt_t[0:64, 0:48].rearrange("p (h n) -> p h n", h=12), [pt_t], [G_qhT])
    qg = sgs[0:4, 0:768].rearrange("p (c g k d) -> p c g k d", c=2, g=3, k=2)
    for c in range(2):
        for gg in range(3):
            for kk in range(2):
                h = (2 * c + kk) * 3 + gg
                CP("dve", qg[:, c, gg, kk, :], q12[:, h, :], [G_sbs], [G_sgs])
    qgb = g.sgb[0]
    ACT(qgb[0:4, 0:768], sgs[0:4, 0:768], AF.Copy, [G_sgs], [qgb])
    tok_T(qgb, qgb, [(i * 128, 128) for i in range(6)], qTs[:, :, :, :].rearrange("p c g n -> p (c g) n"), G_qTs)
    for s in range(4):
        S.dma("sp", lpg[0:1, 0:256], sbs[s:s + 1, 768:1024], reads=[G_sbs], writes=[G_lpg])
        S.dma("sp", lvp[0:1, :, 0:64], sbs[s:s + 1, 1024:1280].rearrange("p (k d) -> p k d", k=4), reads=[G_sbs], writes=[G_lpg])
        pkm = npo()
        for p in range(NPG + 1):
            local = (p == NPG)
            bi = nbuf[0] % NB
            nbuf[0] += 1
            if local:
                kp_ap, kp_t = lpg, G_lpg
            else:
                kp_ap, kp_t = cpg[bi], G_cpg[bi]
                gather(I["p_k"], kp_ap[:, 0:256], kp_t, s, p)
                for kvh in range(4):
                    MM(pkm, pkm[0:64, kvh * 64 + p // 2:kvh * 64 + p // 2 + 1], kp_ap[:, kvh * 64:(kvh + 1) * 64], ones1[:, 0:1],
                       p == 0 and kvh == 0, p == NPG - 1 and kvh == 3, [kp_t, G_on])
            pt_t = npt()
            TR(pt_t, pt_t[:, 0:128], kp_ap[:, 0:128], [kp_t])
            TR(pt_t, pt_t[:, 128:256], kp_ap[:, 128:256], [kp_t])
            ci = p % 2
            CP("dve", cT[ci][:, 0:256], pt_t[:, 0:256], [pt_t], [G_cT[ci]])
            ps = nps()
            for kvh in range(4):
                r0 = (kvh % 2) * 64
                MM(ps, ps[:, kvh * 3:(kvh + 1) * 3], cT[ci][r0:r0 + 64, (kvh // 2) * 128:(kvh // 2) * 128 + 128],
                   qTs[r0:r0 + 64, kvh // 2, :, s], True, True, [G_cT[ci], G_qTs])
            CP("dve", Sall[:, p, :], ps[:, 0:12], [ps], [G_Sall])
        kms = cT[0]
        ACT(kms[0:64, 0:256], pkm[0:64, 0:256], AF.Copy, [pkm], [G_cT[0]])
        psg = nps()
        for kvh in range(4):
            MM(psg, psg[0:12, kvh * 64:(kvh + 1) * 64], qhT[0:64, :, s], kms[0:64, kvh * 64:(kvh + 1) * 64], True, True,
               [G_qhT, G_cT[0]])
        TS("dve", gts[0:12, :], psg[0:12, 0:64], maskk[0:12, 0:1], None, ALU.mult, None, [psg, G_mk], [G_gt])
        for kvh in range(1, 4):
            STT("dve", gts[0:12, :], psg[0:12, kvh * 64:(kvh + 1) * 64], maskk[0:12, kvh:kvh + 1], gts[0:12, :], ALU.mult, ALU.add,
                [psg, G_mk, G_gt], [G_gt])
        S.op("dve", lambda e: e.max(out=top8s[0:12, :], in_=gts[0:12, :]), reads=[G_gt], writes=[G_gt])
        TS("dve", bsel[0:12, :], gts[0:12, :], top8s[0:12, 2:3], None, ALU.is_ge, None, [G_gt], [G_gt])
        TS("dve", bsel[0:12, :], bsel[0:12, :], -1.0, BIG, ALU.add, ALU.mult, [G_gt], [G_gt])
        TT("dve", Yb[0:12, 0:768].rearrange("p (b h) -> p b h", h=12), bsel[0:12, :].rearrange("p (b o) -> p b o", o=1).to_broadcast([12, 64, 12]),
           ident[0:12, 0:12].rearrange("p (o h) -> p o h", o=1).to_broadcast([12, 64, 12]), ALU.mult, [G_gt, ident], [G_Yb])
        for hf in range(2):
            psb = nps()
            MM(psb, psb[:, 0:384], ones1[0:12, 0:128], Yb[0:12, hf * 384:(hf + 1) * 384], True, True, [G_on, G_Yb])
            CP("dve", biasB[:, hf * 384:(hf + 1) * 384], psb[:, 0:384], [psb], [G_bB])
        S4 = BUF[:, 4608:4608 + 1536].rearrange("p (b t h) -> p b t h", b=64, t=2)
        TT("dve", S4, S4, biasB[:, 0:768].rearrange("p (b o h) -> p b o h", b=64, o=1).to_broadcast([128, 64, 2, 12]), ALU.add,
           [G_Sall, G_bB], [G_Sall])
        ACT(PTall, Sall, AF.Exp, [G_Sall], [G_PT], scale=0.125)
        TS("dve", PTall[:, NPG, :], PTall[:, NPG, :], rmask[:, 0:1], None, ALU.mult, None, [G_PT, G_rm], [G_PT])
        po = npo()
        for p in range(NPG + 1):
            local = (p == NPG)
            bi = nbuf[0] % NB
            nbuf[0] += 1
            if local:
                vp_ap, vp_t = lvp, G_lpg
            else:
                vp_ap, vp_t = vpg[bi], G_vpg[bi]
                gather(I["p_v"], cpg[bi][:, 0:256], G_cpg[bi], s, p)
                CP("pool", vp_ap[:, :, 0:64], cpg[bi][:, 0:256].rearrange("p (k d) -> p k d", k=4), [G_cpg[bi]], [vp_t])
            for kvh in range(4):
                MM(po, po[0:3, kvh * 65:(kvh + 1) * 65], PTall[:, p, kvh * 3:(kvh + 1) * 3], vp_ap[:, kvh, :], p == 0 and kvh == 0, local and kvh == 3, [G_PT, vp_t])
        o3 = olat[0:3, 0:260].rearrange("p (k n) -> p k n", k=4)
        RCP(o3[:, :, 64:65], po[0:3, 0:260].rearrange("p (k n) -> p k n", k=4)[:, :, 64:65], [po], [G_olat])
        TT("dve", o3[:, :, 0:64], po[0:3, 0:260].rearrange("p (k n) -> p k n", k=4)[:, :, 0:64], o3[:, :, 64:65].to_broadcast([3, 4, 64]), ALU.mult,
           [po, G_olat], [G_olat])
        ACT(Yb[0:3, 0:256].rearrange("p (k n) -> p k n", k=4), o3[:, :, 0:64], AF.Copy, [G_olat], [G_Yb])
        pt_t = npt()
        for kvh in range(4):
            TR(pt_t, pt_t[0:64, kvh * 4:kvh * 4 + 3], Yb[0:3, kvh * 64:(kvh + 1) * 64], [G_Yb], npart=3)
        for h in range(12):
            r0 = (h % 2) * 64
            cc = (h // 3) * 4 + (h % 3)
            ACT(mixT[r0:r0 + 64, h // 2, s:s + 1], pt_t[0:64, cc:cc + 1], AF.Copy, [pt_t], [G_mix])
    mem_attn_s(1)
    wo = WA[:, 0:8192].rearrange("p (k n) -> p k n", k=8)
    wload(WA, wo, kcw(I["w_o"][1]))
    post_attention(1, wo, WA)
    S.dma("sp", O["y_s"][:, :], xcur[0:4, :], reads=[G_xcur])


class _TileAP:
    def __init__(self, tile, ap):
        self.tile, self.ap = tile, ap

    def __getitem__(self, k):
        return self.ap[k]

    lw = property(lambda self: self.tile.lw, lambda self, v: setattr(self.tile, "lw", v))
    readers = property(lambda self: self.tile.readers, lambda self, v: setattr(self.tile, "readers", v))
    excl = property(lambda self: self.tile.excl)
    dsem = property(lambda self: self.tile.dsem, lambda self, v: setattr(self.tile, "dsem", v))
    dcount = property(lambda self: self.tile.dcount, lambda self, v: setattr(self.tile, "dcount", v))
    name = property(lambda self: self.tile.name)


def _consts():
    c = {}
    c["ident"] = np.eye(128, dtype=np.float32)
    k = np.arange(128)[:, None, None]
    j = np.arange(4)[None, :, None]
    q = np.arange(512)[None, None, :]
    c["masks"] = (j * 128 + k <= q).astype(np.float32).reshape(128, 2048)
    pos = np.arange(SEQ, dtype=np.float32)

    def tabs(d, p):
        inv = (np.float32(10000.0) ** (-np.arange(0, d, 2, dtype=np.float32) / np.float32(d))).astype(np.float32)
        ang = p[:, None].astype(np.float32) * inv[None, :]
        return np.cos(ang).astype(np.float32), np.sin(ang).astype(np.float32)
    co, si = tabs(32, pos)
    c["cos32"] = np.concatenate([co, co], 1)
    c["sin32"] = np.concatenate([-si, si], 1)
    c96 = np.ones((96, SEQ), np.float32)
    s96 = np.zeros((96, SEQ), np.float32)
    c96[64:80] = co.T
    c96[80:96] = co.T
    s96[64:80] = -si.T
    s96[80:96] = si.T
    c["c96"], c["s96"] = c96, s96
    co, si = tabs(64, pos)
    c["cos64"] = np.concatenate([co, co], 1)
    c["sin64"] = np.concatenate([-si, si], 1)
    c["onehot8"] = (np.arange(SEQ)[None, :] // 256 == np.arange(8)[:, None]).astype(np.float32)
    ps = np.full((4,), 16384.0, np.float32)
    co, si = tabs(32, ps)
    c["scos32"] = np.concatenate([co, co], 1)
    c["ssin32"] = np.concatenate([-si, si], 1)
    co, si = tabs(64, ps)
    c["scos64"] = np.concatenate([co, co], 1)
    c["ssin64"] = np.concatenate([-si, si], 1)
    return c


def sample_shared_inputs(inp, c):
    w_uk = np.ascontiguousarray(np.asarray(inp["w_uk"][0], dtype=np.float32))
    return {
        "p_ckv": np.asarray(inp["cache_mla_ckv"][0]).reshape(NPOOL * 128, 256),
        "p_kr": np.asarray(inp["cache_mla_krope"][0]).reshape(NPOOL * 128, 32),
        "p_k": np.asarray(inp["cache_moba_k"][0]).reshape(NPOOL * 128, 256),
        "p_v": np.asarray(inp["cache_moba_v"][0]).reshape(NPOOL * 128, 256),
        "w_ukT": np.ascontiguousarray(w_uk.transpose(2, 1, 0).reshape(64, 12 * 256)),
        "scos32": c["scos32"], "ssin32": c["ssin32"], "scos64": c["scos64"], "ssin64": c["ssin64"],
    }


def sample_core_inputs(inp, core):
    sl = slice(4 * core, 4 * core + 4)
    sc = np.asarray(inp["state_conv"], dtype=np.float32)[:, sl]
    sconv = sc.reshape(2, 4, 2, NFC, 128).transpose(0, 4, 3, 1, 2).reshape(2, 128, NFC * 8)
    return {
        "xs": np.ascontiguousarray(np.asarray(inp["x_sample"], dtype=np.float32)[sl, 0]),
        "ptab": np.ascontiguousarray(np.asarray(inp["page_table"], dtype=np.int32)[sl].reshape(1, 4 * 128)),
        "cmk": np.ascontiguousarray(np.asarray(inp["cache_mem_k"], dtype=np.float32)[:, sl].reshape(2, 4, 256, 256)),
        "cmv": np.ascontiguousarray(np.asarray(inp["cache_mem_v"], dtype=np.float32)[:, sl].reshape(2, 4, 256, 256)),
        "sconv": np.ascontiguousarray(sconv),
    }


def sample_outputs(R):
    n = len(R)
    st = lambda name: np.stack([np.asarray(R[i][name]) for i in range(n)])
    cs = st("conv_s").reshape(n, 2, 128, NFC, 4, 2)
    conv_s = np.ascontiguousarray(cs.transpose(1, 0, 4, 5, 3, 2).reshape(2, n * 4, 2, DFF))
    return dict(y_s=st("y_s").reshape(n * 4, 1, D), ckv_s=st("ckv_s").reshape(1, n * 4, 1, 256),
                krope_s=st("krope_s").reshape(1, n * 4, 1, 32), mobak_s=st("mobak_s").reshape(1, n * 4, 1, 4, 64),
                mobav_s=st("mobav_s").reshape(1, n * 4, 1, 4, 64), conv_s=conv_s)


_PROG = {}


def kernel(**inp):
    flags = dict(FLAGS)
    key = (flags["sample"], flags["layers"], flags.get("stop"), flags.get("prompt", True))
    if key not in _PROG:
        _PROG[key] = build_program(flags)
    nc = _PROG[key]
    f32 = lambda a: np.ascontiguousarray(np.asarray(a), dtype=np.float32)
    c = _consts()
    w_q_b = f32(inp["w_q_b"][0])
    sw = w_q_b.reshape(384, 12, 96).copy()
    sw[:, :, 64:80] = w_q_b.reshape(384, 12, 96)[:, :, 80:96]
    sw[:, :, 80:96] = w_q_b.reshape(384, 12, 96)[:, :, 64:80]
    conv_w = f32(inp["conv_w"])
    conv_b = f32(inp["conv_b"])
    shared = {
        "ident": c["ident"], "masks": c["masks"], "c96": c["c96"], "s96": c["s96"], "cos32": c["cos32"], "sin32": c["sin32"],
        "cos64": c["cos64"], "sin64": c["sin64"], "onehot8": c["onehot8"],
        "w_in_a": f32(inp["w_in_a"][0]), "g_q": f32(inp["g_q"]), "w_q_b": w_q_b, "w_q_b_sw": sw.reshape(384, 1152),
        "g_kv": f32(inp["g_kv"]), "w_uk": f32(inp["w_uk"][0]).reshape(256, 768), "w_uv": f32(inp["w_uv"][0]).reshape(256, 768),
        "w_in_b": f32(inp["w_in_b"][0]), "w_mem_k": f32(inp["w_mem_k"]), "w_mem_v": f32(inp["w_mem_v"]), "w_o": f32(inp["w_o"]),
        "ln1_g": f32(inp["ln1_g"]), "ln1_b": f32(inp["ln1_b"]), "ln2_g": f32(inp["ln2_g"]), "ln2_b": f32(inp["ln2_b"]),
        "w_up": f32(inp["w_up"]), "w_down": f32(inp["w_down"]),
        "convw": np.ascontiguousarray(conv_w.reshape(2, 3, NFC, 128).transpose(0, 3, 2, 1).reshape(2, 128, NFC * 3)),
        "convb": np.ascontiguousarray(conv_b.reshape(2, NFC, 128).transpose(0, 2, 1)),
    }
    if flags["sample"]:
        shared.update(sample_shared_inputs(inp, c))
    xp = f32(inp["x_prompt"])
    memp = f32(inp["mem_prompt"])
    in_maps = []
    for core in range(8):
        m = dict(shared)
        m["xp"] = xp[core]
        m["memp"] = memp[core]
        if flags["sample"]:
            m.update(sample_core_inputs(inp, core))
        in_maps.append(m)
    res = run_bass_kernel_spmd(nc, in_maps, core_ids=list(range(8)))
    R = res.results
    st = lambda name: np.stack([np.asarray(R[i][name]) for i in range(8)])
    y_p = st("y_p")
    ckv_p = st("ckv_p")[None]
    krope_p = st("krope_p")[None]
    mobak_p = st("mobak_p").reshape(8, SEQ, 4, 64)[None]
    mobav_p = st("mobav_p").reshape(8, SEQ, 4, 64)[None]
    memk = st("memk_p").transpose(1, 0, 2, 3).reshape(2, 8, 256, 4, 64)
    memv = st("memv_p").transpose(1, 0, 2, 3).reshape(2, 8, 256, 4, 64)
    cp = st("conv_p").reshape(8, 2, 128, NFC, 2)
    conv_p = np.ascontiguousarray(cp.transpose(1, 0, 4, 3, 2).reshape(2, 8, 2, DFF))
    if flags["sample"]:
        so = sample_outputs(R)
    else:
        z = lambda *s: np.zeros(s, np.float32)
        so = dict(y_s=z(32, 1, D), ckv_s=z(1, 32, 1, 256), krope_s=z(1, 32, 1, 32), mobak_s=z(1, 32, 1, 4, 64),
                  mobav_s=z(1, 32, 1, 4, 64), conv_s=z(2, 32, 2, DFF))
    return (y_p, so["y_s"], ckv_p, krope_p, mobak_p, mobav_p, memk, memv, conv_p,
            so["ckv_s"], so["krope_s"], so["mobak_s"], so["mobav_s"], so["conv_s"])
```

```python
from contextlib import ExitStack
import numpy as np
import concourse.bass as bass
import concourse.mybir as mybir

F32 = mybir.dt.float32
BF16 = mybir.dt.bfloat16
I32 = mybir.dt.int32
ALU = mybir.AluOpType
AF = mybir.ActivationFunctionType
AX = mybir.AxisListType
import os as _os
NO_SELF_WAIT = _os.environ.get("NOSELF", "0") == "1"


class Tile:
    __slots__ = ("t", "name", "lw", "readers", "dsem", "dcount", "sch", "excl")

    def __init__(self, sch, t, name):
        self.sch = sch
        self.t = t
        self.name = name
        self.lw = None
        self.readers = {}
        self.dsem = None
        self.dcount = 0
        self.excl = False

    def __getitem__(self, k):
        return self.t[k]


class Eng:
    def __init__(self, name, h, sem):
        self.name = name
        self.h = h
        self.sem = sem
        self.count = 0
        self.waited = {}
        self.prog = []


class Sched:
    def __init__(self, nc, es):
        self.nc = nc
        self.es = es
        self.eng = {}
        self.sb_bytes = 0
        self.nsem = 0
        self.dma_tiles = []
        for name, h in (("pe", nc.tensor), ("act", nc.scalar), ("dve", nc.vector),
                        ("pool", nc.gpsimd), ("sp", nc.sync)):
            self.eng[name] = Eng(name, h, self.new_sem("e_" + name))

    def new_sem(self, name):
        self.nsem += 1
        return self.es.enter_context(self.nc.semaphore(name))

    def sbuf(self, name, shape, dtype):
        t = self.es.enter_context(self.nc.sbuf_tensor("sb_" + name, list(shape), dtype))
        n = 1
        for d in shape[1:]:
            n *= d
        self.sb_bytes += n * (4 if dtype in (F32, I32) else 2)
        return Tile(self, t, name)

    def psum(self, name, shape, dtype):
        t = self.es.enter_context(self.nc.psum_tensor("pp_" + name, list(shape), dtype))
        tl = Tile(self, t, name)
        tl.excl = True
        return tl

    def view(self, tile, name=None):
        return Tile(self, tile.t, name or tile.name)

    def _deps(self, E, reads, writes):
        deps = []
        for t in reads:
            if t.lw is not None:
                deps.append(t.lw)
            if t.excl:
                deps.extend(ev for k, ev in t.readers.items() if k != id(E.sem))
        for t in writes:
            if t.lw is not None:
                deps.append(t.lw)
            deps.extend(t.readers.values())
        for sem, val in deps:
            if sem is E.sem and (E.name == "pe" or NO_SELF_WAIT):
                continue
            k = id(sem)
            if E.waited.get(k, 0) < val:
                E.waited[k] = val
                E.prog.append(("w", sem, val))

    def op(self, eng, fn, reads=(), writes=()):
        E = self.eng[eng]
        self._deps(E, reads, writes)
        E.count += 1
        E.prog.append(("op", fn))
        ev = (E.sem, E.count)
        for t in reads:
            t.readers[id(E.sem)] = ev
        for t in writes:
            t.lw = ev
            t.readers = {}

    def dma(self, eng, out, in_, reads=(), writes=(), fn=None, **kw):
        E = self.eng[eng]
        self._deps(E, reads, writes)
        tl = list(writes) + list(reads)
        t = tl[0]
        if t.dsem is None:
            t.dsem = self.new_sem("d_" + t.name)
            self.dma_tiles.append(t)
        t.dcount += 1
        ev = (t.dsem, 16 * t.dcount)
        E.prog.append(("dma", out, in_, t.dsem, kw, fn))
        for x in reads:
            x.readers[id(t.dsem)] = ev
        for x in writes:
            x.lw = ev
            x.readers = {}

    def finish(self):
        E = self.eng["sp"]
        for t in self.dma_tiles:
            E.prog.append(("w", t.dsem, 16 * t.dcount))

    def emit(self):
        self.finish()
        with self.nc.Block() as block:
            def run(E):
                def body(e):
                    for it in E.prog:
                        if it[0] == "w":
                            e.wait_ge(it[1], it[2])
                        elif it[0] == "op":
                            it[1](e).then_inc(E.sem, 1)
                        else:
                            _, out, in_, dsem, kw, fn = it
                            if fn is not None:
                                fn(e).then_inc(dsem, 16)
                            else:
                                e.dma_start(out=out, in_=in_, **kw).then_inc(dsem, 16)
                return body
            block.tensor(run(self.eng["pe"]))
            block.scalar(run(self.eng["act"]))
            block.vector(run(self.eng["dve"]))
            block.gpsimd(run(self.eng["pool"]))
            block.sync(run(self.eng["sp"]))

from concourse.bass_utils import run_bass_kernel_spmd
import os

D = 1024
SEQ = 2048
NT = 16
DFF = 2816
NFC = 22
ALPHA = float((2 * 2) ** 0.25)
LN_EPS = 1e-5
RMS_EPS = 1e-6
BIG = 30000.0
NPOOL = 5120
FLAGS = {"sample": True, "layers": 2, "stop": "", "prompt": True}


class StopBuild(Exception):
    pass


class K:
    pass


def build_program(flags):
    nc = bass.Bass("TRN2", target_bir_lowering=False)
    g = K()
    g.nc = nc

    def din(name, shape, dt=F32):
        return nc.dram_tensor(name, list(shape), dt, kind="ExternalInput").ap()

    def dout(name, shape, dt=F32):
        return nc.dram_tensor(name, list(shape), dt, kind="ExternalOutput").ap()

    I = {}
    for name, shape in [
        ("xp", [SEQ, D]), ("memp", [256, D]), ("ident", [128, 128]), ("masks", [128, 4 * 512]),
        ("c96", [96, SEQ]), ("s96", [96, SEQ]), ("cos32", [SEQ, 32]), ("sin32", [SEQ, 32]),
        ("cos64", [SEQ, 64]), ("sin64", [SEQ, 64]), ("onehot8", [8, SEQ]),
        ("w_in_a", [D, 928]), ("g_q", [1, 384]), ("w_q_b", [384, 1152]), ("w_q_b_sw", [384, 1152]),
        ("g_kv", [1, 256]), ("w_uk", [256, 768]), ("w_uv", [256, 768]), ("w_in_b", [D, 1536]),
        ("w_mem_k", [2, D, 256]), ("w_mem_v", [2, D, 256]), ("w_o", [2, D, D]),
        ("ln1_g", [2, D]), ("ln1_b", [2, D]), ("ln2_g", [2, D]), ("ln2_b", [2, D]),
        ("w_up", [2, D, 2 * DFF]), ("w_down", [2, DFF, D]),
        ("convw", [2, 128, NFC * 3]), ("convb", [2, 128, NFC]),
    ]:
        I[name] = din(name, shape)
    O = {}
    for name, shape in [
        ("y_p", [SEQ, D]), ("ckv_p", [SEQ, 256]), ("krope_p", [SEQ, 32]), ("mobak_p", [SEQ, 256]),
        ("mobav_p", [SEQ, 256]), ("memk_p", [2, 256, 256]), ("memv_p", [2, 256, 256]),
        ("conv_p", [2, 128, NFC * 2]),
    ]:
        O[name] = dout(name, shape)
    if flags["sample"]:
        declare_sample_io(g, I, O, din, dout)
    X1 = nc.dram_tensor("x1_scr", [SEQ, D], F32, kind="Internal").ap()
    X2 = nc.dram_tensor("x2_scr", [SEQ, D], F32, kind="Internal").ap()

    with ExitStack() as es:
        S = Sched(nc, es)
        g.S = S
        g.I = I
        g.O = O
        g.ps = [S.psum("ps%d" % i, [128, 512], F32) for i in range(4)]
        g.po = [S.psum("po%d" % i, [128, 512], F32) for i in range(2)]
        g.pt = [S.psum("pt%d" % i, [128, 1024], BF16) for i in range(2)]
        g.ips = 0
        g.ipo = 0
        g.ipt = 0

        def nps():
            g.ips = (g.ips + 1) % 4
            return g.ps[g.ips]

        def npo():
            g.ipo = (g.ipo + 1) % 2
            return g.po[g.ipo]

        def npt():
            g.ipt = (g.ipt + 1) % 2
            return g.pt[g.ipt]
        g.nps, g.npo, g.npt = nps, npo, npt

        ident = S.sbuf("ident", [128, 128], BF16)
        masks = S.sbuf("masks", [128, 4, 512], BF16)
        g.ident = ident
        S.dma("pool", ident[:], I["ident"][:, :], writes=[ident])
        S.dma("pool", masks[:], I["masks"].rearrange("p (j q) -> p j q", j=4), writes=[masks])
        XT = S.sbuf("XT", [128, 8, SEQ], BF16)
        MIX = S.sbuf("MIX", [128, 8, SEQ], BF16)
        BUF = S.sbuf("BUF", [128, 10240], F32)
        WA = S.sbuf("WA", [128, 8 * 1536], BF16)
        WB = S.sbuf("WB", [128, 10240], BF16)
        TAB = S.sbuf("TAB", [128, 4096], F32)
        lnp = S.sbuf("lnp", [128, 2, D], F32)
        gqb = S.sbuf("gqb", [128, 384 + 256], F32)
        cvw = S.sbuf("cvw", [128, NFC, 3], F32)
        cvb = S.sbuf("cvb", [128, NFC], F32)
        cst = S.sbuf("cst", [128, NFC, 2], F32)
        xf = [S.sbuf("xf%d" % i, [128, D], F32) for i in range(1)]
        xb = [S.sbuf("xb%d" % i, [128, D], BF16) for i in range(1)]
        zt = [S.sbuf("zt%d" % i, [128, D], F32) for i in range(1)]
        sq = S.sbuf("sq", [128, D], F32)
        st = S.sbuf("st", [128, 16], F32)
        pT = [S.sbuf("pT%d" % i, [128, 512], BF16) for i in range(3)]
        rc = [S.sbuf("rc%d" % i, [128, 512], F32) for i in range(1)]
        stg = [S.sbuf("stg%d" % i, [128, 512], F32) for i in range(2)]
        sgb = [S.sbuf("sgb%d" % i, [128, 1024], BF16) for i in range(1)]
        g.ctr = {}

        def rot(lst, key):
            g.ctr[key] = (g.ctr.get(key, -1) + 1) % len(lst)
            return lst[g.ctr[key]]

        def MM(ps, out, lhsT, rhs, start, stop, reads):
            S.op("pe", lambda e, o=out, l=lhsT, r=rhs, a=start, b=stop: e.matmul(o, lhsT=l, rhs=r, start=a, stop=b),
                 reads=reads, writes=[ps])

        def TR(pt_t, out, in_, reads, npart=128):
            S.op("pe", lambda e, o=out, i=in_, n=npart: e.transpose(out=o, in_=i, identity=ident[0:n, 0:n]),
                 reads=list(reads) + [ident], writes=[pt_t])

        def ACT(out, in_, func, reads, writes, bias=None, scale=None):
            kw = {}
            if bias is not None:
                kw["bias"] = bias
            if scale is not None:
                kw["scale"] = scale
            S.op("act", lambda e, o=out, i=in_, f=func, k=kw: e.activation(out=o, in_=i, func=f, **k),
                 reads=reads, writes=writes)

        def TT(eng, out, in0, in1, op, reads, writes):
            S.op(eng, lambda e, o=out, a=in0, b=in1, p=op: e.tensor_tensor(out=o, in0=a, in1=b, op=p),
                 reads=reads, writes=writes)

        def TS(eng, out, in0, s1, s2, op0, op1, reads, writes):
            if op1 is None:
                S.op(eng, lambda e, o=out, a=in0, x=s1, p=op0: e.tensor_scalar(out=o, in0=a, scalar1=x, scalar2=None, op0=p),
                     reads=reads, writes=writes)
            else:
                S.op(eng, lambda e, o=out, a=in0, x=s1, y=s2, p=op0, q=op1:
                     e.tensor_scalar(out=o, in0=a, scalar1=x, scalar2=y, op0=p, op1=q), reads=reads, writes=writes)

        def STT(eng, out, in0, sc, in1, op0, op1, reads, writes):
            S.op(eng, lambda e, o=out, a=in0, s=sc, b=in1, p=op0, q=op1:
                 e.scalar_tensor_tensor(out=o, in0=a, scalar=s, in1=b, op0=p, op1=q), reads=reads, writes=writes)

        def CP(eng, out, in_, reads, writes):
            S.op(eng, lambda e, o=out, i=in_: e.tensor_copy(out=o, in_=i), reads=reads, writes=writes)

        def MS(eng, ap, val, writes):
            S.op(eng, lambda e, a=ap, v=val: e.memset(a, v), writes=writes)

        def RS(out, in_, reads, writes):
            S.op("dve", lambda e, o=out, i=in_: e.reduce_sum(out=o, in_=i, axis=AX.X), reads=reads, writes=writes)

        def RCP(out, in_, reads, writes):
            S.op("dve", lambda e, o=out, i=in_: e.reciprocal(out=o, in_=i), reads=reads, writes=writes)

        def barrier():
            evs = [(E.sem, E.count) for E in S.eng.values() if E.count > 0]
            evs += [(t.dsem, 16 * t.dcount) for t in S.dma_tiles]
            for E in S.eng.values():
                for sem, val in evs:
                    if sem is E.sem:
                        continue
                    k = id(sem)
                    if E.waited.get(k, 0) < val:
                        E.waited[k] = val
                        E.prog.append(("w", sem, val))
        g.MM, g.TR, g.ACT, g.TT, g.TS, g.STT, g.CP, g.MS, g.RS, g.RCP, g.barrier, g.rot = \
            MM, TR, ACT, TT, TS, STT, CP, MS, RS, RCP, barrier, rot
        g.stg, g.sgb, g.st, g.sq, g.zt, g.xf, g.xb, g.pT, g.rc = stg, sgb, st, sq, zt, xf, xb, pT, rc
        g.lnp, g.gqb, g.cvw, g.cvb = lnp, gqb, cvw, cvb

        import os
        def wload(dst_tile, dst_ap, src_ap):
            if os.environ.get("SKIPW") == "1":
                return
            for kc in range(dst_ap.shape[1]):
                S.dma("pool", dst_ap[:, kc, :], src_ap[:, kc, :], writes=[dst_tile])
        g.wload = wload

        def kcw(w2d):
            return w2d.rearrange("(kc p) n -> p kc n", p=128)

        def rstd_of(ps_slice, n, np_, eps, reads):
            ACT(sq[0:np_, 0:n], ps_slice, AF.Square, reads, [sq])
            RS(st[0:np_, 0:1], sq[0:np_, 0:n], [sq], [st])
            ACT(st[0:np_, 1:2], st[0:np_, 0:1], AF.Sqrt, [st], [st], bias=eps, scale=1.0 / n)
            RCP(st[0:np_, 2:3], st[0:np_, 1:2], [st], [st])
            return st[0:np_, 2:3]
        g.rstd_of = rstd_of

        def layernorm(z, np_, li, out_tile, out_ap):
            zz = z[0:np_, :]
            RS(st[0:np_, 4:5], zz, [z], [st])
            ACT(sq[0:np_, :], zz, AF.Square, [z], [sq])
            RS(st[0:np_, 5:6], sq[0:np_, :], [sq], [st])
            TS("dve", st[0:np_, 6:7], st[0:np_, 4:5], 1.0 / D, None, ALU.mult, None, [st], [st])
            TT("dve", st[0:np_, 7:8], st[0:np_, 6:7], st[0:np_, 6:7], ALU.mult, [st], [st])
            STT("dve", st[0:np_, 8:9], st[0:np_, 5:6], 1.0 / D, st[0:np_, 7:8], ALU.mult, ALU.subtract, [st], [st])
            ACT(st[0:np_, 9:10], st[0:np_, 8:9], AF.Sqrt, [st], [st], bias=LN_EPS, scale=1.0)
            RCP(st[0:np_, 10:11], st[0:np_, 9:10], [st], [st])
            TS("dve", zz, zz, st[0:np_, 6:7], st[0:np_, 10:11], ALU.subtract, ALU.mult, [z, st], [z])
            TT("pool", zz, zz, lnp[0:np_, 0, :], ALU.mult, [z, lnp], [z])
            TT("pool", out_ap, zz, lnp[0:np_, 1, :], ALU.add, [z, lnp], [out_tile])
        g.layernorm = layernorm

        def to_xT(src_bf, dst, t):
            pt_t = npt()
            for kc in range(8):
                TR(pt_t, pt_t[:, kc * 128:(kc + 1) * 128], src_bf[:, kc * 128:(kc + 1) * 128], [src_bf])
            CP("dve", dst[:, :, t * 128:(t + 1) * 128], pt_t[:, :].rearrange("p (k n) -> p k n", k=8), [pt_t], [dst])
        g.to_xT = to_xT

        def attention(qT, qtile, kT, ktile, Kp0, Kp1, vaug_fn, vtile, nkt, causal, scale, den_lo, out_chunk):
            num0 = 64 if den_lo else 0
            den0 = 0 if den_lo else 64
            for qb in range(4):
                po = npo()
                kts = list(range(4 * qb + 4)) if causal else list(range(nkt))
                LOOK = 2
                pss = {}

                def issue_score(i, qb=qb, kts=kts, pss=pss):
                    kt = kts[i]
                    ps = nps()
                    MM(ps, ps[:, :], kT[Kp0:Kp1, kt * 128:(kt + 1) * 128], qT[Kp0:Kp1, qb * 512:(qb + 1) * 512],
                       True, True, [ktile, qtile])
                    pss[i] = ps
                for i in range(min(LOOK, len(kts))):
                    issue_score(i)
                for i, kt in enumerate(kts):
                    ps = pss.pop(i)
                    p = rot(pT, "pT")
                    ACT(p[:, :], ps[:, :], AF.Exp, [ps], [p], scale=scale)
                    if causal and kt >= 4 * qb:
                        TT("pool", p[:, :], p[:, :], masks[:, kt - 4 * qb, :], ALU.mult, [p, masks], [p])
                    if i + LOOK < len(kts):
                        issue_score(i + LOOK)
                    MM(po, po[:, :], vaug_fn(kt), p[:, :], i == 0, i == len(kts) - 1, [vtile, p])
                r = rot(rc, "rc")
                RCP(r[den0:den0 + 64, :], po[den0:den0 + 64, :], [po], [r])
                TT("dve", MIX[num0:num0 + 64, out_chunk, qb * 512:(qb + 1) * 512], po[num0:num0 + 64, :],
                   r[den0:den0 + 64, :], ALU.mult, [po, r], [MIX])
        g.attention = attention

        BUFb = S.view(BUF)
        Gk = [S.view(XT, "Gk0"), S.view(XT, "Gk1")]
        Gq = [S.view(XT, "Gq0"), S.view(XT, "Gq1")]
        Gv = [S.view(XT, "Gv0"), S.view(XT, "Gv1")]
        Gt = [S.view(BUF, "Gt0"), S.view(BUF, "Gt1")]
        g.mkT_sb = S.sbuf("mkT_sb", [128, 2, 256], BF16)
        g.vmem_sb = S.sbuf("vmem_sb", [128, 2, 4, 128], BF16)
        g.X1t = Tile(S, None, "X1t")
        g.X2t = Tile(S, None, "X2t")
        g.wtile = [S.view(WB, "wt0"), S.view(WB, "wt1")]
        g.wdtile = [S.view(WA, "wd0"), S.view(WA, "wd1")]
        g.gftile = [S.view(MIX, "gf0"), S.view(MIX, "gf1")]
        g.fttile = [S.view(MIX, "ft0"), S.view(MIX, "ft1")]
        g.hTtile = S.view(MIX, "hTt")

        def bview(off_words, nwords, pattern=None, **kw):
            ap = BUF[:, off_words:off_words + nwords].bitcast(BF16)
            if pattern:
                ap = ap.rearrange(pattern, **kw)
            return ap

        xsrc = I["xp"]
        def stop_if(tag):
            if flags.get("stop") == tag:
                raise StopBuild()
        try:
          for L in range(flags["layers"] if flags.get("prompt", True) else 0):
              mla = (L == 0)
              barrier()
              def load_ln(which):
                  for i, nm in enumerate([which + "_g", which + "_b"]):
                      S.dma("sp", lnp[:, i, :], I[nm][L:L + 1, :].partition_broadcast(128)[:, 0, :], writes=[lnp])
              load_ln("ln1")
              S.dma("sp", cvw[:], I["convw"][L].rearrange("p (c t) -> p c t", t=3), writes=[cvw])
              S.dma("sp", cvb[:], I["convb"][L], writes=[cvb])
              WMt = WA if mla else WB
              WM = WMt[:, 8192:12288] if mla else WMt[:, 0:4096]
              wmk = WM[:, 0:2048].rearrange("p (k n) -> p k n", k=8)
              wmv = WM[:, 2048:4096].rearrange("p (k n) -> p k n", k=8)
              wload(WMt, wmk, kcw(I["w_mem_k"][L]))
              wload(WMt, wmv, kcw(I["w_mem_v"][L]))
              if mla:
                  NIN = 928
                  wload(WA, WA[:, 0:8 * NIN].rearrange("p (k n) -> p k n", k=8), kcw(I["w_in_a"]))
                  S.dma("sp", gqb[:, 0:384], I["g_q"].partition_broadcast(128)[:, 0, :], writes=[gqb])
                  S.dma("sp", gqb[:, 384:640], I["g_kv"].partition_broadcast(128)[:, 0, :], writes=[gqb])
                  S.dma("sp", TAB[:, 0:512].rearrange("p (t d) -> p t d", d=32), I["cos32"].rearrange("(t p) d -> p t d", p=128), writes=[TAB])
                  S.dma("sp", TAB[:, 512:1024].rearrange("p (t d) -> p t d", d=32), I["sin32"].rearrange("(t p) d -> p t d", p=128), writes=[TAB])
                  cosT = TAB[:, 0:512].rearrange("p (t d) -> p t d", d=32)
                  sinT = TAB[:, 512:1024].rearrange("p (t d) -> p t d", d=32)
              else:
                  NIN = 1536
                  wload(WA, WA[:, 0:8 * NIN].rearrange("p (k n) -> p k n", k=8), kcw(I["w_in_b"]))
                  S.dma("sp", TAB[:, 0:1024].rearrange("p (t d) -> p t d", d=64), I["cos64"].rearrange("(t p) d -> p t d", p=128), writes=[TAB])
                  S.dma("sp", TAB[:, 1024:2048].rearrange("p (t d) -> p t d", d=64), I["sin64"].rearrange("(t p) d -> p t d", p=128), writes=[TAB])
                  cosT = TAB[:, 0:1024].rearrange("p (t d) -> p t d", d=64)
                  sinT = TAB[:, 1024:2048].rearrange("p (t d) -> p t d", d=64)
              win = WA[:, 0:8 * NIN].rearrange("p (k n) -> p k n", k=8)

              if mla:
                  qnT = bview(0, 3072, "p (k n) -> p k n", k=3)
                  cnT = bview(3072, 2048, "p (k n) -> p k n", k=2)
                  krT = bview(5120, 1024)
                  qmT = bview(6144, 2048, "p (k n) -> p k n", k=2)
                  XT2 = XT[:, :, :].rearrange("p k n -> p (k n)")
                  kTh = [XT2[:, 0:2048], XT2[:, 2048:4096]]
                  qTh = [XT2[:, 4096:6144], XT2[:, 6144:8192]]
                  vaug = [XT2[:, 8192:10240].rearrange("p (t n) -> p t n", t=16), XT2[:, 10240:12288].rearrange("p (t n) -> p t n", t=16)]
                  tmpf = [BUF[:, 8960:9472], BUF[:, 9472:9984]]
              elif flags.get('stop') != 'P0all':
                  XT2 = XT[:, :, :].rearrange("p k n -> p (k n)")
                  qtok = XT2[:, 0:12288].rearrange("p (t h d) -> p t h d", t=16, h=12)
                  qTa = XT2[:, 12288:14336]
                  qa = XT2[:, 14336:15488].rearrange("p (t c) -> p t c", t=16)
                  kTa = bview(0, 4096, "p (k n) -> p k n", k=4)
                  vtok = bview(4096, 2048, "p (t n) -> p t n", t=16)
                  qmT = bview(6144, 2048, "p (k n) -> p k n", k=2)
                  vaug = [bview(8192, 1024, "p (t n) -> p t n", t=16), bview(9216, 1024, "p (t n) -> p t n", t=16)]
                  xTt = WB[:, 4096:5120].rearrange("p (k n) -> p k n", k=8)
                  biasb = WB[:, 5120:6656].rearrange("p (t h b) -> p t h b", t=16, h=12)
                  qTt = WB[:, 6656:8192].rearrange("p (h n) -> p h n", h=12)
                  sbm = WB[:, 8192:8704]
                  kmT = WB[:, 8704:8736].rearrange("p (k b) -> p k b", k=4)
                  G_qtok, G_qTa, G_qa = S.view(XT, "G_qtok"), S.view(XT, "G_qTa"), S.view(XT, "G_qa")
                  G_kTa, G_vtok, G_qmT = S.view(BUF, "G_kTa"), S.view(BUF, "G_vtok"), S.view(BUF, "G_qmT")
                  G_xTt, G_bias, G_qTt, G_sbm, G_kmT = (S.view(WB, "G_xTt"), S.view(WB, "G_bias"), S.view(WB, "G_qTt"),
                                                       S.view(WB, "G_sbm"), S.view(WB, "G_kmT"))
              stop_if('S0')
              memT = MIX
              for mt in range(2):
                  f = rot(xf, "xf")
                  S.dma("sp", f[:], I["memp"][mt * 128:(mt + 1) * 128, :], writes=[f])
                  b = rot(xb, "xb")
                  ACT(b[:], f[:], AF.Copy, [f], [b])
                  to_xT(b, memT, mt)
              skipviews = flags.get("stop") == "P0all"
              mkT_ap, vmem_ap = g.mkT_sb[:, :, :], g.vmem_sb[:, :, :, :]
              mkT_tile, vmem_tile = g.mkT_sb, g.vmem_sb
              if not skipviews:
                  MS("pool", vmem_ap, 1.0, [vmem_tile])
              for mt in range(2):
                  for wi, (w, oname) in enumerate([(wmk, "memk_p"), (wmv, "memv_p")]):
                      ps = nps()
                      for kc in range(8):
                          MM(ps, ps[:, 0:256], memT[:, kc, mt * 128:(mt + 1) * 128], w[:, kc, :], kc == 0, kc == 7, [MIX, WMt])
                      sg = rot(stg, "stg")
                      CP("dve", sg[:, 0:256], ps[:, 0:256], [ps], [sg])
                      if True:
                          S.dma("sp", O[oname][L, mt * 128:(mt + 1) * 128, :], sg[:, 0:256], reads=[sg])
                      if wi == 1 and not skipviews:
                          for j in range(4):
                              c0 = 0 if j % 2 == 0 else 64
                              if os.environ.get("DBG") == "dvecp":
                                  CP("dve", vmem_ap[:, mt, j, c0:c0 + 64], ps[:, j * 64:(j + 1) * 64], [ps], [vmem_tile])
                              else:
                                  ACT(vmem_ap[:, mt, j, c0:c0 + 64], ps[:, j * 64:(j + 1) * 64], AF.Copy, [ps], [vmem_tile])
              for pr in range(2):
                  ps = nps()
                  for kc in range(8):
                      MM(ps, ps[:, 0:256], wmk[:, kc, pr * 128:(pr + 1) * 128], memT[:, kc, 0:256], kc == 0, kc == 7, [MIX, WMt])
                  if not skipviews:
                      if os.environ.get("DBG") == "dvecp":
                          CP("dve", mkT_ap[:, pr, :], ps[:, 0:256], [ps], [mkT_tile])
                      else:
                          ACT(mkT_ap[:, pr, :], ps[:, 0:256], AF.Copy, [ps], [mkT_tile])

              stop_if('P0')
              if flags.get('stop') == 'P0all':
                  continue
              if not mla:
                  MS("pool", biasb, -BIG, [G_bias])
                  for kvh in range(4):
                      S.dma("pool", kTa[64:72, kvh, :], I["onehot8"][:, :], writes=[G_kTa])
              for t in range(NT):
                  f = rot(xf, "xf")
                  S.dma("sp", f[:], xsrc[t * 128:(t + 1) * 128, :], writes=[f])
                  b = rot(xb, "xb")
                  ACT(b[:], f[:], AF.Copy, [f], [b])
                  if mla:
                      to_xT(b, XT, t)
                      xts = [XT[:, kc, t * 128:(t + 1) * 128] for kc in range(8)]
                      xtile = XT
                  else:
                      pt_x = npt()
                      for kc in range(8):
                          TR(pt_x, pt_x[:, kc * 128:(kc + 1) * 128], b[:, kc * 128:(kc + 1) * 128], [b])
                      CP("dve", xTt, pt_x[:, :].rearrange("p (k n) -> p k n", k=8), [pt_x], [G_xTt])
                      xts = [xTt[:, kc, :] for kc in range(8)]
                      xtile = G_xTt

                  def proj(c0, c1):
                      ps = nps()
                      for kc in range(8):
                          MM(ps, ps[:, 0:c1 - c0], xts[kc], win[:, kc, c0:c1], kc == 0, kc == 7, [xtile, WA])
                      return ps
                  if mla:
                      psA = proj(0, 384)
                      r = rstd_of(psA[:, 0:384], 384, 128, RMS_EPS, [psA])
                      sb = rot(sgb, "sgb")
                      STT("dve", sb[:, 0:384], psA[:, 0:384], r, gqb[:, 0:384], ALU.mult, ALU.mult, [psA, st, gqb], [sb])
                      psB = proj(384, 672)
                      r = rstd_of(psB[:, 0:256], 256, 128, RMS_EPS, [psB])
                      sg = rot(stg, "stg")
                      STT("dve", sg[:, 0:256], psB[:, 0:256], r, gqb[:, 384:640], ALU.mult, ALU.mult, [psB, st, gqb], [sg])
                      ACT(sb[:, 384:640], sg[:, 0:256], AF.Copy, [sg], [sb])
                      kx = psB[:, 256:288]
                      TT("dve", sg[:, 256:288], kx, cosT[:, t, :], ALU.mult, [psB, TAB], [sg])
                      TT("dve", sg[:, 288:304], psB[:, 272:288], sinT[:, t, 0:16], ALU.mult, [psB, TAB], [sg])
                      TT("dve", sg[:, 304:320], psB[:, 256:272], sinT[:, t, 16:32], ALU.mult, [psB, TAB], [sg])
                      TT("dve", sg[:, 256:288], sg[:, 256:288], sg[:, 288:320], ALU.add, [sg], [sg])
                      S.dma("sp", O["ckv_p"][t * 128:(t + 1) * 128, :], sg[:, 0:256], reads=[sg])
                      S.dma("sp", O["krope_p"][t * 128:(t + 1) * 128, :], sg[:, 256:288], reads=[sg])
                      MS("pool", sb[:, 640:704], 0.0, [sb])
                      ACT(sb[:, 704:736], sg[:, 256:288], AF.Copy, [sg], [sb])
                      psC = proj(672, 928)
                      ACT(sb[:, 768:1024], psC[:, 0:256], AF.Copy, [psC], [sb])
                      pt_t = npt()
                      for j in range(3):
                          TR(pt_t, pt_t[:, j * 128:(j + 1) * 128], sb[:, j * 128:(j + 1) * 128], [sb])
                      for j in range(2):
                          TR(pt_t, pt_t[:, (3 + j) * 128:(4 + j) * 128], sb[:, 384 + j * 128:384 + (j + 1) * 128], [sb])
                      TR(pt_t, pt_t[0:96, 5 * 128:6 * 128], sb[:, 640:736], [sb])
                      for j in range(2):
                          TR(pt_t, pt_t[:, (6 + j) * 128:(7 + j) * 128], sb[:, 768 + j * 128:768 + (j + 1) * 128], [sb])
                      tsl = slice(t * 128, (t + 1) * 128)
                      CP("dve", qnT[:, :, tsl], pt_t[:, 0:384].rearrange("p (k n) -> p k n", k=3), [pt_t], [BUFb])
                      CP("dve", cnT[:, :, tsl], pt_t[:, 384:640].rearrange("p (k n) -> p k n", k=2), [pt_t], [BUFb])
                      ACT(krT[0:96, tsl], pt_t[0:96, 640:768], AF.Copy, [pt_t], [BUFb])
                      ACT(qmT[:, :, tsl], pt_t[:, 768:1024].rearrange("p (k n) -> p k n", k=2), AF.Copy, [pt_t], [BUFb])
                  else:
                      tsl = slice(t * 128, (t + 1) * 128)
                      own = t // 2
                      cosB6 = cosT[:, t:t + 1, :].to_broadcast([128, 6, 64])
                      sinB6 = sinT[:, t:t + 1, :].to_broadcast([128, 6, 64])
                      cosB4 = cosT[:, t:t + 1, :].to_broadcast([128, 4, 64])
                      sinB4 = sinT[:, t:t + 1, :].to_broadcast([128, 4, 64])
                      t1 = sq[:, 0:384].rearrange("p (h d) -> p h d", h=6)
                      t2 = sq[:, 384:768].rearrange("p (h d) -> p h d", h=6)

                      def rope(src3, nh, cB, sB, dst3, dst_tile, src_tile):
                          a, b2 = t1[:, 0:nh, :], t2[:, 0:nh, :]
                          TT("dve", a, src3, cB, ALU.mult, [src_tile, TAB], [sq])
                          TT("dve", b2[:, :, 0:32], src3[:, :, 32:64], sB[:, :, 0:32], ALU.mult, [src_tile, TAB], [sq])
                          TT("dve", b2[:, :, 32:64], src3[:, :, 0:32], sB[:, :, 32:64], ALU.mult, [src_tile, TAB], [sq])
                          TT("dve", dst3, a, b2, ALU.add, [sq], [dst_tile])
                      for qh in range(2):
                          psq = proj(qh * 384, (qh + 1) * 384)
                          rope(psq[:, 0:384].rearrange("p (h d) -> p h d", h=6), 6, cosB6, sinB6,
                               qtok[:, t, qh * 6:(qh + 1) * 6, :], G_qtok, psq)
                      pskv = proj(768, 1280)
                      sgk = rot(stg, "stg")
                      rope(pskv[:, 0:256].rearrange("p (h d) -> p h d", h=4), 4, cosB4, sinB4,
                           sgk[:, 0:256].rearrange("p (h d) -> p h d", h=4), sgk, pskv)
                      S.dma("sp", O["mobak_p"][tsl, :], sgk[:, 0:256], reads=[sgk])
                      ACT(sbm[:, 0:256], sgk[:, 0:256], AF.Copy, [sgk], [G_sbm])
                      sgv = rot(stg, "stg")
                      CP("dve", sgv[:, 0:256], pskv[:, 256:512], [pskv], [sgv])
                      S.dma("sp", O["mobav_p"][tsl, :], sgv[:, 0:256], reads=[sgv])
                      ACT(vtok[:, t, :], sgv[:, 0:256], AF.Copy, [sgv], [G_vtok])
                      psm = proj(1280, 1536)
                      ACT(sbm[:, 256:512], psm[:, 0:256], AF.Copy, [psm], [G_sbm])
                      pt_t = npt()
                      for kvh in range(4):
                          TR(pt_t, pt_t[0:64, kvh * 128:(kvh + 1) * 128], sbm[:, kvh * 64:(kvh + 1) * 64], [G_sbm])
                      for j in range(2):
                          TR(pt_t, pt_t[:, (4 + j) * 128:(5 + j) * 128], sbm[:, 256 + j * 128:256 + (j + 1) * 128], [G_sbm])
                      CP("dve", kTa[0:64, :, tsl], pt_t[0:64, 0:512].rearrange("p (k n) -> p k n", k=4), [pt_t], [G_kTa])
                      ACT(qmT[:, :, tsl], pt_t[:, 512:768].rearrange("p (k n) -> p k n", k=2), AF.Copy, [pt_t], [G_qmT])
                      if own >= 1:
                          if own >= 4:
                              for h0, nh in ((0, 8), (8, 4)):
                                  ptq = npt()
                                  for hh in range(nh):
                                      TR(ptq, ptq[0:64, hh * 128:(hh + 1) * 128], qtok[:, t, h0 + hh, :], [G_qtok])
                                  CP("dve", qTt[0:64, h0:h0 + nh, :], ptq[0:64, 0:nh * 128].rearrange("p (h n) -> p h n", h=nh), [ptq], [G_qTt])
                              psg = nps()
                              for h in range(12):
                                  MM(psg, psg[:, h * 8:h * 8 + own], qTt[0:64, h, :], kmT[0:64, h // 3, 0:own], True, True, [G_qTt, G_kmT])
                              gt = sq[:, 768:864].rearrange("p (h b) -> p h b", h=12)
                              top8 = sq[:, 864:960].rearrange("p (h b) -> p h b", h=12)
                              selt = sq[:, 0:96].rearrange("p (h b) -> p h b", h=12)
                              MS("dve", gt, -1.0e30, [sq])
                              CP("dve", gt[:, :, 0:own], psg[:, 0:96].rearrange("p (h b) -> p h b", h=12)[:, :, 0:own], [psg], [sq])
                              for h in range(12):
                                  S.op("dve", lambda e, o=top8[:, h, :], i=gt[:, h, :]: e.max(out=o, in_=i), reads=[sq], writes=[sq])
                                  TS("dve", selt[:, h, 0:own], gt[:, h, 0:own], top8[:, h, 2:3], None, ALU.is_ge, None, [sq], [sq])
                              TS("dve", biasb[:, t, :, 0:own], selt[:, :, 0:own], -1.0, BIG, ALU.add, ALU.mult, [sq], [G_bias])
                          else:
                              MS("pool", biasb[:, t, :, 0:own], 0.0, [G_bias])
                      MS("pool", biasb[:, t, :, own:own + 1], 0.0, [G_bias])
                      if t % 2 == 1:
                          bb = t // 2
                          kmf = sq[0:64, 960:964]
                          S.op("dve", lambda e, o=kmf, i=kTa[0:64, :, bb * 256:(bb + 1) * 256]: e.reduce_sum(out=o, in_=i, axis=AX.X),
                               reads=[G_kTa], writes=[sq])
                          TS("dve", kmT[0:64, :, bb], kmf, 1.0 / 256.0, None, ALU.mult, None, [sq], [G_kmT])

              stop_if('P1')
              if mla:
                  barrier()
                  wq = WB[:, 0:3456].rearrange("p (k n) -> p k n", k=3)
                  wqs = WB[:, 3456:6912].rearrange("p (k n) -> p k n", k=3)
                  wuk = WB[:, 6912:8448].rearrange("p (k n) -> p k n", k=2)
                  wuv = WB[:, 8448:9984].rearrange("p (k n) -> p k n", k=2)
                  wload(WB, wq, kcw(I["w_q_b"]))
                  wload(WB, wqs, kcw(I["w_q_b_sw"]))
                  wload(WB, wuk, kcw(I["w_uk"]))
                  wload(WB, wuv, kcw(I["w_uv"]))
                  S.dma("sp", TAB[0:96, 0:2048], I["c96"][:, :], writes=[TAB])
                  S.dma("sp", TAB[0:96, 2048:4096], I["s96"][:, :], writes=[TAB])
                  wload(WA, WA[:, 0:8192].rearrange("p (k n) -> p k n", k=8), kcw(I["w_o"][L]))
                  MS("pool", vaug[0][:, :, 64:128], 1.0, [Gv[0]])
                  MS("pool", vaug[1][:, :, 0:64], 1.0, [Gv[1]])
                  for h in range(12):
                      kT_h = kTh[h % 2]
                      qT_h = qTh[h % 2]
                      va = vaug[h % 2]
                      gk, gq, gv = Gk[h % 2], Gq[h % 2], Gv[h % 2]
                      v0 = 0 if h % 2 == 0 else 64
                      for tb in range(4):
                          bs = slice(tb * 512, (tb + 1) * 512)
                          ps = nps()
                          for kc in range(2):
                              MM(ps, ps[0:64, :], wuk[:, kc, h * 64:(h + 1) * 64], cnT[:, kc, bs], kc == 0, kc == 1, [WB, BUFb])
                          ACT(kT_h[0:64, bs], ps[0:64, :], AF.Copy, [ps], [gk])
                          ps1 = nps()
                          for kc in range(3):
                              MM(ps1, ps1[0:96, :], wq[:, kc, h * 96:(h + 1) * 96], qnT[:, kc, bs], kc == 0, kc == 2, [WB, BUFb])
                          ps2 = nps()
                          for kc in range(3):
                              MM(ps2, ps2[0:96, :], wqs[:, kc, h * 96:(h + 1) * 96], qnT[:, kc, bs], kc == 0, kc == 2, [WB, BUFb])
                          TT("dve", tmpf[0][0:96, :], ps1[0:96, :], TAB[0:96, tb * 512:(tb + 1) * 512], ALU.mult, [ps1, TAB], [Gt[0]])
                          TT("dve", tmpf[1][0:96, :], ps2[0:96, :], TAB[0:96, 2048 + tb * 512:2048 + (tb + 1) * 512], ALU.mult, [ps2, TAB], [Gt[1]])
                          TT("pool", qT_h[0:96, bs], tmpf[0][0:96, :], tmpf[1][0:96, :], ALU.add, [Gt[0], Gt[1]], [gq])
                      CP("pool", kT_h[64:96, :], krT[64:96, :], [BUFb], [gk])
                      for half in range(2):
                          ps = nps()
                          for tt in range(8):
                              t = half * 8 + tt
                              for kc in range(2):
                                  MM(ps, ps[:, tt * 64:(tt + 1) * 64], cnT[:, kc, t * 128:(t + 1) * 128], wuv[:, kc, h * 64:(h + 1) * 64],
                                     kc == 0, kc == 1, [WB, BUFb])
                          ACT(va[:, half * 8:(half + 1) * 8, v0:v0 + 64], ps[:, :].rearrange("p (t d) -> p t d", t=8), AF.Copy, [ps], [gv])
                      attention(qT_h, gq, kT_h, gk, 0, 96, (lambda kt, va=va: va[:, kt, :]), gv, 16, True,
                                float(96 ** -0.5), h % 2 == 1, h // 2)
              else:
                  barrier()
                  wload(WA, WA[:, 0:8192].rearrange("p (k n) -> p k n", k=8), kcw(I["w_o"][L]))
                  Gv2 = [S.view(BUF, "Gv2_0"), S.view(BUF, "Gv2_1")]
                  MS("pool", vaug[0][:, :, 64:128], 1.0, [Gv2[0]])
                  MS("pool", vaug[1][:, :, 0:64], 1.0, [Gv2[1]])
                  for h in range(12):
                      kvh = h // 3
                      par = h % 2
                      va, gv = vaug[par], Gv2[par]
                      v0 = 0 if par == 0 else 64
                      CP("pool", qa[:, :, 0:64], qtok[:, :, h, :], [G_qtok], [G_qa])
                      CP("pool", qa[:, :, 64:72], biasb[:, :, h, :], [G_bias], [G_qa])
                      for half in range(2):
                          ptq = npt()
                          for tt in range(8):
                              TR(ptq, ptq[0:72, tt * 128:(tt + 1) * 128], qa[:, half * 8 + tt, :], [G_qa])
                          CP("dve", qTa[0:72, half * 1024:(half + 1) * 1024], ptq[0:72, :], [ptq], [G_qTa])
                      CP("pool", va[:, :, v0:v0 + 64], vtok[:, :, kvh * 64:(kvh + 1) * 64], [G_vtok], [gv])
                      attention(qTa, G_qTa, kTa[:, kvh, :], G_kTa, 0, 72, (lambda kt, va=va: va[:, kt, :]), gv, 16, True,
                                0.125, par == 1, h // 2)
                  BUFb_q = G_qmT
              for j in range(4):
                  p0 = (j % 2) * 64
                  attention(qmT[:, j // 2, :], (BUFb if mla else G_qmT), mkT_ap[:, j // 2, :], mkT_tile, p0, p0 + 64,
                            (lambda kt, j=j: vmem_ap[:, kt, j, :]), vmem_tile, 2, False, 0.125, j % 2 == 1, 6 + j // 2)

              stop_if('P2')
              barrier()
              wo = WA[:, 0:8192].rearrange("p (k n) -> p k n", k=8)
              for t in range(NT):
                  pa, pb = nps(), nps()
                  for hf, ps in enumerate((pa, pb)):
                      for kc in range(8):
                          MM(ps, ps[:, :], MIX[:, kc, t * 128:(t + 1) * 128], wo[:, kc, hf * 512:(hf + 1) * 512], kc == 0, kc == 7, [MIX, WA])
                  f = rot(xf, "xf")
                  S.dma("sp", f[:], xsrc[t * 128:(t + 1) * 128, :], writes=[f])
                  z = rot(zt, "zt")
                  for hf, ps in enumerate((pa, pb)):
                      STT("dve", z[:, hf * 512:(hf + 1) * 512], f[:, hf * 512:(hf + 1) * 512], ALPHA, ps[:, :], ALU.mult, ALU.add, [f, ps], [z])
                  layernorm(z, 128, 0, z, z[:, :])
                  S.dma("sp", X1[t * 128:(t + 1) * 128, :], z[:, :], reads=[z], writes=[g.X1t])
                  b = rot(xb, "xb")
                  ACT(b[:], z[:], AF.Copy, [z], [b])
                  to_xT(b, XT, t)

              stop_if('P3')
              barrier()
              load_ln("ln2")
              acc = BUF[:, 0:8192].rearrange("p (t n) -> p t n", t=8)
              MIXf = MIX[:, :, :].rearrange("p k n -> p (k n)").bitcast(F32)
              g.gfull = [MIXf[:, 0:1026], MIXf[:, 1026:2052]]
              g.ftmp = [MIXf[:, 2052:2564], MIXf[:, 2564:3076]]
              g.hT = MIXf[:, 3076:4100].bitcast(BF16).rearrange("p (j n) -> p j n", j=2)
              g.halo = MIXf[:, 4100:4144].rearrange("p (c t) -> p c t", t=2)
              g.wdn = [WA[:, 0:2048].rearrange("p (j n) -> p j n", j=2), WA[:, 2048:4096].rearrange("p (j n) -> p j n", j=2)]
              g.MIXt, g.WAt, g.WBt, g.XTt, g.BUFb = MIX, WA, WB, XT, BUFb
              MS("pool", g.halo, 0.0, [MIX])
              xdst = X2 if L == 0 else O["y_p"]
              for half in range(2):
                  for t in range(8):
                      tt = half * 8 + t
                      S.dma("sp", acc[:, t, :], X1[tt * 128:(tt + 1) * 128, :], reads=[g.X1t], writes=[BUFb])
                  ffn(g, L, half, acc, cst)
                  for t in range(8):
                      tt = half * 8 + t
                      z = rot(zt, "zt")
                      ACT(z[:], acc[:, t, :], AF.Copy, [BUFb], [z])
                      layernorm(z, 128, 2, z, z[:, :])
                      S.dma("sp", xdst[tt * 128:(tt + 1) * 128, :], z[:, :], reads=[z], writes=[g.X2t])
              S.dma("sp", O["conv_p"][L].rearrange("p (c t) -> p c t", t=2), cst[:], reads=[cst])
              xsrc = X2
        except StopBuild:
            pass
        if flags["sample"]:
            g.XT, g.MIX, g.BUF, g.WA, g.WB, g.TAB, g.flags = XT, MIX, BUF, WA, WB, TAB, flags
            g.wdn = [WA[:, 0:2048].rearrange("p (j n) -> p j n", j=2), WA[:, 2048:4096].rearrange("p (j n) -> p j n", j=2)]
            sample_path(g)
        S.emit()
        print("SBUF KB/partition:", S.sb_bytes / 1024.0, "sems:", S.nsem, {k: len(E.prog) for k, E in S.eng.items()})
    return nc


def ffn(g, L, half, acc, cst):
    S, I = g.S, g.I
    MM, ACT, TT, TS, STT, CP, MS, rot, nps = g.MM, g.ACT, g.TT, g.TS, g.STT, g.CP, g.MS, g.rot, g.nps
    XT, MIX, WA, WB, BUFb = g.XTt, g.MIXt, g.WAt, g.WBt, g.BUFb
    tok0 = half * 1024
    for t in range(8):
        S.op("act", lambda e, o=acc[:, t, :]: e.mul(out=o, in_=o, mul=ALPHA), reads=[BUFb], writes=[BUFb])
    wup = I["w_up"][L].rearrange("(kc p) n -> p kc n", p=128)
    wdn = I["w_down"][L]
    for gi in range(11):
        w = WB[:, (gi % 2) * 4096:(gi % 2) * 4096 + 4096]
        wu = w[:, 0:2048].rearrange("p (k n) -> p k n", k=8)
        wg = w[:, 2048:4096].rearrange("p (k n) -> p k n", k=8)
        wt = g.wtile[gi % 2]
        if os.environ.get("DMA3D", "1") == "1":
            S.dma("pool", wu, wup[:, :, gi * 256:(gi + 1) * 256], writes=[wt])
            S.dma("pool", wg, wup[:, :, DFF + gi * 256:DFF + (gi + 1) * 256], writes=[wt])
        else:
            for kc in range(8):
                S.dma("pool", wu[:, kc, :], wup[:, kc, gi * 256:(gi + 1) * 256], writes=[wt])
                S.dma("pool", wg[:, kc, :], wup[:, kc, DFF + gi * 256:DFF + (gi + 1) * 256], writes=[wt])
        wd = g.wdn[gi % 2]
        wdt = g.wdtile[gi % 2]
        S.dma("pool", wd, wdn[gi * 256:(gi + 1) * 256, :].rearrange("(j p) n -> p j n", p=128), writes=[wdt])
        for j in range(2):
            fc = gi * 2 + j
            gf = g.gfull[j]
            gft = g.gftile[j]
            CP("pool", gf[:, 0:2], g.halo[:, fc, :], [MIX], [gft])
            for tb in range(2):
                bs = slice(tok0 + tb * 512, tok0 + (tb + 1) * 512)
                pu, pg = nps(), nps()
                for kc in range(8):
                    MM(pu, pu[:, :], wu[:, kc, j * 128:(j + 1) * 128], XT[:, kc, bs], kc == 0, kc == 7, [wt, XT])
                for kc in range(8):
                    MM(pg, pg[:, :], wg[:, kc, j * 128:(j + 1) * 128], XT[:, kc, bs], kc == 0, kc == 7, [wt, XT])
                o = 2 + tb * 512
                ACT(gf[:, o:o + 512], pg[:, :], AF.Copy, [pg], [gft])
                tm = g.ftmp[tb]
                tmt = g.fttile[tb]
                TS("dve", tm, gf[:, o:o + 512], g.cvw[:, fc, 2:3], g.cvb[:, fc:fc + 1], ALU.mult, ALU.add, [gft, g.cvw, g.cvb], [tmt])
                STT("dve", tm, gf[:, o - 1:o + 511], g.cvw[:, fc, 1:2], tm, ALU.mult, ALU.add, [gft, g.cvw, tmt], [tmt])
                STT("dve", tm, gf[:, o - 2:o + 510], g.cvw[:, fc, 0:1], tm, ALU.mult, ALU.add, [gft, g.cvw, tmt], [tmt])
                ACT(tm, tm, AF.Gelu, [tmt], [tmt])
                TT("dve", g.hT[:, j, tb * 512:(tb + 1) * 512], tm, pu[:, :], ALU.mult, [tmt, pu], [g.hTtile])
            CP("pool", g.halo[:, fc, :], gf[:, 1024:1026], [gft], [MIX])
            if half == 1:
                CP("pool", cst[:, fc, :], gf[:, 1024:1026], [gft], [cst])
        for t in range(8):
            pa, pb = nps(), nps()
            for hf, ps in enumerate((pa, pb)):
                for j in range(2):
                    MM(ps, ps[:, :], g.hT[:, j, t * 128:(t + 1) * 128], wd[:, j, hf * 512:(hf + 1) * 512], j == 0, j == 1, [g.hTtile, wdt])
                TT("dve", acc[:, t, hf * 512:(hf + 1) * 512], acc[:, t, hf * 512:(hf + 1) * 512], ps[:, :], ALU.add, [BUFb, ps], [BUFb])


def moba_inproj_tile(*a, **k):
    raise NotImplementedError("MoBA layer not implemented in this revision")


def moba_attention(*a, **k):
    raise NotImplementedError("MoBA layer not implemented in this revision")


NPG = 128
SPOS_SCALE_A = float(96 ** -0.5)


def declare_sample_io(g, I, O, din, dout):
    for name, shape, dt in [
        ("xs", [4, D], F32), ("ptab", [1, 4 * NPG], I32),
        ("p_ckv", [NPOOL * 128, 256], F32), ("p_kr", [NPOOL * 128, 32], F32),
        ("p_k", [NPOOL * 128, 256], F32), ("p_v", [NPOOL * 128, 256], F32),
        ("cmk", [2, 4, 256, 256], F32), ("cmv", [2, 4, 256, 256], F32),
        ("sconv", [2, 128, NFC * 8], F32), ("w_ukT", [64, 12 * 256], F32),
        ("scos32", [4, 32], F32), ("ssin32", [4, 32], F32), ("scos64", [4, 64], F32), ("ssin64", [4, 64], F32),
    ]:
        I[name] = din(name, shape, dt)
    for name, shape in [("y_s", [4, D]), ("ckv_s", [4, 256]), ("krope_s", [4, 32]), ("mobak_s", [4, 256]),
                        ("mobav_s", [4, 256]), ("conv_s", [2, 128, NFC * 8])]:
        O[name] = dout(name, shape)


def sample_path(g):
    S, I, O = g.S, g.I, g.O
    MM, TR, ACT, TT, TS, STT, CP, MS, RS, RCP = g.MM, g.TR, g.ACT, g.TT, g.TS, g.STT, g.CP, g.MS, g.RS, g.RCP
    rot, nps, npo, npt, barrier, wload = g.rot, g.nps, g.npo, g.npt, g.barrier, g.wload
    XT, MIX, BUF, WA, WB, TAB = g.XT, g.MIX, g.BUF, g.WA, g.WB, g.TAB
    st, sq, lnp, gqb, cvw, cvb, ident = g.st, g.sq, g.lnp, g.gqb, g.cvw, g.cvb, g.ident
    flags = g.flags
    barrier()

    def kcw(w2d):
        return w2d.rearrange("(kc p) n -> p kc n", p=128)

    XT2 = XT[:, :, :].rearrange("p k n -> p (k n)")
    MX2 = MIX[:, :, :].rearrange("p k n -> p (k n)")
    G = lambda base, nm: S.view(base, nm)
    xsT = XT2[:, 0:32].rearrange("p (k n) -> p k n", k=8); G_xsT = G(XT, "s_xsT")
    mixT = XT2[:, 32:64].rearrange("p (k n) -> p k n", k=8); G_mix = G(XT, "s_mix")
    x1T = XT2[:, 64:96].rearrange("p (k n) -> p k n", k=8); G_x1T = G(XT, "s_x1T")
    qnT = XT2[:, 96:108].rearrange("p (k n) -> p k n", k=3); G_qnT = G(XT, "s_qnT")
    qmTs = XT2[:, 108:116].rearrange("p (k n) -> p k n", k=2); G_qmT = G(XT, "s_qmT")
    qlT = XT2[:, 116:212].rearrange("p (c h n) -> p c h n", c=2, h=12); G_qlT = G(XT, "s_qlT")
    qrT = XT2[:, 212:260].rearrange("p (h n) -> p h n", h=12); G_qrT = G(XT, "s_qrT")
    qhT = XT2[:, 260:308].rearrange("p (h n) -> p h n", h=12); G_qhT = G(XT, "s_qhT")
    olT = XT2[:, 308:404].rearrange("p (c h n) -> p c h n", c=2, h=12); G_olT = G(XT, "s_olT")
    qTs = XT2[:, 404:428].rearrange("p (c g n) -> p c g n", c=2, g=3); G_qTs = G(XT, "s_qTs")
    sbs = XT2[:, 512:2560]; G_sbs = G(XT, "s_sbs")
    hTs = XT2[:, 2560:2568].rearrange("p (j n) -> p j n", j=2); G_hTs = G(XT, "s_hTs")
    pTs = [XT2[:, 2576 + i * 16:2576 + i * 16 + 12] for i in range(4)]; G_pTs = [G(XT, "s_pT%d" % i) for i in range(4)]
    cT = [XT2[:, 4096 + i * 512:4096 + i * 512 + 384] for i in range(2)]; G_cT = [G(XT, "s_cT%d" % i) for i in range(2)]
    PTall = XT2[:, 6144:6144 + 1548].rearrange("p (g h) -> p g h", h=12); G_PT = G(XT, "s_PT")
    Yb = XT2[:, 8192:8960]; G_Yb = G(XT, "s_Yb")
    ones1 = XT2[:, 8960:9088]; G_on = G(XT, "s_ones")
    NB = 4
    cpg = [MX2[:, i * 264:i * 264 + 257] for i in range(NB)]; G_cpg = [G(MIX, "s_cpg%d" % i) for i in range(NB)]
    krp = [MX2[:, 1056 + i * 32:1056 + (i + 1) * 32] for i in range(NB)]; G_krp = [G(MIX, "s_krp%d" % i) for i in range(NB)]
    vpg = [MX2[:, 2048 + i * 264:2048 + i * 264 + 260].rearrange("p (k n) -> p k n", k=4) for i in range(NB)]
    G_vpg = [G(MIX, "s_vpg%d" % i) for i in range(NB)]
    lpg = MX2[:, 4096:4096 + 264]; G_lpg = G(MIX, "s_lpg")
    lkr = MX2[:, 4360:4392]
    lvp = MX2[:, 4400:4660].rearrange("p (k n) -> p k n", k=4)
    wukT = MX2[:, 8192:8192 + 3072].rearrange("p (h r) -> p h r", h=12); G_wukT = G(MIX, "s_wukT")
    xcur = BUF[:, 0:1024]; G_xcur = G(BUF, "s_xcur")
    zs = BUF[:, 1024:2048]; G_zs = G(BUF, "s_zs")
    accs = BUF[:, 2048:3072]; G_acc = G(BUF, "s_acc")
    sgs = BUF[:, 3072:4608]; G_sgs = G(BUF, "s_sgs")
    Sall = BUF[:, 4608:4608 + 1548].rearrange("p (g h) -> p g h", h=12); G_Sall = G(BUF, "s_Sall")
    biasB = BUF[:, 6400:7168]; G_bB = G(BUF, "s_biasB")
    gts = BUF[:, 7168:7232]; top8s = BUF[:, 7232:7240]; bsel = BUF[:, 7240:7304]; G_gt = G(BUF, "s_gt")
    olat = BUF[:, 7424:7424 + 264]; G_olat = G(BUF, "s_olat")
    gst = BUF[:, 7700:7700 + 3 * 88].rearrange("p (a n) -> p a n", a=3); G_gst = G(BUF, "s_gst")
    sct = BUF[:, 8192:8192 + NFC * 8].rearrange("p (c s t) -> p c s t", c=NFC, s=4); G_sct = G(BUF, "s_sct")
    cso = BUF[:, 8400:8400 + NFC * 8].rearrange("p (c s t) -> p c s t", c=NFC, s=4); G_cso = G(BUF, "s_cso")
    rmask = BUF[:, 8600:8601]; G_rm = G(BUF, "s_rmask")
    io = TAB[:, 0:1].bitcast(I32); ptb = TAB[:, 16:16 + 512].bitcast(I32); idx = TAB[:, 544:544 + 512].bitcast(I32)
    G_idx = G(TAB, "s_idx")
    stab = TAB[:, 1100:1100 + 192]; G_stab = G(TAB, "s_stab")
    identf = TAB[0:64, 2048:2112]; G_idf = G(TAB, "s_identf")
    S.dma("sp", identf, I["ident"][0:64, 0:64], writes=[G_idf])

    S.op("pool", lambda e: e.iota(io, pattern=[[1, 1]], base=0, channel_multiplier=1), writes=[G_idx])
    S.dma("sp", ptb, I["ptab"].partition_broadcast(128)[:, 0, :], writes=[G_idx])
    TS("dve", idx, ptb, 128, io[:, 0:1], ALU.mult, ALU.add, [G_idx], [G_idx])
    S.dma("sp", stab[0:4, 0:32], I["scos32"][:, :], writes=[G_stab])
    S.dma("sp", stab[0:4, 32:64], I["ssin32"][:, :], writes=[G_stab])
    S.dma("sp", stab[0:4, 64:128], I["scos64"][:, :], writes=[G_stab])
    S.dma("sp", stab[0:4, 128:192], I["ssin64"][:, :], writes=[G_stab])
    MS("pool", ones1, 1.0, [G_on])
    maskk = BUF[:, 8610:8614]; G_mk = G(BUF, "s_maskk")
    for kvh in range(4):
        S.op("dve", lambda e, o=maskk[0:12, kvh:kvh + 1], i=ident[0:12, kvh * 3:(kvh + 1) * 3]: e.reduce_sum(out=o, in_=i, axis=AX.X),
             reads=[ident], writes=[G_mk])
    MS("pool", rmask, 0.0, [G_rm])
    MS("pool", rmask[0:1, :], 1.0, [G_rm])
    for i in range(NB):
        MS("pool", cpg[i][:, 256:257], 1.0, [G_cpg[i]])
        MS("pool", vpg[i][:, :, 64:65], 1.0, [G_vpg[i]])
    MS("pool", lpg, 0.0, [G_lpg])
    MS("pool", lpg[:, 256:257], 1.0, [G_lpg])
    MS("pool", lkr, 0.0, [G_lpg])
    MS("pool", lvp, 0.0, [G_lpg])
    MS("pool", lvp[:, :, 64:65], 1.0, [G_lpg])
    S.dma("sp", xcur[0:4, :], I["xs"][:, :], writes=[G_xcur])
    S.dma("pool", wukT[0:64, :, :], I["w_ukT"].rearrange("p (h r) -> p h r", h=12), writes=[G_wukT])

    def tok_T(src_tile, src_ap, ncols_blocks, dst_ap, dst_tile, rows=128):
        pt_t = npt()
        nb = len(ncols_blocks)
        for i, (c0, w) in enumerate(ncols_blocks):
            TR(pt_t, pt_t[0:w, i * 4:(i + 1) * 4], src_ap[0:4, c0:c0 + w], [src_tile], npart=4)
        CP("dve", dst_ap, pt_t[0:rows, 0:nb * 4].rearrange("p (k n) -> p k n", k=nb), [pt_t], [dst_tile])

    def x_to_T(xf32, x_tile, dstT, dst_tile):
        ACT(sbs[0:4, 0:1024], xf32[0:4, :], AF.Copy, [x_tile], [G_sbs])
        tok_T(G_sbs, sbs, [(k * 128, 128) for k in range(8)], dstT, dst_tile)

    def proj_s(win, wtile, c0, c1):
        ps = nps()
        for kc in range(8):
            MM(ps, ps[0:4, 0:c1 - c0], xsT[:, kc, :], win[:, kc, c0:c1], kc == 0, kc == 7, [G_xsT, wtile])
        return ps

    def rope_s(src, nh, dh, ctab, stab_, dst, src_tile, dst_tile):
        hd = dh // 2
        a = sq[0:4, 0:nh * dh].rearrange("p (h d) -> p h d", h=nh)
        b2 = sq[0:4, 768:768 + nh * dh].rearrange("p (h d) -> p h d", h=nh) if nh * dh <= 256 else sq[0:4, 512:512 + nh * dh].rearrange("p (h d) -> p h d", h=nh)
        cB = ctab.to_broadcast([4, nh, dh]) if nh > 1 else ctab
        sB = stab_.to_broadcast([4, nh, dh]) if nh > 1 else stab_
        TT("dve", a, src, cB, ALU.mult, [src_tile, G_stab], [sq])
        TT("dve", b2[:, :, 0:hd], src[:, :, hd:dh], sB[:, :, 0:hd], ALU.mult, [src_tile, G_stab], [sq])
        TT("dve", b2[:, :, hd:dh], src[:, :, 0:hd], sB[:, :, hd:dh], ALU.mult, [src_tile, G_stab], [sq])
        TT("dve", dst, a, b2, ALU.add, [sq], [dst_tile])

    def gather(pool_ap, dst_ap, dst_tile, s, p):
        col = s * NPG + p
        S.dma("pool", None, None, reads=[G_idx], writes=[dst_tile],
              fn=lambda e, o=dst_ap, c=col, src=pool_ap: e.indirect_dma_start(
                  out=o, out_offset=None, in_=src[:, :], in_offset=bass.IndirectOffsetOnAxis(ap=idx[:, c:c + 1], axis=0)))

    def mem_attn_s(L):
        mkT_ap, vmem_ap = g.mkT_sb[:, :, :], g.vmem_sb[:, :, :, :]
        for s in range(4):
            MS("pool", vmem_ap, 1.0, [g.vmem_sb])
            for mt in range(2):
                f = rot(g.xf, "xf")
                S.dma("sp", f[:, 0:256], I["cmk"][L, s, mt * 128:(mt + 1) * 128, :], writes=[f])
                S.dma("sp", f[:, 256:512], I["cmv"][L, s, mt * 128:(mt + 1) * 128, :], writes=[f])
                b = rot(g.xb, "xb")
                ACT(b[:, 0:256], f[:, 0:256], AF.Copy, [f], [b])
                pt_t = npt()
                for pr in range(2):
                    TR(pt_t, pt_t[:, pr * 128:(pr + 1) * 128], b[:, pr * 128:(pr + 1) * 128], [b])
                CP("dve", mkT_ap[:, :, mt * 128:(mt + 1) * 128], pt_t[:, 0:256].rearrange("p (k n) -> p k n", k=2), [pt_t], [g.mkT_sb])
                for j in range(4):
                    c0 = 0 if j % 2 == 0 else 64
                    ACT(vmem_ap[:, mt, j, c0:c0 + 64], f[:, 256 + j * 64:256 + (j + 1) * 64], AF.Copy, [f], [g.vmem_sb])
            for j in range(4):
                p0 = (j % 2) * 64
                num0 = 64 if j % 2 == 1 else 0
                den0 = 64 - num0
                po = npo()
                for mt in range(2):
                    ps = nps()
                    MM(ps, ps[:, 0:1], mkT_ap[p0:p0 + 64, j // 2, mt * 128:(mt + 1) * 128], qmTs[p0:p0 + 64, j // 2, s:s + 1], True, True,
                       [g.mkT_sb, G_qmT])
                    p = rot(g.pT, "pT")
                    ACT(p[:, 0:1], ps[:, 0:1], AF.Exp, [ps], [p], scale=0.125)
                    MM(po, po[:, 0:1], vmem_ap[:, mt, j, :], p[:, 0:1], mt == 0, mt == 1, [g.vmem_sb, p])
                r = rot(g.rc, "rc")
                RCP(r[den0:den0 + 64, 0:1], po[den0:den0 + 64, 0:1], [po], [r])
                TT("dve", mixT[num0:num0 + 64, 6 + j // 2, s:s + 1], po[num0:num0 + 64, 0:1], r[den0:den0 + 64, 0:1], ALU.mult, [po, r], [G_mix])

    def post_attention(L, wo, wotile):
        for hf in range(2):
            ps = nps()
            for kc in range(8):
                MM(ps, ps[0:4, :], mixT[:, kc, :], wo[:, kc, hf * 512:(hf + 1) * 512], kc == 0, kc == 7, [G_mix, wotile])
            STT("dve", zs[0:4, hf * 512:(hf + 1) * 512], xcur[0:4, hf * 512:(hf + 1) * 512], ALPHA, ps[0:4, :], ALU.mult, ALU.add, [G_xcur, ps], [G_zs])
        for i, nm in enumerate(["ln1_g", "ln1_b"]):
            S.dma("sp", lnp[:, i, :], I[nm][L:L + 1, :].partition_broadcast(128)[:, 0, :], writes=[lnp])
        g.layernorm(G_zs_t, 4, 0, G_zs_t, zs[0:4, :])
        x_to_T(zs, G_zs, x1T, G_x1T)
        S.op("act", lambda e: e.mul(out=accs[0:4, :], in_=zs[0:4, :], mul=ALPHA), reads=[G_zs], writes=[G_acc])
        S.dma("sp", sct, I["sconv"][L].rearrange("p (c s t) -> p c s t", c=NFC, s=4), writes=[G_sct])
        S.dma("sp", cvw[:], I["convw"][L].rearrange("p (c t) -> p c t", t=3), writes=[cvw])
        S.dma("sp", cvb[:], I["convb"][L], writes=[cvb])
        wup = I["w_up"][L].rearrange("(kc p) n -> p kc n", p=128)
        wdn = I["w_down"][L]
        for gi in range(11):
            w = WB[:, (gi % 2) * 4096:(gi % 2) * 4096 + 4096]
            wu = w[:, 0:2048].rearrange("p (k n) -> p k n", k=8)
            wg = w[:, 2048:4096].rearrange("p (k n) -> p k n", k=8)
            wt = g.wtile[gi % 2]
            for kc in range(8):
                S.dma("pool", wu[:, kc, :], wup[:, kc, gi * 256:(gi + 1) * 256], writes=[wt])
                S.dma("pool", wg[:, kc, :], wup[:, kc, DFF + gi * 256:DFF + (gi + 1) * 256], writes=[wt])
            wd = g.wdn[gi % 2]
            wdt = g.wdtile[gi % 2]
            for j in range(2):
                S.dma("pool", wd[:, j, :], wdn[gi * 256 + j * 128:gi * 256 + (j + 1) * 128, :], writes=[wdt])
            for j in range(2):
                fc = gi * 2 + j
                pu, pg = nps(), nps()
                for kc in range(8):
                    MM(pu, pu[:, 0:4], wu[:, kc, j * 128:(j + 1) * 128], x1T[:, kc, :], kc == 0, kc == 7, [wt, G_x1T])
                for kc in range(8):
                    MM(pg, pg[:, 0:4], wg[:, kc, j * 128:(j + 1) * 128], x1T[:, kc, :], kc == 0, kc == 7, [wt, G_x1T])
                gn = gst[:, 0, 0:4]
                tm = gst[:, 1, 0:4]
                ACT(gn, pg[:, 0:4], AF.Copy, [pg], [G_gst])
                TS("dve", tm, gn, cvw[:, fc, 2:3], cvb[:, fc:fc + 1], ALU.mult, ALU.add, [G_gst, cvw, cvb], [G_gst])
                STT("dve", tm, sct[:, fc, :, 1], cvw[:, fc, 1:2], tm, ALU.mult, ALU.add, [G_sct, cvw, G_gst], [G_gst])
                STT("dve", tm, sct[:, fc, :, 0], cvw[:, fc, 0:1], tm, ALU.mult, ALU.add, [G_sct, cvw, G_gst], [G_gst])
                ACT(tm, tm, AF.Gelu, [G_gst], [G_gst])
                TT("dve", hTs[:, j, :], tm, pu[:, 0:4], ALU.mult, [G_gst, pu], [G_hTs])
                CP("pool", cso[:, fc, :, 0], sct[:, fc, :, 1], [G_sct], [G_cso])
                CP("pool", cso[:, fc, :, 1], gn, [G_gst], [G_cso])
            for hf in range(2):
                ps = nps()
                for j in range(2):
                    MM(ps, ps[0:4, :], hTs[:, j, :], wd[:, j, hf * 512:(hf + 1) * 512], j == 0, j == 1, [G_hTs, wdt])
                TT("dve", accs[0:4, hf * 512:(hf + 1) * 512], accs[0:4, hf * 512:(hf + 1) * 512], ps[0:4, :], ALU.add, [G_acc, ps], [G_acc])
        S.dma("sp", O["conv_s"][L].rearrange("p (c s t) -> p c s t", c=NFC, s=4), cso, reads=[G_cso])
        for i, nm in enumerate(["ln2_g", "ln2_b"]):
            S.dma("sp", lnp[:, i, :], I[nm][L:L + 1, :].partition_broadcast(128)[:, 0, :], writes=[lnp])
        g.layernorm(G_acc_t, 4, 2, G_xcur, xcur[0:4, :])

    G_zs_t = _TileAP(G_zs, zs)
    G_acc_t = _TileAP(G_acc, accs)

    L = 0
    win = WA[:, 0:8 * 928].rearrange("p (k n) -> p k n", k=8)
    wload(WA, win, kcw(I["w_in_a"]))
    wq = WB[:, 0:3456].rearrange("p (k n) -> p k n", k=3)
    wuv = WB[:, 8448:9984].rearrange("p (k n) -> p k n", k=2)
    wload(WB, wq, kcw(I["w_q_b"]))
    wload(WB, wuv, kcw(I["w_uv"]))
    S.dma("sp", gqb[:, 0:384], I["g_q"].partition_broadcast(128)[:, 0, :], writes=[gqb])
    S.dma("sp", gqb[:, 384:640], I["g_kv"].partition_broadcast(128)[:, 0, :], writes=[gqb])
    x_to_T(xcur, G_xcur, xsT, G_xsT)
    psA = proj_s(win, WA, 0, 384)
    r = g.rstd_of(psA[0:4, 0:384], 384, 4, RMS_EPS, [psA])
    STT("dve", sbs[0:4, 0:384], psA[0:4, 0:384], r, gqb[0:4, 0:384], ALU.mult, ALU.mult, [psA, st, gqb], [G_sbs])
    tok_T(G_sbs, sbs, [(k * 128, 128) for k in range(3)], qnT, G_qnT)
    psB = proj_s(win, WA, 384, 672)
    r = g.rstd_of(psB[0:4, 0:256], 256, 4, RMS_EPS, [psB])
    STT("dve", sgs[0:4, 0:256], psB[0:4, 0:256], r, gqb[0:4, 384:640], ALU.mult, ALU.mult, [psB, st, gqb], [G_sgs])
    rope_s(psB[0:4, 256:288].rearrange("p (h d) -> p h d", h=1), 1, 32, stab[0:4, 0:32].rearrange("p (h d) -> p h d", h=1),
           stab[0:4, 32:64].rearrange("p (h d) -> p h d", h=1), sgs[0:4, 256:288].rearrange("p (h d) -> p h d", h=1), psB, G_sgs)
    S.dma("sp", O["ckv_s"][:, :], sgs[0:4, 0:256], reads=[G_sgs])
    S.dma("sp", O["krope_s"][:, :], sgs[0:4, 256:288], reads=[G_sgs])
    ACT(sbs[0:4, 1152:1440], sgs[0:4, 0:288], AF.Copy, [G_sgs], [G_sbs])
    psC = proj_s(win, WA, 672, 928)
    ACT(sbs[0:4, 1536:1792], psC[0:4, 0:256], AF.Copy, [psC], [G_sbs])
    tok_T(G_sbs, sbs, [(1536, 128), (1664, 128)], qmTs, G_qmT)
    for grp in range(3):
        ps = nps()
        for kc in range(3):
            MM(ps, ps[0:4, 0:384], qnT[:, kc, :], wq[:, kc, grp * 384:(grp + 1) * 384], kc == 0, kc == 2, [G_qnT, WB])
        q3 = ps[0:4, 0:384].rearrange("p (h d) -> p h d", h=4)
        dst = sgs[0:4, 384 + grp * 384:384 + (grp + 1) * 384].rearrange("p (h d) -> p h d", h=4)
        CP("dve", dst[:, :, 0:64], q3[:, :, 0:64], [ps], [G_sgs])
        rope_s(q3[:, :, 64:96], 4, 32, stab[0:4, 0:32].rearrange("p (h d) -> p h d", h=1), stab[0:4, 32:64].rearrange("p (h d) -> p h d", h=1),
               dst[:, :, 64:96], ps, G_sgs)
    ACT(sbs[0:4, 0:1152], sgs[0:4, 384:1536], AF.Copy, [G_sgs], [G_sbs])
    q12 = sbs[0:4, 0:1152].rearrange("p (h d) -> p h d", h=12)
    pt_t = npt()
    for h in range(12):
        TR(pt_t, pt_t[0:64, h * 4:(h + 1) * 4], q12[:, h, 0:64], [G_sbs], npart=4)
        TR(pt_t, pt_t[0:32, 64 + h * 4:64 + (h + 1) * 4], q12[:, h, 64:96], [G_sbs], npart=4)
    CP("dve", qhT[0:64, :, :], pt_t[0:64, 0:48].rearrange("p (h n) -> p h n", h=12), [pt_t], [G_qhT])
    CP("dve", qrT[0:32, :, :], pt_t[0:32, 64:112].rearrange("p (h n) -> p h n", h=12), [pt_t], [G_qrT])
    for c in range(2):
        ps = nps()
        for h in range(12):
            MM(ps, ps[:, h * 4:(h + 1) * 4], wukT[0:64, h, c * 128:(c + 1) * 128], qhT[0:64, h, :], True, True, [G_wukT, G_qhT])
        ACT(qlT[:, c, :, :], ps[:, 0:48].rearrange("p (h n) -> p h n", h=12), AF.Copy, [ps], [G_qlT])
    nbuf = [0]
    for s in range(4):
        po = npo()
        S.dma("sp", lpg[0:1, 0:256], sbs[s:s + 1, 1152:1408], reads=[G_sbs], writes=[G_lpg])
        S.dma("sp", lkr[0:1, 0:32], sbs[s:s + 1, 1408:1440], reads=[G_sbs], writes=[G_lpg])
        NP1 = NPG + 1
        stA = {}

        def mlaA(p, s=s):
            local = (p == NPG)
            bi = nbuf[0] % NB
            nbuf[0] += 1
            if local:
                cp_ap, cp_t, kr_ap, kr_t = lpg, G_lpg, lkr, G_lpg
            else:
                cp_ap, cp_t, kr_ap, kr_t = cpg[bi], G_cpg[bi], krp[bi], G_krp[bi]
                gather(I["p_ckv"], cp_ap[:, 0:256], cp_t, s, p)
                gather(I["p_kr"], kr_ap, kr_t, s, p)
            pt_t = npt()
            TR(pt_t, pt_t[:, 0:128], cp_ap[:, 0:128], [cp_t])
            TR(pt_t, pt_t[:, 128:256], cp_ap[:, 128:256], [cp_t])
            TR(pt_t, pt_t[0:32, 256:384], kr_ap, [kr_t])
            ci = p % 2
            CP("dve", cT[ci][:, 0:256], pt_t[:, 0:256], [pt_t], [G_cT[ci]])
            ACT(cT[ci][0:32, 256:384], pt_t[0:32, 256:384], AF.Copy, [pt_t], [G_cT[ci]])
            stA[p] = (cp_ap, cp_t, local)

        def mlaB(p, s=s):
            ci = p % 2
            local = stA[p][2]
            ps = nps()
            MM(ps, ps[:, 0:12], cT[ci][:, 0:128], qlT[:, 0, :, s], True, False, [G_cT[ci], G_qlT])
            MM(ps, ps[:, 0:12], cT[ci][:, 128:256], qlT[:, 1, :, s], False, False, [G_cT[ci], G_qlT])
            MM(ps, ps[:, 0:12], cT[ci][0:32, 256:384], qrT[0:32, :, s], False, True, [G_cT[ci], G_qrT])
            pi = p % 4
            ACT(pTs[pi], ps[:, 0:12], AF.Exp, [ps], [G_pTs[pi]], scale=SPOS_SCALE_A)
            if local:
                TS("dve", pTs[pi], pTs[pi], rmask[:, 0:1], None, ALU.mult, None, [G_pTs[pi], G_rm], [G_pTs[pi]])

        def mlaC(p, po=po):
            cp_ap, cp_t, local = stA.pop(p)
            pi = p % 4
            MM(po, po[0:12, 0:257], pTs[pi], cp_ap[:, 0:257], p == 0, local, [G_pTs[pi], cp_t])
        mlaA(0)
        for p in range(NP1):
            if p + 1 < NP1:
                mlaA(p + 1)
            mlaB(p)
            if p >= 1:
                mlaC(p - 1)
        mlaC(NP1 - 1)
        RCP(olat[0:12, 256:257], po[0:12, 256:257], [po], [G_olat])
        TS("dve", olat[0:12, 0:256], po[0:12, 0:256], olat[0:12, 256:257], None, ALU.mult, None, [po, G_olat], [G_olat])
        ACT(Yb[0:12, 0:256], olat[0:12, 0:256], AF.Copy, [G_olat], [G_Yb])
        pt_t = npt()
        for c in range(2):
            TR(pt_t, pt_t[:, c * 12:(c + 1) * 12], Yb[0:12, c * 128:(c + 1) * 128], [G_Yb], npart=12)
        CP("dve", olT[:, :, :, s], pt_t[:, 0:24].rearrange("p (c h) -> p c h", c=2), [pt_t], [G_olT])
    for pr in range(6):
        for hh in range(2):
            h = pr * 2 + hh
            ps = nps()
            for c in range(2):
                MM(ps, ps[:, 0:4], wuv[:, c, pr * 128:(pr + 1) * 128], olT[:, c, h, :], c == 0, c == 1, [WB, G_olT])
            r0 = hh * 64
            ACT(mixT[r0:r0 + 64, pr, :], ps[r0:r0 + 64, 0:4], AF.Copy, [ps], [G_mix])
    mem_attn_s(0)
    wo = WA[:, 0:8192].rearrange("p (k n) -> p k n", k=8)
    wload(WA, wo, kcw(I["w_o"][0]))
    post_attention(0, wo, WA)
    if flags["layers"] < 2:
        return
    L = 1
    win = WA[:, 0:8 * 1536].rearrange("p (k n) -> p k n", k=8)
    wload(WA, win, kcw(I["w_in_b"]))
    x_to_T(xcur, G_xcur, xsT, G_xsT)
    cos64 = stab[0:4, 64:128].rearrange("p (h d) -> p h d", h=1)
    sin64 = stab[0:4, 128:192].rearrange("p (h d) -> p h d", h=1)
    for qh in range(2):
        ps = proj_s(win, WA, qh * 384, (qh + 1) * 384)
        rope_s(ps[0:4, 0:384].rearrange("p (h d) -> p h d", h=6), 6, 64, cos64, sin64,
               sgs[0:4, qh * 384:(qh + 1) * 384].rearrange("p (h d) -> p h d", h=6), ps, G_sgs)
    pskv = proj_s(win, WA, 768, 1280)
    rope_s(pskv[0:4, 0:256].rearrange("p (h d) -> p h d", h=4), 4, 64, cos64, sin64,
           sgs[0:4, 768:1024].rearrange("p (h d) -> p h d", h=4), pskv, G_sgs)
    CP("dve", sgs[0:4, 1024:1280], pskv[0:4, 256:512], [pskv], [G_sgs])
    S.dma("sp", O["mobak_s"][:, :], sgs[0:4, 768:1024], reads=[G_sgs])
    S.dma("sp", O["mobav_s"][:, :], sgs[0:4, 1024:1280], reads=[G_sgs])
    ACT(sbs[0:4, 0:1280], sgs[0:4, 0:1280], AF.Copy, [G_sgs], [G_sbs])
    psm = proj_s(win, WA, 1280, 1536)
    ACT(sbs[0:4, 1280:1536], psm[0:4, 0:256], AF.Copy, [psm], [G_sbs])
    tok_T(G_sbs, sbs, [(1280, 128), (1408, 128)], qmTs, G_qmT)
    q12 = sbs[0:4, 0:768].rearrange("p (h d) -> p h d", h=12)
    pt_t = npt()
    for h in range(12):
        TR(pt_t, pt_t[0:64, h * 4:(h + 1) * 4], q12[:, h, :], [G_sbs], npart=4)
    CP("dve", qhT[0:64, :, :], pt_t[0:64, 0:48].rearrange("p (h n) -> p h n", h=12), [pt_t], [G_qhT])
    qg = sgs[0:4, 0:768].rearrange("p (c g k d) -> p c g k d", c=2, g=3, k=2)
    for c in range(2):
        for gg in range(3):
            for kk in range(2):
                h = (2 * c + kk) * 3 + gg
                CP("dve", qg[:, c, gg, kk, :], q12[:, h, :], [G_sbs], [G_sgs])
    qgb = g.sgb[0]
    ACT(qgb[0:4, 0:768], sgs[0:4, 0:768], AF.Copy, [G_sgs], [qgb])
    tok_T(qgb, qgb, [(i * 128, 128) for i in range(6)], qTs[:, :, :, :].rearrange("p c g n -> p (c g) n"), G_qTs)
    for s in range(4):
        S.dma("sp", lpg[0:1, 0:256], sbs[s:s + 1, 768:1024], reads=[G_sbs], writes=[G_lpg])
        S.dma("sp", lvp[0:1, :, 0:64], sbs[s:s + 1, 1024:1280].rearrange("p (k d) -> p k d", k=4), reads=[G_sbs], writes=[G_lpg])
        pkm = npo()
        for p in range(NPG + 1):
            local = (p == NPG)
            bi = nbuf[0] % NB
            nbuf[0] += 1
            if local:
                kp_ap, kp_t = lpg, G_lpg
            else:
                kp_ap, kp_t = cpg[bi], G_cpg[bi]
                gather(I["p_k"], kp_ap[:, 0:256], kp_t, s, p)
                for kvh in range(4):
                    MM(pkm, pkm[0:64, kvh * 64 + p // 2:kvh * 64 + p // 2 + 1], kp_ap[:, kvh * 64:(kvh + 1) * 64], ones1[:, 0:1],
                       p == 0 and kvh == 0, p == NPG - 1 and kvh == 3, [kp_t, G_on])
            pt_t = npt()
            TR(pt_t, pt_t[:, 0:128], kp_ap[:, 0:128], [kp_t])
            TR(pt_t, pt_t[:, 128:256], kp_ap[:, 128:256], [kp_t])
            ci = p % 2
            CP("dve", cT[ci][:, 0:256], pt_t[:, 0:256], [pt_t], [G_cT[ci]])
            ps = nps()
            for kvh in range(4):
                r0 = (kvh % 2) * 64
                MM(ps, ps[:, kvh * 3:(kvh + 1) * 3], cT[ci][r0:r0 + 64, (kvh // 2) * 128:(kvh // 2) * 128 + 128],
                   qTs[r0:r0 + 64, kvh // 2, :, s], True, True, [G_cT[ci], G_qTs])
            CP("dve", Sall[:, p, :], ps[:, 0:12], [ps], [G_Sall])
        kms = cT[0]
        ACT(kms[0:64, 0:256], pkm[0:64, 0:256], AF.Copy, [pkm], [G_cT[0]])
        psg = nps()
        for kvh in range(4):
            MM(psg, psg[0:12, kvh * 64:(kvh + 1) * 64], qhT[0:64, :, s], kms[0:64, kvh * 64:(kvh + 1) * 64], True, True,
               [G_qhT, G_cT[0]])
        TS("dve", gts[0:12, :], psg[0:12, 0:64], maskk[0:12, 0:1], None, ALU.mult, None, [psg, G_mk], [G_gt])
        for kvh in range(1, 4):
            STT("dve", gts[0:12, :], psg[0:12, kvh * 64:(kvh + 1) * 64], maskk[0:12, kvh:kvh + 1], gts[0:12, :], ALU.mult, ALU.add,
                [psg, G_mk, G_gt], [G_gt])
        S.op("dve", lambda e: e.max(out=top8s[0:12, :], in_=gts[0:12, :]), reads=[G_gt], writes=[G_gt])
        TS("dve", bsel[0:12, :], gts[0:12, :], top8s[0:12, 2:3], None, ALU.is_ge, None, [G_gt], [G_gt])
        TS("dve", bsel[0:12, :], bsel[0:12, :], -1.0, BIG, ALU.add, ALU.mult, [G_gt], [G_gt])
        TT("dve", Yb[0:12, 0:768].rearrange("p (b h) -> p b h", h=12), bsel[0:12, :].rearrange("p (b o) -> p b o", o=1).to_broadcast([12, 64, 12]),
           ident[0:12, 0:12].rearrange("p (o h) -> p o h", o=1).to_broadcast([12, 64, 12]), ALU.mult, [G_gt, ident], [G_Yb])
        for hf in range(2):
            psb = nps()
            MM(psb, psb[:, 0:384], ones1[0:12, 0:128], Yb[0:12, hf * 384:(hf + 1) * 384], True, True, [G_on, G_Yb])
            CP("dve", biasB[:, hf * 384:(hf + 1) * 384], psb[:, 0:384], [psb], [G_bB])
        S4 = BUF[:, 4608:4608 + 1536].rearrange("p (b t h) -> p b t h", b=64, t=2)
        TT("dve", S4, S4, biasB[:, 0:768].rearrange("p (b o h) -> p b o h", b=64, o=1).to_broadcast([128, 64, 2, 12]), ALU.add,
           [G_Sall, G_bB], [G_Sall])
        ACT(PTall, Sall, AF.Exp, [G_Sall], [G_PT], scale=0.125)
        TS("dve", PTall[:, NPG, :], PTall[:, NPG, :], rmask[:, 0:1], None, ALU.mult, None, [G_PT, G_rm], [G_PT])
        po = npo()
        for p in range(NPG + 1):
            local = (p == NPG)
            bi = nbuf[0] % NB
            nbuf[0] += 1
            if local:
                vp_ap, vp_t = lvp, G_lpg
            else:
                vp_ap, vp_t = vpg[bi], G_vpg[bi]
                gather(I["p_v"], cpg[bi][:, 0:256], G_cpg[bi], s, p)
                CP("pool", vp_ap[:, :, 0:64], cpg[bi][:, 0:256].rearrange("p (k d) -> p k d", k=4), [G_cpg[bi]], [vp_t])
            for kvh in range(4):
                MM(po, po[0:3, kvh * 65:(kvh + 1) * 65], PTall[:, p, kvh * 3:(kvh + 1) * 3], vp_ap[:, kvh, :], p == 0 and kvh == 0, local and kvh == 3, [G_PT, vp_t])
        o3 = olat[0:3, 0:260].rearrange("p (k n) -> p k n", k=4)
        RCP(o3[:, :, 64:65], po[0:3, 0:260].rearrange("p (k n) -> p k n", k=4)[:, :, 64:65], [po], [G_olat])
        TT("dve", o3[:, :, 0:64], po[0:3, 0:260].rearrange("p (k n) -> p k n", k=4)[:, :, 0:64], o3[:, :, 64:65].to_broadcast([3, 4, 64]), ALU.mult,
           [po, G_olat], [G_olat])
        ACT(Yb[0:3, 0:256].rearrange("p (k n) -> p k n", k=4), o3[:, :, 0:64], AF.Copy, [G_olat], [G_Yb])
        pt_t = npt()
        for kvh in range(4):
            TR(pt_t, pt_t[0:64, kvh * 4:kvh * 4 + 3], Yb[0:3, kvh * 64:(kvh + 1) * 64], [G_Yb], npart=3)
        for h in range(12):
            r0 = (h % 2) * 64
            cc = (h // 3) * 4 + (h % 3)
            ACT(mixT[r0:r0 + 64, h // 2, s:s + 1], pt_t[0:64, cc:cc + 1], AF.Copy, [pt_t], [G_mix])
    mem_attn_s(1)
    wo = WA[:, 0:8192].rearrange("p (k n) -> p k n", k=8)
    wload(WA, wo, kcw(I["w_o"][1]))
    post_attention(1, wo, WA)
    S.dma("sp", O["y_s"][:, :], xcur[0:4, :], reads=[G_xcur])


class _TileAP:
    def __init__(self, tile, ap):
        self.tile, self.ap = tile, ap

    def __getitem__(self, k):
        return self.ap[k]

    lw = property(lambda self: self.tile.lw, lambda self, v: setattr(self.tile, "lw", v))
    readers = property(lambda self: self.tile.readers, lambda self, v: setattr(self.tile, "readers", v))
    excl = property(lambda self: self.tile.excl)
    dsem = property(lambda self: self.tile.dsem, lambda self, v: setattr(self.tile, "dsem", v))
    dcount = property(lambda self: self.tile.dcount, lambda self, v: setattr(self.tile, "dcount", v))
    name = property(lambda self: self.tile.name)


def _consts():
    c = {}
    c["ident"] = np.eye(128, dtype=np.float32)
    k = np.arange(128)[:, None, None]
    j = np.arange(4)[None, :, None]
    q = np.arange(512)[None, None, :]
    c["masks"] = (j * 128 + k <= q).astype(np.float32).reshape(128, 2048)
    pos = np.arange(SEQ, dtype=np.float32)

    def tabs(d, p):
        inv = (np.float32(10000.0) ** (-np.arange(0, d, 2, dtype=np.float32) / np.float32(d))).astype(np.float32)
        ang = p[:, None].astype(np.float32) * inv[None, :]
        return np.cos(ang).astype(np.float32), np.sin(ang).astype(np.float32)
    co, si = tabs(32, pos)
    c["cos32"] = np.concatenate([co, co], 1)
    c["sin32"] = np.concatenate([-si, si], 1)
    c96 = np.ones((96, SEQ), np.float32)
    s96 = np.zeros((96, SEQ), np.float32)
    c96[64:80] = co.T
    c96[80:96] = co.T
    s96[64:80] = -si.T
    s96[80:96] = si.T
    c["c96"], c["s96"] = c96, s96
    co, si = tabs(64, pos)
    c["cos64"] = np.concatenate([co, co], 1)
    c["sin64"] = np.concatenate([-si, si], 1)
    c["onehot8"] = (np.arange(SEQ)[None, :] // 256 == np.arange(8)[:, None]).astype(np.float32)
    ps = np.full((4,), 16384.0, np.float32)
    co, si = tabs(32, ps)
    c["scos32"] = np.concatenate([co, co], 1)
    c["ssin32"] = np.concatenate([-si, si], 1)
    co, si = tabs(64, ps)
    c["scos64"] = np.concatenate([co, co], 1)
    c["ssin64"] = np.concatenate([-si, si], 1)
    return c


def sample_shared_inputs(inp, c):
    w_uk = np.ascontiguousarray(np.asarray(inp["w_uk"][0], dtype=np.float32))
    return {
        "p_ckv": np.asarray(inp["cache_mla_ckv"][0]).reshape(NPOOL * 128, 256),
        "p_kr": np.asarray(inp["cache_mla_krope"][0]).reshape(NPOOL * 128, 32),
        "p_k": np.asarray(inp["cache_moba_k"][0]).reshape(NPOOL * 128, 256),
        "p_v": np.asarray(inp["cache_moba_v"][0]).reshape(NPOOL * 128, 256),
        "w_ukT": np.ascontiguousarray(w_uk.transpose(2, 1, 0).reshape(64, 12 * 256)),
        "scos32": c["scos32"], "ssin32": c["ssin32"], "scos64": c["scos64"], "ssin64": c["ssin64"],
    }


def sample_core_inputs(inp, core):
    sl = slice(4 * core, 4 * core + 4)
    sc = np.asarray(inp["state_conv"], dtype=np.float32)[:, sl]
    sconv = sc.reshape(2, 4, 2, NFC, 128).transpose(0, 4, 3, 1, 2).reshape(2, 128, NFC * 8)
    return {
        "xs": np.ascontiguousarray(np.asarray(inp["x_sample"], dtype=np.float32)[sl, 0]),
        "ptab": np.ascontiguousarray(np.asarray(inp["page_table"], dtype=np.int32)[sl].reshape(1, 4 * 128)),
        "cmk": np.ascontiguousarray(np.asarray(inp["cache_mem_k"], dtype=np.float32)[:, sl].reshape(2, 4, 256, 256)),
        "cmv": np.ascontiguousarray(np.asarray(inp["cache_mem_v"], dtype=np.float32)[:, sl].reshape(2, 4, 256, 256)),
        "sconv": np.ascontiguousarray(sconv),
    }


def sample_outputs(R):
    n = len(R)
    st = lambda name: np.stack([np.asarray(R[i][name]) for i in range(n)])
    cs = st("conv_s").reshape(n, 2, 128, NFC, 4, 2)
    conv_s = np.ascontiguousarray(cs.transpose(1, 0, 4, 5, 3, 2).reshape(2, n * 4, 2, DFF))
    return dict(y_s=st("y_s").reshape(n * 4, 1, D), ckv_s=st("ckv_s").reshape(1, n * 4, 1, 256),
                krope_s=st("krope_s").reshape(1, n * 4, 1, 32), mobak_s=st("mobak_s").reshape(1, n * 4, 1, 4, 64),
                mobav_s=st("mobav_s").reshape(1, n * 4, 1, 4, 64), conv_s=conv_s)


_PROG = {}


def kernel(**inp):
    flags = dict(FLAGS)
    key = (flags["sample"], flags["layers"], flags.get("stop"), flags.get("prompt", True))
    if key not in _PROG:
        _PROG[key] = build_program(flags)
    nc = _PROG[key]
    f32 = lambda a: np.ascontiguousarray(np.asarray(a), dtype=np.float32)
    c = _consts()
    w_q_b = f32(inp["w_q_b"][0])
    sw = w_q_b.reshape(384, 12, 96).copy()
    sw[:, :, 64:80] = w_q_b.reshape(384, 12, 96)[:, :, 80:96]
    sw[:, :, 80:96] = w_q_b.reshape(384, 12, 96)[:, :, 64:80]
    conv_w = f32(inp["conv_w"])
    conv_b = f32(inp["conv_b"])
    shared = {
        "ident": c["ident"], "masks": c["masks"], "c96": c["c96"], "s96": c["s96"], "cos32": c["cos32"], "sin32": c["sin32"],
        "cos64": c["cos64"], "sin64": c["sin64"], "onehot8": c["onehot8"],
        "w_in_a": f32(inp["w_in_a"][0]), "g_q": f32(inp["g_q"]), "w_q_b": w_q_b, "w_q_b_sw": sw.reshape(384, 1152),
        "g_kv": f32(inp["g_kv"]), "w_uk": f32(inp["w_uk"][0]).reshape(256, 768), "w_uv": f32(inp["w_uv"][0]).reshape(256, 768),
        "w_in_b": f32(inp["w_in_b"][0]), "w_mem_k": f32(inp["w_mem_k"]), "w_mem_v": f32(inp["w_mem_v"]), "w_o": f32(inp["w_o"]),
        "ln1_g": f32(inp["ln1_g"]), "ln1_b": f32(inp["ln1_b"]), "ln2_g": f32(inp["ln2_g"]), "ln2_b": f32(inp["ln2_b"]),
        "w_up": f32(inp["w_up"]), "w_down": f32(inp["w_down"]),
        "convw": np.ascontiguousarray(conv_w.reshape(2, 3, NFC, 128).transpose(0, 3, 2, 1).reshape(2, 128, NFC * 3)),
        "convb": np.ascontiguousarray(conv_b.reshape(2, NFC, 128).transpose(0, 2, 1)),
    }
    if flags["sample"]:
        shared.update(sample_shared_inputs(inp, c))
    xp = f32(inp["x_prompt"])
    memp = f32(inp["mem_prompt"])
    in_maps = []
    for core in range(8):
        m = dict(shared)
        m["xp"] = xp[core]
        m["memp"] = memp[core]
        if flags["sample"]:
            m.update(sample_core_inputs(inp, core))
        in_maps.append(m)
    res = run_bass_kernel_spmd(nc, in_maps, core_ids=list(range(8)))
    R = res.results
    st = lambda name: np.stack([np.asarray(R[i][name]) for i in range(8)])
    y_p = st("y_p")
    ckv_p = st("ckv_p")[None]
    krope_p = st("krope_p")[None]
    mobak_p = st("mobak_p").reshape(8, SEQ, 4, 64)[None]
    mobav_p = st("mobav_p").reshape(8, SEQ, 4, 64)[None]
    memk = st("memk_p").transpose(1, 0, 2, 3).reshape(2, 8, 256, 4, 64)
    memv = st("memv_p").transpose(1, 0, 2, 3).reshape(2, 8, 256, 4, 64)
    cp = st("conv_p").reshape(8, 2, 128, NFC, 2)
    conv_p = np.ascontiguousarray(cp.transpose(1, 0, 4, 3, 2).reshape(2, 8, 2, DFF))
    if flags["sample"]:
        so = sample_outputs(R)
    else:
        z = lambda *s: np.zeros(s, np.float32)
        so = dict(y_s=z(32, 1, D), ckv_s=z(1, 32, 1, 256), krope_s=z(1, 32, 1, 32), mobak_s=z(1, 32, 1, 4, 64),
                  mobav_s=z(1, 32, 1, 4, 64), conv_s=z(2, 32, 2, DFF))
    return (y_p, so["y_s"], ckv_p, krope_p, mobak_p, mobav_p, memk, memv, conv_p,
            so["ckv_s"], so["krope_s"], so["mobak_s"], so["mobav_s"], so["conv_s"])
```
